# Optimizing a Trainium2 kernel written in Bass

```python
import math
import numpy as np
import jax
import jax.numpy as jnp
from jax import lax

D_MODEL = 1024
BATCH = 8
SEQ = 4096
DEPTH = 1

HEAD_DIM = 64
ROPE_THETA = 10000.0
NORM_EPS = 1e-6
NEG_INF = -1e30

DIFF_HEADS = 4
DIFF_V_DIM = 2 * HEAD_DIM
DIFF_WIDTH = DIFF_HEADS * DIFF_V_DIM
Q_BLOCK = 128

NSA_HEADS = 8
NSA_KV_GROUPS = 2
NSA_HEADS_PER_GROUP = NSA_HEADS // NSA_KV_GROUPS
NSA_WIDTH = NSA_HEADS * HEAD_DIM
KV_COLS = NSA_KV_GROUPS * HEAD_DIM
CMP_BLOCK = 32
CMP_STRIDE = 16
CMP_HIDDEN = 4 * HEAD_DIM
SLC_BLOCK = 64
SLC_TOPK = 16
WINDOW = 512
NSA_Q_BLOCK = 64
N_BRANCHES = 3

MIX_WIDTH = DIFF_WIDTH + NSA_WIDTH
IN_COLS = 2 * DIFF_HEADS * 2 * HEAD_DIM + DIFF_WIDTH + NSA_WIDTH + 6 * KV_COLS + NSA_HEADS * N_BRANCHES
D_FF = ((8 * D_MODEL + 3 * 256 - 1) // (3 * 256)) * 256

kernel_name = 'hybrid_diffattn_nsa_block'


def rms_norm(x, g):
    xf = x.astype(jnp.float32)
    y = xf * lax.rsqrt(jnp.mean(xf * xf, axis=-1, keepdims=True) + NORM_EPS)
    return (y * g.astype(jnp.float32)).astype(x.dtype)


def rope_tables(S):
    inv = 1.0 / (ROPE_THETA ** (jnp.arange(0, HEAD_DIM, 2, dtype=jnp.float32) / HEAD_DIM))
    ang = jnp.arange(S, dtype=jnp.float32)[:, None] * inv[None, :]
    return jnp.cos(ang), jnp.sin(ang)


def apply_rope(x, cos, sin):
    x1, x2 = jnp.split(x, 2, axis=-1)
    c = cos.astype(x.dtype)
    s = sin.astype(x.dtype)
    return jnp.concatenate([x1 * c - x2 * s, x1 * s + x2 * c], axis=-1)


def masked_softmax(s, mask):
    p = jax.nn.softmax(jnp.where(mask, s, NEG_INF), axis=-1)
    return jnp.where(mask, p, 0.0)


def diff_attention(q, k, v, lam):
    B, H, _, S, d = q.shape
    scale = d ** -0.5
    kpos = jnp.arange(S)

    def block(i):
        s0 = i * Q_BLOCK
        qb = lax.dynamic_slice_in_dim(q, s0, Q_BLOCK, axis=3)
        s = jnp.einsum('bhmqd,bhmkd->bhmqk', qb, k).astype(jnp.float32) * scale
        qpos = s0 + jnp.arange(Q_BLOCK)
        mask = kpos[None, :] <= qpos[:, None]
        p = jax.nn.softmax(jnp.where(mask, s, NEG_INF), axis=-1)
        a = p[:, :, 0] - lam * p[:, :, 1]
        return jnp.einsum('bhqk,bhke->bhqe', a.astype(v.dtype), v)

    o = lax.map(block, jnp.arange(S // Q_BLOCK))
    return o.transpose(1, 0, 3, 2, 4).reshape(B, S, H, v.shape[-1])


def compress_tokens(t, tok_idx, pos, w1, w2):
    blocks = t[:, :, tok_idx] + pos.astype(t.dtype)
    flat = blocks.reshape(blocks.shape[0], blocks.shape[1], blocks.shape[2], -1)
    return jax.nn.silu(flat @ w1) @ w2


def selection_overlap(n_cmp, n_sel):
    c0 = np.arange(n_cmp)[:, None] * CMP_STRIDE
    s0 = np.arange(n_sel)[None, :] * SLC_BLOCK
    ov = np.clip(np.minimum(c0 + CMP_BLOCK, s0 + SLC_BLOCK) - np.maximum(c0, s0), 0, None)
    return (ov / CMP_BLOCK).astype(np.float32)


def nsa_attention(q, kc_tok, vc_tok, ks, vs, kw, vw, gates,
                  k_pos, k_w1, k_w2, v_pos, v_w1, v_w2):
    B, G, Hg, S, d = q.shape
    dt = q.dtype
    scale = d ** -0.5
    QB = NSA_Q_BLOCK
    n_cmp = (S - CMP_BLOCK) // CMP_STRIDE + 1
    n_sel = S // SLC_BLOCK
    top_n = min(SLC_TOPK, n_sel)

    tok_idx = np.arange(n_cmp)[:, None] * CMP_STRIDE + np.arange(CMP_BLOCK)[None, :]
    kc = compress_tokens(kc_tok, tok_idx, k_pos, k_w1, k_w2)
    vc = compress_tokens(vc_tok, tok_idx, v_pos, v_w1, v_w2)
    cmp_end = jnp.asarray(tok_idx[:, -1])
    overlap = jnp.asarray(selection_overlap(n_cmp, n_sel))

    ks_blk = ks.reshape(B, G, n_sel, SLC_BLOCK, d)
    vs_blk = vs.reshape(B, G, n_sel, SLC_BLOCK, d)
    kw_pad = jnp.pad(kw, ((0, 0), (0, 0), (WINDOW, 0), (0, 0)))
    vw_pad = jnp.pad(vw, ((0, 0), (0, 0), (WINDOW, 0), (0, 0)))
    gather_blocks = jax.vmap(jax.vmap(lambda t, i: t[i]))
    blk = jnp.arange(n_sel)

    def block(i):
        s0 = i * QB
        qb = lax.dynamic_slice_in_dim(q, s0, QB, axis=3)
        gb = lax.dynamic_slice_in_dim(gates, s0, QB, axis=3)
        qpos = s0 + jnp.arange(QB)

        sc = jnp.einsum('bghqd,bgnd->bghqn', qb, kc).astype(jnp.float32) * scale
        pc = masked_softmax(sc, cmp_end[None, :] <= qpos[:, None])
        o_cmp = jnp.einsum('bghqn,bgnd->bghqd', pc.astype(dt), vc)

        imp = jnp.einsum('bghqn,nj->bgqj', pc, overlap)
        cur = qpos // SLC_BLOCK
        valid = blk[None, :] * SLC_BLOCK <= qpos[:, None]
        forced = (blk[None, :] == 0) | (blk[None, :] == cur[:, None]) | (blk[None, :] == cur[:, None] - 1)
        imp = jnp.where(forced, jnp.inf, jnp.where(valid, imp, -jnp.inf))
        _, sel = lax.top_k(imp, top_n)
        kg = gather_blocks(ks_blk, sel)
        vg = gather_blocks(vs_blk, sel)
        tpos = sel[..., None] * SLC_BLOCK + jnp.arange(SLC_BLOCK)
        smask = (tpos <= qpos[:, None, None]).reshape(B, G, 1, QB, top_n * SLC_BLOCK)
        ss = jnp.einsum('bghqd,bgqnld->bghqnl', qb, kg).astype(jnp.float32) * scale
        ps = masked_softmax(ss.reshape(B, G, Hg, QB, top_n * SLC_BLOCK), smask)
        o_slc = jnp.einsum('bghqm,bgqmd->bghqd', ps.astype(dt),
                           vg.reshape(B, G, QB, top_n * SLC_BLOCK, d))

        kwb = lax.dynamic_slice_in_dim(kw_pad, s0, WINDOW + QB, axis=2)
        vwb = lax.dynamic_slice_in_dim(vw_pad, s0, WINDOW + QB, axis=2)
        wpos = s0 - WINDOW + jnp.arange(WINDOW + QB)
        wmask = (wpos[None, :] <= qpos[:, None]) & (wpos[None, :] > qpos[:, None] - WINDOW) & (wpos[None, :] >= 0)
        sw = jnp.einsum('bghqd,bgkd->bghqk', qb, kwb).astype(jnp.float32) * scale
        pw = masked_softmax(sw, wmask)
        o_win = jnp.einsum('bghqk,bgkd->bghqd', pw.astype(dt), vwb)

        return gb[..., 0:1] * o_cmp + gb[..., 1:2] * o_slc + gb[..., 2:3] * o_win

    o = lax.map(block, jnp.arange(S // QB))
    return o.transpose(1, 0, 4, 2, 3, 5).reshape(B, S, G * Hg * d)


def hybrid_mixer(h, layer, w_in, lambda_q1, lambda_k1, lambda_q2, lambda_k2, diff_subln,
                 k_cmp_pos, k_cmp_w1, k_cmp_w2, v_cmp_pos, v_cmp_w1, v_cmp_w2, w_out):
    B, S, _ = h.shape
    cos, sin = rope_tables(S)
    proj = h @ w_in
    sizes = [DIFF_HEADS * 2 * HEAD_DIM, DIFF_HEADS * 2 * HEAD_DIM, DIFF_WIDTH, NSA_WIDTH,
             KV_COLS, KV_COLS, KV_COLS, KV_COLS, KV_COLS, KV_COLS, NSA_HEADS * N_BRANCHES]
    dq, dk, dv, nq, kc, vc, ks, vs, kw, vw, gt = jnp.split(
        proj, np.cumsum(sizes)[:-1].tolist(), axis=-1)

    dq = apply_rope(dq.reshape(B, S, DIFF_HEADS, 2, HEAD_DIM).transpose(0, 2, 3, 1, 4), cos, sin)
    dk = apply_rope(dk.reshape(B, S, DIFF_HEADS, 2, HEAD_DIM).transpose(0, 2, 3, 1, 4), cos, sin)
    dv = dv.reshape(B, S, DIFF_HEADS, DIFF_V_DIM).transpose(0, 2, 1, 3)
    lam_init = 0.8 - 0.6 * math.exp(-0.3 * layer)
    f32 = jnp.float32
    lam = (jnp.exp(jnp.sum(lambda_q1.astype(f32) * lambda_k1.astype(f32)))
           - jnp.exp(jnp.sum(lambda_q2.astype(f32) * lambda_k2.astype(f32))) + lam_init)
    o_diff = diff_attention(dq, dk, dv, lam)
    o_diff = (rms_norm(o_diff, diff_subln) * (1.0 - lam_init)).reshape(B, S, DIFF_WIDTH)

    def q_heads(t):
        return t.reshape(B, S, NSA_KV_GROUPS, NSA_HEADS_PER_GROUP, HEAD_DIM).transpose(0, 2, 3, 1, 4)

    def kv_heads(t):
        return t.reshape(B, S, NSA_KV_GROUPS, HEAD_DIM).transpose(0, 2, 1, 3)

    nq = apply_rope(q_heads(nq), cos, sin)
    kc = apply_rope(kv_heads(kc), cos, sin)
    ks = apply_rope(kv_heads(ks), cos, sin)
    kw = apply_rope(kv_heads(kw), cos, sin)
    gates = jax.nn.sigmoid(gt.reshape(B, S, NSA_KV_GROUPS, NSA_HEADS_PER_GROUP, N_BRANCHES)
                           .transpose(0, 2, 3, 1, 4))
    o_nsa = nsa_attention(nq, kc, kv_heads(vc), ks, kv_heads(vs), kw, kv_heads(vw), gates,
                          k_cmp_pos, k_cmp_w1, k_cmp_w2, v_cmp_pos, v_cmp_w1, v_cmp_w2)

    return jnp.concatenate([o_diff, o_nsa], axis=-1) @ w_out


def setup_inputs(seed: int = 0) -> dict:
    key = jax.random.key(seed)
    k = jax.random.split(key, 24)
    L = DEPTH

    def nrm(kk, shape, scale):
        return jax.random.normal(kk, shape, jnp.float32) * scale

    def gain(kk, n):
        return 1.0 + 0.05 * jax.random.normal(kk, (L, n), jnp.float32)

    cmp_in = CMP_BLOCK * HEAD_DIM
    return {
        'x': nrm(k[0], (BATCH, SEQ, D_MODEL), 1.0),
        'attn_pre_norm': gain(k[1], D_MODEL),
        'w_in': nrm(k[2], (L, D_MODEL, IN_COLS), D_MODEL ** -0.5),
        'lambda_q1': nrm(k[3], (L, HEAD_DIM), 0.1),
        'lambda_k1': nrm(k[4], (L, HEAD_DIM), 0.1),
        'lambda_q2': nrm(k[5], (L, HEAD_DIM), 0.1),
        'lambda_k2': nrm(k[6], (L, HEAD_DIM), 0.1),
        'diff_subln': gain(k[7], DIFF_V_DIM),
        'k_cmp_pos': nrm(k[8], (L, CMP_BLOCK, HEAD_DIM), 0.1),
        'k_cmp_w1': nrm(k[9], (L, cmp_in, CMP_HIDDEN), cmp_in ** -0.5),
        'k_cmp_w2': nrm(k[10], (L, CMP_HIDDEN, HEAD_DIM), CMP_HIDDEN ** -0.5),
        'v_cmp_pos': nrm(k[11], (L, CMP_BLOCK, HEAD_DIM), 0.1),
        'v_cmp_w1': nrm(k[12], (L, cmp_in, CMP_HIDDEN), cmp_in ** -0.5),
        'v_cmp_w2': nrm(k[13], (L, CMP_HIDDEN, HEAD_DIM), CMP_HIDDEN ** -0.5),
        'w_out': nrm(k[14], (L, MIX_WIDTH, D_MODEL), MIX_WIDTH ** -0.5),
        'attn_post_norm': gain(k[15], D_MODEL),
        'ffn_pre_norm': gain(k[16], D_MODEL),
        'w_gate': nrm(k[17], (L, D_MODEL, D_FF), D_MODEL ** -0.5),
        'w_up': nrm(k[18], (L, D_MODEL, D_FF), D_MODEL ** -0.5),
        'w_down': nrm(k[19], (L, D_FF, D_MODEL), D_FF ** -0.5),
        'ffn_post_norm': gain(k[20], D_MODEL),
    }


def reference(x, attn_pre_norm, w_in, lambda_q1, lambda_k1, lambda_q2, lambda_k2, diff_subln,
              k_cmp_pos, k_cmp_w1, k_cmp_w2, v_cmp_pos, v_cmp_w1, v_cmp_w2, w_out,
              attn_post_norm, ffn_pre_norm, w_gate, w_up, w_down, ffn_post_norm):
    for l in range(DEPTH):
        h = rms_norm(x, attn_pre_norm[l])
        mix = hybrid_mixer(h, l, w_in[l], lambda_q1[l], lambda_k1[l], lambda_q2[l], lambda_k2[l],
                           diff_subln[l], k_cmp_pos[l], k_cmp_w1[l], k_cmp_w2[l],
                           v_cmp_pos[l], v_cmp_w1[l], v_cmp_w2[l], w_out[l])
        x = x + rms_norm(mix, attn_post_norm[l])
        h = rms_norm(x, ffn_pre_norm[l])
        f = (jax.nn.silu(h @ w_gate[l]) * (h @ w_up[l])) @ w_down[l]
        x = x + rms_norm(f, ffn_post_norm[l])
    return x
```

```python
import numpy as np
import concourse.bass as bass
import concourse.mybir as mybir
from concourse.bass_utils import run_bass_kernel_spmd

F32 = mybir.dt.float32
BF16 = mybir.dt.bfloat16
ALU = mybir.AluOpType
AF = mybir.ActivationFunctionType
AX = mybir.AxisListType

S, D, NT, KC = 4096, 1024, 32, 8
DFF, FC = 2816, 22
NEG = -30000.0
EPS = 1e-6
LAM_INIT = 0.2
NCONST = 2432
ENGS = ("pe", "act", "dve", "pool", "sp")


class _Op:
    __slots__ = ("eng", "fn", "deps", "sig", "ticket", "chan", "is_dma")

    def __init__(self, eng, fn, deps, chan=None):
        self.eng, self.fn, self.deps = eng, fn, deps
        self.sig, self.ticket, self.chan = False, 0, chan
        self.is_dma = chan is not None


class Prog:
    def __init__(self, nc):
        self.nc = nc
        self.ops = []
        self.last_w = {}
        self.readers = {}
        self.chan_last = {}
        self.eng_last = {}

    def _deps(self, reads, writes):
        d = set()
        for r in reads:
            w = self.last_w.get(r)
            if w is not None:
                d.add(w)
        for w_ in writes:
            w = self.last_w.get(w_)
            if w is not None:
                d.add(w)
            d.update(self.readers.get(w_, ()))
        return d

    def _commit(self, idx, reads, writes):
        for r in reads:
            self.readers.setdefault(r, []).append(idx)
        for w_ in writes:
            self.last_w[w_] = idx
            self.readers[w_] = []

    def op(self, eng, fn, reads=(), writes=()):
        d = self._deps(reads, writes)
        idx = len(self.ops)
        if eng == "pe":
            d = {x for x in d if self.ops[x].eng != "pe" or self.ops[x].is_dma}
        self.ops.append(_Op(eng, fn, d))
        self._commit(idx, reads, writes)
        self.eng_last[eng] = idx
        return idx

    def dma(self, eng, chan, out, in_, reads=(), writes=(), after_all=False, **kw):
        d = self._deps(reads, writes)
        if after_all:
            d.update(self.eng_last.values())
        prev = self.chan_last.get(chan)
        if prev is not None:
            d.add(prev)
        idx = len(self.ops)
        self.ops.append(_Op(eng, lambda e: e.dma_start(out=out, in_=in_, **kw), d, chan=chan))
        self.chan_last[chan] = idx
        self._commit(idx, reads, writes)
        return idx

    def barrier(self, keep_chans=(), keep_res=()):
        deps = set(self.eng_last.values()) | {v for c, v in self.chan_last.items() if c not in keep_chans}
        kept = {r: self.last_w[r] for r in keep_res if r in self.last_w}
        for eng in ENGS:
            idx = len(self.ops)
            self.ops.append(_Op(eng, None, set(deps)))
            self.eng_last[eng] = idx
        self.last_w.clear()
        self.readers.clear()
        self.last_w.update(kept)

    def emit(self):
        nc, ops = self.nc, self.ops
        for o in ops:
            for d in o.deps:
                ops[d].sig = True
        cnt = {e: 0 for e in ENGS}
        chan_cnt = {}
        for o in ops:
            if o.is_dma:
                o.sig = True
                chan_cnt[o.chan] = chan_cnt.get(o.chan, 0) + 16
                o.ticket = chan_cnt[o.chan]
            elif o.fn is None:
                o.sig = False
            elif o.sig:
                cnt[o.eng] += 1
                o.ticket = cnt[o.eng]
        sems = {e: nc.alloc_semaphore("s_" + e) for e in ENGS if e != "sp"}
        csems = {c: nc.alloc_semaphore("c_" + str(c)) for c in chan_cnt}
        per_eng = {e: [] for e in ENGS}
        for i, o in enumerate(ops):
            per_eng[o.eng].append(i)

        def run(engname):
            def f(e):
                waited = {}
                for i in per_eng[engname]:
                    o = ops[i]
                    need = {}
                    for d in o.deps:
                        od = ops[d]
                        if od.fn is None and not od.is_dma:
                            continue
                        key = ("c", od.chan) if od.is_dma else ("e", od.eng)
                        if need.get(key, 0) < od.ticket:
                            need[key] = od.ticket
                    for key, val in need.items():
                        if waited.get(key, 0) >= val:
                            continue
                        waited[key] = val
                        e.wait_ge(csems[key[1]] if key[0] == "c" else sems[key[1]], val)
                    if o.fn is None:
                        continue
                    ins = o.fn(e)
                    if o.is_dma:
                        ins.then_inc(csems[o.chan], 16)
                    elif o.sig:
                        ins.then_inc(sems[o.eng], 1)
                if engname == "sp":
                    for c, v in chan_cnt.items():
                        e.wait_ge(csems[c], v)
                    for en in ("pe", "act", "dve", "pool"):
                        if cnt[en]:
                            e.wait_ge(sems[en], cnt[en])
            return f

        with nc.Block() as block:
            block.tensor(run("pe"))
            block.scalar(run("act"))
            block.vector(run("dve"))
            block.gpsimd(run("pool"))
            block.sync(run("sp"))
        return {e: len(per_eng[e]) for e in ENGS}


class Arena:
    def __init__(self, nc, base, limit):
        self.nc, self.o, self.limit = nc, base, limit

    def a(self, name, shape, dt):
        nb = 2 if dt == BF16 else 4
        size = int(np.prod(shape[1:])) * nb
        off = (self.o + 31) // 32 * 32
        self.o = off + size
        assert self.o <= self.limit, (name, self.o, self.limit)
        return self.nc.alloc_sbuf_tensor_at(name, list(shape), dt, offset=off)


def build(stage="all", dbg=False):
    nc = bass.Bass("TRN2", target_bir_lowering=False)
    P = Prog(nc)

    def din(name, shape, dt=F32):
        return nc.dram_tensor(name, list(shape), dt, kind="ExternalInput").ap()

    x = din("x", [S, D])
    consts = din("consts", [128, NCONST])
    gpreT_d = din("gpreT", [128, 8])
    g2T_d = din("g2T", [128, 8])
    w_in_u = din("w_in_u", [128, 8, 2840])
    lam4 = din("lam4", [4, 64])
    subln = din("subln", [1, 128])
    cw1 = [din("kw1", [64, 32, 256]), din("vw1", [64, 32, 256])]
    cposT = [din("kposT", [64, 32]), din("vposT", [64, 32])]
    cw2 = [din("kw2", [128, 2, 64]), din("vw2", [128, 2, 64])]
    w_out_r = din("w_out_r", [128, 8, D])
    gpostA_d = din("gpostA", [1, D])
    gpostF_d = din("gpostF", [1, D])
    wg_d = din("wg", [128, 8, DFF])
    wu_d = din("wu", [128, 8, DFF])
    wd_d = din("wd", [128, FC, D])
    out = nc.dram_tensor("out", [S, D], F32, kind="ExternalOutput").ap()
    ao_s = nc.dram_tensor("ao_s", [S, D], BF16, kind="ExternalOutput" if dbg else "Internal").ap()
    ao_v = ao_s.rearrange("(t p) c -> p t c", p=128)

    BASE = 16512
    TOP = 229344
    CA0 = Arena(nc, BASE, BASE + 3 * 1024)
    PC_OFF = BASE + 3 * 1024
    CA = Arena(nc, PC_OFF, BASE + 26 * 1024)
    HT_OFF = BASE + 26 * 1024
    UA = Arena(nc, HT_OFF + 65536, TOP)
    hT = nc.alloc_sbuf_tensor_at("hT", [128, 8, S], BF16, offset=HT_OFF)

    pb = [nc.alloc_psum_tensor("pb%d" % i, [128, 512], F32) for i in range(8)]
    pbb = [p[:, :].bitcast(BF16) for p in pb]

    def MM(o, lhsT, rhs, start, stop, r, w, sg=False):
        if sg:
            P.op("pe", lambda e: e.matmul(o, lhsT=lhsT, rhs=rhs, start=start, stop=stop, skip_group_check=True), r, w)
        else:
            P.op("pe", lambda e: e.matmul(o, lhsT=lhsT, rhs=rhs, start=start, stop=stop), r, w)

    def ACT(o, i, func, r, w, **kw):
        P.op("act", lambda e: e.activation(out=o, in_=i, func=func, **kw), r, w)

    def TS(eng, o, i, s1, s2, op0, op1, r, w):
        P.op(eng, lambda e: e.tensor_scalar(out=o, in0=i, scalar1=s1, scalar2=s2, op0=op0, op1=op1), r, w)

    def TS1(eng, o, i, s1, op0, r, w):
        P.op(eng, lambda e: e.tensor_scalar(out=o, in0=i, scalar1=s1, scalar2=None, op0=op0), r, w)

    def TT(eng, o, a, b, op, r, w):
        P.op(eng, lambda e: e.tensor_tensor(out=o, in0=a, in1=b, op=op), r, w)

    def STT(o, a, sc, b, op0, op1, r, w):
        P.op("dve", lambda e: e.scalar_tensor_tensor(out=o, in0=a, scalar=sc, in1=b, op0=op0, op1=op1), r, w)

    def CP(eng, o, i, r, w):
        if eng == "act":
            P.op("act", lambda e: e.copy(out=o, in_=i), r, w)
        else:
            P.op(eng, lambda e: e.tensor_copy(out=o, in_=i), r, w)

    def MS(eng, o, val, w):
        P.op(eng, lambda e: e.memset(o, val), (), w)

    def RECIP(o, i, r, w):
        P.op("dve", lambda e: e.reciprocal(out=o, in_=i), r, w)

    def TR(o, i, r, w):
        P.op("pe", lambda e: e.transpose(out=o, in_=i, identity=ident[:]), list(r) + ["ident"], w)

    def ASEL(o, i, pattern, cmp, fill, base, cm, r, w):
        P.op("pool", lambda e: e.affine_select(out=o, in_=i, pattern=pattern, compare_op=cmp, fill=fill,
                                               base=base, channel_multiplier=cm), r, w)

    cst = CA.a("cst", [128, NCONST], F32)
    P.dma("sp", "cst", cst[:], consts, (), ["cst"])
    cosT = cst[:, 0:1024].rearrange("p (t f) -> p t f", f=32)
    sinT = cst[:, 1024:2048].rearrange("p (t f) -> p t f", f=32)
    CAP0, FLO0, OV0 = 2048, 2176, 2304
    gpreT = CA0.a("gpreT", [128, 8], F32)
    g2T = CA0.a("g2T", [128, 8], F32)
    P.dma("sp", "gv", gpreT[:], gpreT_d, (), ["gpreT"])
    P.dma("sp", "gv", g2T[:], g2T_d, (), ["g2T"])
    ident = CA0.a("ident", [128, 128], BF16)
    maskC = CA.a("maskC", [128, 512], BF16)
    maskW = CA.a("maskW", [128, 512], BF16)
    mhalf = CA0.a("mhalf", [128, 4], F32)
    zf = UA.a("zf", [128, 512], F32)
    MS("pool", zf[:], 0.0, ["zf"])
    MS("pool", mhalf[:], -0.5, ["mhalf"])
    ASEL(ident[:], zf[:, 0:128], [[-1, 128]], ALU.not_equal, 1.0, 0, 1, ["zf"], ["ident"])
    ASEL(maskC[:], zf[:], [[0, 4], [1, 128]], ALU.is_ge, NEG, 0, -1, ["zf"], ["maskC"])
    ASEL(maskW[:], zf[:], [[0, 4], [-1, 128]], ALU.is_gt, NEG, 0, 1, ["zf"], ["maskW"])
    maskC4 = maskC[:, :].rearrange("p (h q) -> p h q", h=4)
    maskW4 = maskW[:, :].rearrange("p (h q) -> p h q", h=4)
    lamt = UA.a("lamt", [128, 4, 64], F32)
    P.dma("sp", "gv", lamt[:], lam4.partition_broadcast(128), (), ["lamt"])
    lprod = UA.a("lprod", [128, 2, 64], F32)
    lsum = CA.a("lsum", [128, 2], F32)
    lexp = CA.a("lexp", [128, 2], F32)
    neglam = CA.a("neglam", [128, 1], F32)
    TT("dve", lprod[:], lamt[:, 0:4:2, :], lamt[:, 1:4:2, :], ALU.mult, ["lamt"], ["lprod"])
    P.op("dve", lambda e: e.reduce_sum(out=lsum[:], in_=lprod[:], axis=AX.X), ["lprod"], ["lsum"])
    ACT(lexp[:], lsum[:], AF.Exp, ["lsum"], ["lexp"])
    TT("dve", neglam[:], lexp[:, 1:2], lexp[:, 0:1], ALU.subtract, ["lexp"], ["neglam0"])
    TS1("dve", neglam[:], neglam[:], -LAM_INIT, ALU.add, ["neglam0"], ["neglam"])
    subtab = CA.a("subtab", [128, 128], F32)
    P.dma("sp", "gv", subtab[:], subln[0].partition_broadcast(128), (), ["subtab0"])
    TS1("dve", subtab[:], subtab[:], (1.0 - LAM_INIT) * float(np.sqrt(128.0)), ALU.mult, ["subtab0"], ["subtab"])
    posT = [CA.a("posT%d" % i, [64, 32], BF16) for i in range(2)]
    w2 = [CA.a("w2_%d" % i, [128, 2, 64], BF16) for i in range(2)]
    for i in range(2):
        P.dma("pool", "gvp", posT[i][:], cposT[i], (), [("posT", i)])
        P.dma("pool", "gvp", w2[i][:], cw2[i], (), [("w2", i)])
    VCA = CA.a("VCA", [128, 2, 130], BF16)
    MS("dve", VCA[:, :, 64:65], 1.0, ["VCA1"])
    CP("dve", VCA[:, :, 65:129], cst[:, OV0:OV0 + 128].rearrange("p (n j) -> p n j", n=2), ["cst"], ["VCAov"])
    maskcmp = CA.a("maskcmp", [128, 33, 128], BF16)
    onesb = UA.a("onesb", [128, 128], BF16)
    MS("pool", onesb[:], 1.0, ["onesb"])
    for i in range(17):
        ASEL(maskcmp[:, i, :], onesb[:], [[1, 128]], ALU.is_ge, 0.0, 128 * i - 31, -16, ["onesb"], [("mcmp", i)])
    for i in range(16, 32):
        ASEL(maskcmp[:, 17 + i - 16, :], onesb[:], [[1, 128]], ALU.is_ge, 0.0, 128 * i - 31 - 2048, -16,
             ["onesb"], [("mcmp", 17 + i - 16)])
    ssA = CA.a("ssA", [128, 32], F32)
    rsA = CA.a("rsA", [128, 32], F32)
    junk = CA0.a("junk", [128, D], BF16)

    P.barrier()
    UA.o = HT_OFF + 65536
    import os
    n_diff = int(os.environ.get("DBG_HEADS", 4)) if stage != "none" else 0
    n_chunks = int(os.environ.get("DBG_CH", 16))
    XA = Arena(nc, TOP - 21 * 1024, TOP)
    xs = [XA.a("xs%d" % i, [128, 2, D], F32) for i in range(2)]
    xn = [XA.a("xn%d" % i, [128, D], BF16) for i in range(2)]
    gpre_bc = gpreT[:, :].unsqueeze(2).to_broadcast([128, 8, 128])
    x_v = x.rearrange("(t p) d -> p t d", p=128)

    def A_tile(t):
        tp, jx = t // 2, t % 2
        xb = tp % 2
        b = t % 2
        if jx == 0:
            P.dma("sp", "xs%d" % xb, xs[xb][:], x_v[:, 2 * tp:2 * tp + 2, :], (), [("xs", xb)])
        xin = xs[xb][:, jx, :]
        ACT(junk[:], xin, AF.Square, [("xs", xb)], [("ssA", t), "junk"], accum_out=ssA[:, t:t + 1])
        TS1("pool", rsA[:, t:t + 1], ssA[:, t:t + 1], 1024.0 * EPS, ALU.add, [("ssA", t)], [("rsA0", t)])
        TT("pool", rsA[:, t:t + 1], rsA[:, t:t + 1], mhalf[:, 0:1], ALU.pow, [("rsA0", t), "mhalf"], [("rsA", t)])
        TS("dve", xn[b][:], xin, rsA[:, t:t + 1], 32.0, ALU.mult, ALU.mult, [("xs", xb), ("rsA", t)], [("xn", b)])
        pT = pbb[6 + b].rearrange("p (k t) -> p k t", k=8)
        for kc in range(8):
            TR(pT[:, kc, :], xn[b][:, kc * 128:(kc + 1) * 128], [("xn", b)], [("pb", 6 + b)])
        TT("dve", hT[:, :, t * 128:(t + 1) * 128], pT, gpre_bc, ALU.mult, [("pb", 6 + b), "gpreT"], [("hT", t)])

    if n_diff == 0:
        for t in range(NT):
            A_tile(t)

    hT_all = [("hT", t) for t in range(NT)]

    import os

    class Step:
        __slots__ = ("pre", "s", "pv", "post")

        def __init__(self, s, pv, pre=None, post=None):
            self.s, self.pv, self.pre, self.post = s, pv, pre, post

    def run_steps(steps, L=2):
        n = len(steps)
        nxt = 0
        for k in range(n):
            while nxt < n and nxt <= k + L:
                st = steps[nxt]
                if st.pre is not None:
                    if nxt > k:
                        break
                    st.pre()
                st.s()
                nxt += 1
            steps[k].pv()
            if steps[k].post:
                steps[k].post()

    SBANKS = (3, 4, 1)
    sctr = [0]

    PA0 = Arena(nc, PC_OFF, HT_OFF)
    gF = PA0.a("gF", [128, D], F32)
    xs2 = [PA0.a("xs2_%d" % i, [128, 2, D], F32) for i in range(2)]
    st3 = PA0.a("st3", [128, 6, 64], F32)
    PA = Arena(nc, HT_OFF, TOP)
    Wg = PA.a("Wg", [128, 8, DFF], BF16)
    Wu = PA.a("Wu", [128, 8, DFF], BF16)
    assert PA.o <= HT_OFF + 65536 + 31424

    UA_mark = UA.o
    Wd_ = [UA.a("Wdh%d" % i, [128, 8, 384], BF16) for i in range(2)]
    QK = UA.a("QK", [128, 3, S], BF16)
    Vh = UA.a("Vh", [128, NT, 130], BF16)
    AOd = UA.a("AOd", [128, NT, 128], BF16)
    qk = [UA.a("qk%d" % i, [128, 256], F32) for i in range(2)]
    rp = [UA.a("rp%d" % i, [128, 256], BF16) for i in range(2)]
    tmpd = [UA.a("tmpd%d" % i, [128, 2, 128], F32) for i in range(2)]
    tmpp = [UA.a("tmpp%d" % i, [128, 2, 128], F32) for i in range(2)]
    pTs = [UA.a("pTs%d" % i, [128, 2, 256], BF16) for i in range(4)]
    rcd = UA.a("rcd", [128, 16, 4], F32)
    cf1 = UA.a("cf1", [128, 16, 2], F32)
    asb = [UA.a("asb%d" % i, [128, 128], F32) for i in range(2)]
    a2sb = [UA.a("a2sb%d" % i, [128, 128], F32) for i in range(2)]
    ssd = UA.a("ssd", [128, 4 * NT], F32)
    rsd = UA.a("rsd", [128, 4 * NT], F32)
    if n_diff:
        MS("dve", Vh[:, :, 128:129], 1.0, ["Vones"])
        MS("pool", QK[64:128, 0, :], 0.0, ["Qz0"])
        MS("pool", QK[0:64, 2, :], 0.0, ["Qz1"])
        P.dma("pool", "wdh0", Wd_[0][:], w_in_u[:, :, 0:384], (), [("Wdh", 0)])

    def diff_inproj(h):
        wb = h % 2
        if h + 1 < n_diff:
            P.dma("pool", "wdh%d" % ((h + 1) % 2), Wd_[(h + 1) % 2][:], w_in_u[:, :, (h + 1) * 384:(h + 2) * 384], (),
                  [("Wdh", (h + 1) % 2)])
        def dfront(t):
            tb = t % 2
            pin = pb[tb]
            for kc in range(8):
                MM(pin[:, 0:384], hT[:, kc, t * 128:(t + 1) * 128], Wd_[wb][:, kc, :], kc == 0, kc == 7,
                   [("hT", t), ("Wdh", wb)], [("pb", tb)])
            ACT(qk[tb][:], pin[:, 0:256], AF.Copy, [("pb", tb)], [("qk", tb)])
            ACT(Vh[:, t, 0:128], pin[:, 256:384], AF.Copy, [("pb", tb)], [("Vh", t)])
            v4 = qk[tb][:, :].rearrange("p (a h d) -> p a h d", a=4, h=2)
            r4 = rp[tb][:, :].rearrange("p (a h d) -> p a h d", a=4, h=2)
            cb = cosT[:, t, :].unsqueeze(1).to_broadcast([128, 4, 32])
            sb = sinT[:, t, :].unsqueeze(1).to_broadcast([128, 4, 32])
            td = tmpd[tb][:, :, :].rearrange("p a (h d) -> p a h d", h=4)
            tp = tmpp[tb][:, :, :].rearrange("p a (h d) -> p a h d", h=4)
            TT("dve", td[:, 0], v4[:, :, 0, :], cb, ALU.mult, [("qk", tb), "cst"], [("td0", tb)])
            TT("dve", td[:, 1], v4[:, :, 1, :], sb, ALU.mult, [("qk", tb), "cst"], [("td1", tb)])
            TT("dve", r4[:, :, 0, :], td[:, 0], td[:, 1], ALU.subtract, [("td0", tb), ("td1", tb)], [("rpA", tb)])
            TT("pool", tp[:, 0], v4[:, :, 1, :], cb, ALU.mult, [("qk", tb), "cst"], [("tp0", tb)])
            TT("pool", tp[:, 1], v4[:, :, 0, :], sb, ALU.mult, [("qk", tb), "cst"], [("tp1", tb)])
            TT("pool", r4[:, :, 1, :], tp[:, 0], tp[:, 1], ALU.add, [("tp0", tb), ("tp1", tb)], [("rpB", tb)])

        def dback(t):
            tb = t % 2
            pT = pbb[2].rearrange("p (k t) -> p k t", k=8)
            TR(pT[:, 0, :], rp[tb][:, 0:128], [("rpA", tb), ("rpB", tb)], [("pb", 2)])
            TR(pT[:, 1, :], rp[tb][:, 128:256], [("rpA", tb), ("rpB", tb)], [("pb", 2)])
            CP("act", QK[0:64, 0, t * 128:(t + 1) * 128], pT[0:64, 0, :], [("pb", 2)], [("QKa", t)])
            CP("act", QK[64:128, 2, t * 128:(t + 1) * 128], pT[64:128, 0, :], [("pb", 2)], [("QKb", t)])
            CP("act", QK[:, 1, t * 128:(t + 1) * 128], pT[:, 1, :], [("pb", 2)], [("QK", t)])

        fuseA = (h == 0)
        if fuseA:
            A_tile(0)
            A_tile(1)
        dfront(0)
        for t in range(NT):
            if fuseA and t + 2 < NT:
                A_tile(t + 2)
            if t + 1 < NT:
                dfront(t + 1)
            dback(t)

    def mk_diff_step(h, c, kt):
        j = kt - 2 * c
        k = sctr[0]
        sctr[0] += 1
        sbk, pbuf = SBANKS[k % 3], k % 4
        obank = (5, 6) if c % 2 == 0 else (7, 2)
        O = [pb[obank[m]][:, 0:258].rearrange("p (j e) -> p j e", e=129) for m in range(2)]
        Sv = pb[sbk][:, :].rearrange("p (m q) -> p m q", m=2)
        qres = [("QKa", 2 * c), ("QKa", 2 * c + 1), ("QKb", 2 * c), ("QKb", 2 * c + 1), "Qz0", "Qz1"]

        def s_():
            kT = QK[:, 1, kt * 128:(kt + 1) * 128]
            for m in range(2):
                qp = 2 * m
                if j < 0:
                    MM(Sv[:, m, :], kT, QK[:, qp, c * 256:(c + 1) * 256], True, True, [("QK", kt)] + qres, [("pb", sbk)])
                else:
                    MM(Sv[:, m, 128 * j:128 * j + 128], ident[:], maskC[:, 0:128], True, False, ["ident", "maskC"], [("pb", sbk)])
                    MM(Sv[:, m, 128 * j:128 * j + 128], kT, QK[:, qp, c * 256 + 128 * j:c * 256 + 128 * j + 128], False, True,
                       [("QK", kt)] + qres, [("pb", sbk)])
                    if j == 0:
                        MM(Sv[:, m, 128:256], kT, QK[:, qp, c * 256 + 128:c * 256 + 256], True, True,
                           [("QK", kt)] + qres, [("pb", sbk)])
            q0 = 128 * j if j > 0 else 0
            ACT(pTs[pbuf][:, :, q0:256], Sv[:, :, q0:256], AF.Exp, [("pb", sbk)], [("pTs", pbuf)], scale=0.125)

        def pv_():
            for m in range(2):
                for jj in range(max(j, 0), 2):
                    MM(O[m][:, jj, :], pTs[pbuf][:, m, 128 * jj:128 * jj + 128], Vh[:, kt, 0:129],
                       kt == 0 and jj == 0, kt == 2 * c + jj, [("pTs", pbuf), ("Vh", kt), "Vones"], [("pb", obank[m])], sg=True)

        def post_():
            for m in range(2):
                RECIP(rcd[:, c, 2 * m:2 * m + 2], O[m][:, :, 128], [("pb", obank[m])], [("rcd", c, m)])
            TS1("dve", cf1[:, c, :], rcd[:, c, 2:4], neglam[:, 0:1], ALU.mult, [("rcd", c, 1), "neglam"], [("cf1", c)])
            for jj in range(2):
                t = 2 * c + jj
                col = h * NT + t
                ab = t % 2
                TS1("dve", asb[ab][:], O[0][:, jj, 0:128], rcd[:, c, jj:jj + 1], ALU.mult,
                    [("pb", obank[0]), ("rcd", c, 0)], [("asb", ab)])
                STT(a2sb[ab][:], O[1][:, jj, 0:128], cf1[:, c, jj:jj + 1], asb[ab][:], ALU.mult, ALU.add,
                    [("pb", obank[1]), ("cf1", c), ("asb", ab)], [("a2sb", ab)])
                ACT(junk[:, 0:128], a2sb[ab][:], AF.Square, [("a2sb", ab)], [("ssd", col), "junk"], accum_out=ssd[:, col:col + 1])
                TS1("pool", rsd[:, col:col + 1], ssd[:, col:col + 1], 128.0 * EPS, ALU.add, [("ssd", col)], [("rsd0", col)])
                TT("pool", rsd[:, col:col + 1], rsd[:, col:col + 1], mhalf[:, 0:1], ALU.pow, [("rsd0", col), "mhalf"], [("rsd", col)])
                STT(AOd[:, t, :], a2sb[ab][:], rsd[:, col:col + 1], subtab[:], ALU.mult, ALU.mult,
                    [("a2sb", ab), ("rsd", col), "subtab"], [("AOd", t)])
            if c == n_chunks - 1:
                P.dma("sp", "aod", ao_v[:, :, h * 128:(h + 1) * 128], AOd[:], [("AOd", t) for t in range(NT)], [("ao_s", h)])

        pre = (lambda: diff_inproj(h)) if (c == 0 and kt == 0) else None
        post = post_ if kt == 2 * c + 1 else None
        return Step(s_, pv_, pre, post)

    run_steps([mk_diff_step(h, c, kt) for h in range(n_diff) for c in range(n_chunks) for kt in range(2 * c + 2)])
    UA.o = UA_mark

    n_nsa = 2 if stage in ("all", "nsa") else 0
    n_q = int(os.environ.get("DBG_NQ", NT))
    P.barrier()
    UA_mark = UA.o
    Wn = UA.a("Wn", [128, 8, 652], BF16)
    W1 = UA.a("W1", [64, 16, 256], BF16)
    qn = [UA.a("qn%d" % i, [128, 448], F32) for i in range(2)]
    rn = [UA.a("rn%d" % i, [128, 512], BF16) for i in range(2)]
    tnd = [UA.a("tnd%d" % i, [128, 2, 224], F32) for i in range(2)]
    tnp = [UA.a("tnp%d" % i, [128, 2, 224], F32) for i in range(2)]
    assert UA.o - (HT_OFF + 65536) >= 31424
    nqT = UA.a("nqT", [128, 4, S], BF16)
    KV4 = UA.a("KV4", [128, 4, S], BF16)
    VS = UA.a("VS", [128, NT, 66], BF16)
    VW = UA.a("VW", [128, NT, 66], BF16)
    gat = UA.a("gat", [128, NT, 12], F32)
    AOn = [UA.a("AOn%d" % i, [128, 2, 256], BF16) for i in range(2)]
    kcmpT = UA.a("kcmpT", [64, 256], BF16)
    hsb = UA.a("hsb", [128, 2, 256], BF16)
    bsb = UA.a("bsb", [128, 2], F32)
    gex = UA.a("gex", [128, 12], F32)
    pcs = [UA.a("pcs%d" % i, [128, 4, 128], BF16) for i in range(2)]
    pss = [UA.a("pss%d" % i, [128, 4, 128], BF16) for i in range(4)]
    NB = CA.a("NB", [128, 128], BF16)
    imp = CA.a("imp", [128, 64], F32)
    imp2 = CA.a("imp2", [128, 64], F32)
    mrp = CA.a("mrp", [128, 64], F32)
    m8 = UA.a("m8", [128, 16], F32)
    rc4 = [UA.a("rc4_%d" % i, [128, 4], F32) for i in range(2)]
    cco = [UA.a("cco%d" % i, [128, 3, 4], F32) for i in range(2)]
    acc = [UA.a("acc%d" % i, [128, 4, 64], F32) for i in range(2)]
    tac = UA.a("tac", [128, 4, 64], F32)
    if n_nsa:
        MS("dve", VS[:, :, 64:65], 1.0, ["VSones"])
        MS("dve", VW[:, :, 64:65], 1.0, ["VWones"])
        MS("dve", NB[:, 0:64], 0.0, ["NB0"])
        MS("dve", hsb[:, :, 255:256], 0.0, ["hsbpad"])
        Et = nqT[0:64, 0, :]
        MS("pool", Et, 1.0, ["Et"])
        ASEL(Et, Et, [[1, S]], ALU.is_ge, 0.0, 0, -64, ["Et"], ["Et"])
        ASEL(Et, Et, [[-1, S]], ALU.is_ge, 0.0, 63, 64, ["Et"], ["Et"])
        CP("dve", KV4[64:128, 1, :], Et, ["Et"] + [("nqT", t) for t in range(NT)], ["E"])
    kv_all = [("KV4", t) for t in range(NT)]
    vca_r = ["VCAv", "VCA1", "VCAov"]

    def nsa_pre(g):
        P.dma("pool", "wn", Wn[:], w_in_u[:, :, 1536 + g * 652:1536 + (g + 1) * 652], (), ["Wn"])
        def nfront(t):
            tb = t % 2
            pa, pbk, pbi = pb[tb], pb[(2, 5)[tb]], (2, 5)[tb]
            for kc in range(8):
                MM(pa[:, :], hT[:, kc, t * 128:(t + 1) * 128], Wn[:, kc, 0:512], kc == 0, kc == 7,
                   [("hT", t), "Wn"], [("pb", tb)])
            for kc in range(8):
                MM(pbk[:, 0:140], hT[:, kc, t * 128:(t + 1) * 128], Wn[:, kc, 512:652], kc == 0, kc == 7,
                   [("hT", t), "Wn"], [("pb", pbi)])
            ACT(qn[tb][:], pa[:, 0:448], AF.Copy, [("pb", tb)], [("qn", tb)])
            ACT(rn[tb][:, 448:512], pa[:, 448:512], AF.Copy, [("pb", tb)], [("rnV", tb)])
            ACT(VS[:, t, 0:64], pbk[:, 0:64], AF.Copy, [("pb", pbi)], [("VS", t)])
            ACT(VW[:, t, 0:64], pbk[:, 64:128], AF.Copy, [("pb", pbi)], [("VW", t)])
            ACT(gex[:], pbk[:, 128:140], AF.Exp, [("pb", pbi)], ["gex"], scale=-1.0)
            TS1("dve", gex[:], gex[:], 1.0, ALU.add, ["gex"], ["gex1"])
            RECIP(gat[:, t, :], gex[:], ["gex1"], [("gat", t)])
            v4 = qn[tb][:, :].rearrange("p (a h d) -> p a h d", a=7, h=2)
            r4 = rn[tb][:, 0:448].rearrange("p (a h d) -> p a h d", a=7, h=2)
            cb = cosT[:, t, :].unsqueeze(1).to_broadcast([128, 7, 32])
            sb = sinT[:, t, :].unsqueeze(1).to_broadcast([128, 7, 32])
            td = tnd[tb][:, :, :].rearrange("p a (h d) -> p a h d", h=7)
            tp = tnp[tb][:, :, :].rearrange("p a (h d) -> p a h d", h=7)
            TT("dve", td[:, 0], v4[:, :, 0, :], cb, ALU.mult, [("qn", tb), "cst"], [("nd0", tb)])
            TT("dve", td[:, 1], v4[:, :, 1, :], sb, ALU.mult, [("qn", tb), "cst"], [("nd1", tb)])
            TT("dve", r4[:, :, 0, :], td[:, 0], td[:, 1], ALU.subtract, [("nd0", tb), ("nd1", tb)], [("rnA", tb)])
            TT("pool", tp[:, 0], v4[:, :, 1, :], cb, ALU.mult, [("qn", tb), "cst"], [("np0", tb)])
            TT("pool", tp[:, 1], v4[:, :, 0, :], sb, ALU.mult, [("qn", tb), "cst"], [("np1", tb)])
            TT("pool", r4[:, :, 1, :], tp[:, 0], tp[:, 1], ALU.add, [("np0", tb), ("np1", tb)], [("rnB", tb)])

        def nback(t):
            tb = t % 2
            pT = pbb[3 + tb].rearrange("p (k t) -> p k t", k=8)
            for k8 in range(8):
                TR(pT[0:64, k8, :], rn[tb][:, k8 * 64:(k8 + 1) * 64], [("rnA", tb), ("rnB", tb), ("rnV", tb)], [("pb", 3 + tb)])
            CP("dve", nqT[0:64, :, t * 128:(t + 1) * 128], pT[0:64, 0:4, :], [("pb", 3 + tb)], [("nqT", t)])
            CP("dve", KV4[0:64, :, t * 128:(t + 1) * 128], pT[0:64, 4:8, :], [("pb", 3 + tb)], [("KV4", t)])

        nfront(0)
        for t in range(NT):
            if t + 1 < NT:
                nfront(t + 1)
            nback(t)

        for kvi in range(2):
            plane = 0 if kvi == 0 else 3
            hid = pb[3][:, :].rearrange("p (j n) -> p j n", j=2)
            for half in range(2):
                P.dma("pool", "w1", W1[:], cw1[kvi][:, half * 16:(half + 1) * 16, :], (), ["W1"])
                for jh in range(2):
                    for l16 in range(16):
                        l = half * 16 + l16
                        MM(hid[:, jh, 0:255], W1[:, l16, jh * 128:(jh + 1) * 128], KV4[0:64, plane, l:l + 16 * 254 + 1:16],
                           l == 0 and jh == 0, l == 31, ["W1"] + kv_all, [("pb", 3)], sg=True)
                for jh in range(2):
                    for l16 in range(16):
                        l = half * 16 + l16
                        MM(pb[4][:, jh:jh + 1], W1[:, l16, jh * 128:(jh + 1) * 128], posT[kvi][:, l:l + 1],
                           l == 0 and jh == 0, l == 31, ["W1", ("posT", kvi)], [("pb", 4)], sg=True)
            CP("dve", bsb[:], pb[4][:, 0:2], [("pb", 4)], ["bsb"])
            for jh in range(2):
                ACT(hsb[:, jh, 0:255], hid[:, jh, 0:255], AF.Silu, [("pb", 3), "bsb", "hsbpad"], [("hsb", jh)],
                    bias=bsb[:, jh:jh + 1])
            if kvi == 0:
                for jh in range(2):
                    MM(pb[5][0:64, 0:256], w2[0][:, jh, :], hsb[:, jh, :], jh == 0, jh == 1,
                       [("hsb", 0), ("hsb", 1), ("w2", 0)], [("pb", 5)])
                CP("dve", kcmpT[:], pb[5][0:64, 0:256], [("pb", 5)], ["kcmpT"])
            else:
                pv = pb[6][:, 0:128].rearrange("p (n d) -> p n d", n=2)
                for nt in range(2):
                    for jh in range(2):
                        MM(pv[:, nt, :], hsb[:, jh, nt * 128:(nt + 1) * 128], w2[1][:, jh, :], jh == 0, jh == 1,
                           [("hsb", 0), ("hsb", 1), ("w2", 1)], [("pb", 6)])
                CP("dve", VCA[:, :, 0:64], pv, [("pb", 6)], ["VCAv"])
        if g == n_nsa - 1 and stage == "all":
            for k in range(4):
                sl = slice(k * 704, (k + 1) * 704)
                P.dma("pool", "wg", Wg[:, :, sl], wg_d[:, :, sl], (), [("Wg", k)], after_all=True)
                P.dma("pool", "wu", Wu[:, :, sl], wu_d[:, :, sl], (), [("Wu", k)], after_all=True)

    def mk_cmp_step(g, i, nt, first, last):
        pi = i % 2
        k = sctr[0]
        sctr[0] += 1
        sbk = SBANKS[k % 3]
        pc = pss[k % 4]
        pcr = ("pss", k % 4)
        Sc = pb[sbk][:, :].rearrange("p (h q) -> p h q", h=4)
        q64 = nqT[0:64, :, i * 128:(i + 1) * 128]
        U = [pb[5 + b][:, 0:258].rearrange("p (j e) -> p j e", e=129) for b in range(2)]

        def s_():
            MM(Sc, kcmpT[:, nt * 128:(nt + 1) * 128], q64, True, True, ["kcmpT", ("nqT", i)], [("pb", sbk)])
            ACT(pc[:], Sc, AF.Exp, [("pb", sbk)], [pcr], scale=0.125)
            midx = None
            if nt == 0 and i <= 16:
                midx = i
            if nt == 1:
                midx = 17 + i - 16
            if midx is not None:
                TT("pool", pc[:], pc[:], maskcmp[:, midx, :].unsqueeze(1).to_broadcast([128, 4, 128]),
                   ALU.mult, [pcr, ("mcmp", midx)], [pcr])

        def pv_():
            for hh in range(4):
                MM(U[hh // 2][:, hh % 2, :], pc[:, hh, :], VCA[:, nt, 0:129], first and hh % 2 == 0, last,
                   [pcr] + vca_r, [("pb", 5 + hh // 2)], sg=True)

        def post_():
            for b in range(2):
                TS1("dve", rc4[pi][:, 2 * b:2 * b + 2], U[b][:, :, 64], 1e-30, ALU.add, [("pb", 5 + b)], [("rc4a", pi, b)])
            RECIP(rc4[pi][:], rc4[pi][:], [("rc4a", pi, 0), ("rc4a", pi, 1)], [("rc4", pi)])
            TS1("dve", imp[:], U[0][:, 0, 65:129], rc4[pi][:, 0:1], ALU.mult, [("pb", 5), ("rc4", pi)], ["imp"])
            for hh in range(1, 4):
                STT(imp[:], U[hh // 2][:, hh % 2, 65:129], rc4[pi][:, hh:hh + 1], imp[:], ALU.mult, ALU.add,
                    [("pb", 5 + hh // 2), ("rc4", pi), "imp"], ["imp"])
            c0 = 62 - 2 * i
            TT("dve", imp2[:], imp[:], cst[:, CAP0 + c0:CAP0 + c0 + 64], ALU.min, ["imp", "cst"], ["imp2"])
            TT("dve", imp2[:], imp2[:], cst[:, FLO0 + c0:FLO0 + c0 + 64], ALU.max, ["imp2", "cst"], ["imp2"])
            MS("dve", imp2[:, 0:1], 3e30, ["imp2"])
            P.op("dve", lambda e: e.max(out=m8[:, 0:8], in_=imp2[:]), ["imp2"], ["m8a"])
            P.op("dve", lambda e: e.match_replace(out=mrp[:], in_to_replace=m8[:, 0:8], in_values=imp2[:], imm_value=-3e30),
                 ["imp2", "m8a"], ["mrp"])
            P.op("dve", lambda e: e.max(out=m8[:, 8:16], in_=mrp[:]), ["mrp"], ["m8b"])
            TS("dve", NB[:, 64:128], imp2[:], m8[:, 15:16], NEG, ALU.is_lt, ALU.mult, ["imp2", "m8b"], ["NB"])
            pTn = pbb[0][:, 0:128]
            TR(pTn, NB[:, :], ["NB", "NB0"], [("pb", 0)])
            CP("act", nqT[64:128, :, i * 128:(i + 1) * 128], pTn[64:128, :].unsqueeze(1).to_broadcast([64, 4, 128]),
               [("pb", 0)], [("nqTb", i)])
            TT("dve", cco[pi][:, 0, :], rc4[pi][:], gat[:, i, 0:12:3], ALU.mult, [("rc4", pi), ("gat", i)], [("cco", pi, 0)])
            for b in range(2):
                TT("dve", acc[pi][:, 2 * b:2 * b + 2, :], U[b][:, :, 0:64],
                   cco[pi][:, 0, 2 * b:2 * b + 2].unsqueeze(2).to_broadcast([128, 2, 64]), ALU.mult,
                   [("pb", 5 + b), ("cco", pi, 0)], [("acc", pi)])

        return Step(s_, pv_, None, post_ if last else None)

    def mk_br_step(g, i, br, kt, kts):
        pi = i % 2
        k = sctr[0]
        sctr[0] += 1
        sbk, pbuf = SBANKS[k % 3], k % 4
        Ss = pb[sbk][:, :].rearrange("p (h q) -> p h q", h=4)
        q64 = nqT[0:64, :, i * 128:(i + 1) * 128]
        q128 = nqT[:, :, i * 128:(i + 1) * 128]
        if br == 2:
            obk, Vt, vres = 2, VW, "VW"
        else:
            obk, Vt, vres = 7, VS, "VS"
        Ob = pb[obk][:, 0:260].rearrange("p (h e) -> p h e", e=65)

        def s_():
            first = True
            if kt == i:
                MM(Ss, ident[:], maskC4, True, False, ["ident", "maskC"], [("pb", sbk)])
                first = False
            elif br == 2 and kt == i - 4:
                MM(Ss, ident[:], maskW4, True, False, ["ident", "maskW"], [("pb", sbk)])
                first = False
            if br == 2:
                MM(Ss, KV4[0:64, 2, kt * 128:(kt + 1) * 128], q64, first, True, [("KV4", kt), ("nqT", i)], [("pb", sbk)])
            else:
                MM(Ss, KV4[:, 1, kt * 128:(kt + 1) * 128], q128, first, True,
                   [("KV4", kt), "E", ("nqTb", i), ("nqT", i)], [("pb", sbk)])
            ACT(pss[pbuf][:], Ss, AF.Exp, [("pb", sbk)], [("pss", pbuf)], scale=0.125)

        def pv_():
            for hh in range(4):
                MM(Ob[:, hh, :], pss[pbuf][:, hh, :], Vt[:, kt, 0:65], kt == kts[0] and hh == 0, kt == kts[-1],
                   [("pss", pbuf), (vres, kt), vres + "ones"], [("pb", obk)], sg=True)

        def post_():
            RECIP(cco[pi][:, br, :], Ob[:, :, 64], [("pb", obk)], [("ccoa", pi, br)])
            TT("dve", cco[pi][:, br, :], cco[pi][:, br, :], gat[:, i, br:12:3], ALU.mult, [("ccoa", pi, br), ("gat", i)], [("cco", pi, br)])
            TT("dve", tac[:], Ob[:, :, 0:64], cco[pi][:, br, :].unsqueeze(2).to_broadcast([128, 4, 64]), ALU.mult,
               [("pb", obk), ("cco", pi, br)], ["tac"])
            TT("pool", acc[pi][:], acc[pi][:], tac[:], ALU.add, ["tac", ("acc", pi)], [("acc", pi)])
            if br == 1:
                ab = (i // 2) % 2
                CP("pool", AOn[ab][:, i % 2, :], acc[pi][:, :, :].rearrange("p h d -> p (h d)"), [("acc", pi)], [("AOn", ab)])
                if i % 2 == 1:
                    P.dma("act", "aon%d" % ab, ao_v[:, i - 1:i + 1, 512 + g * 256:512 + (g + 1) * 256], AOn[ab][:],
                          [("AOn", ab)], [("ao_n", g, i)])

        return Step(s_, pv_, None, post_ if kt == kts[-1] else None)

    def cmp_steps(g, i):
        nts = [0] if i < 16 else [0, 1]
        return [mk_cmp_step(g, i, nt, nt == nts[0], nt == nts[-1]) for nt in nts]

    nsa_steps = []
    for g in range(n_nsa):
        gsteps = []
        if n_q:
            gsteps += cmp_steps(g, 0)
        for i in range(n_q):
            kts_w = list(range(max(0, i - 4), i + 1))
            gsteps += [mk_br_step(g, i, 2, kt, kts_w) for kt in kts_w]
            if i + 1 < n_q:
                gsteps += cmp_steps(g, i + 1)
            kts_s = list(range(0, i + 1))
            gsteps += [mk_br_step(g, i, 1, kt, kts_s) for kt in kts_s]
        if gsteps:
            gsteps[0].pre = (lambda g=g: nsa_pre(g))
        else:
            nsa_pre(g)
        nsa_steps += gsteps
    run_steps(nsa_steps)
    UA.o = UA_mark

    if stage in ("diff", "nsa", "none"):
        st = P.emit()
        return nc, st

    P.barrier()
    Wdn = PA.a("Wdn", [128, FC, D], BF16)
    tm = PA.a("tm", [128, 2, D], F32)
    h2n = PA.a("h2n", [128, D], BF16)
    h2T_off = (PA.o + 31) // 32 * 32
    h2T = PA.a("h2T", [128, 8, 512], BF16)
    ov_base = PA.o
    Wo = PA.a("Wo", [128, 8, D], BF16)
    gA = PA.a("gA", [128, D], F32)
    aos = [PA.a("aos%d" % i, [128, 2, D], BF16) for i in range(2)]
    aoT = [nc.alloc_sbuf_tensor_at("aoT%d" % i, [128, 8, 128], BF16, offset=h2T_off + 2048 * i) for i in range(2)]
    PA.o = ov_base
    sg = [PA.a("sg%d" % i, [128, 512], F32) for i in range(2)]
    actT = PA.a("actT", [128, FC, 512], BF16)
    P.dma("pool", "wo", Wo[:], w_out_r, (), ["Wo"])
    P.dma("sp", "gv", gA[:], gpostA_d[0].partition_broadcast(128), (), ["gA0"])
    P.dma("sp", "gv", gF[:], gpostF_d[0].partition_broadcast(128), (), ["gF0"])
    TS1("pool", gA[:], gA[:], 32.0, ALU.mult, ["gA0"], ["gA"])
    TS1("pool", gF[:], gF[:], 32.0, ALU.mult, ["gF0"], ["gF"])
    for k in range(2):
        sl = slice(k * 11, (k + 1) * 11)
        P.dma("pool", "wd", Wdn[:, sl, :], wd_d[:, sl, :], (), [("Wdn", k)])
    g2_bc = g2T[:, :].unsqueeze(2).to_broadcast([128, 8, 128])
    out_v = out.rearrange("(t p) d -> p t d", p=128)

    def norm_scale(col, ssrc_ap, res_in, tag):
        TS1("pool", st3[:, 1, col:col + 1], ssrc_ap, 1024.0 * EPS, ALU.add, res_in, [(tag + "0", col)])
        TT("pool", st3[:, 1, col:col + 1], st3[:, 1, col:col + 1], mhalf[:, 0:1], ALU.pow, [(tag + "0", col), "mhalf"], [(tag, col)])

    def c0_front(t):
        tp, jx = t // 2, t % 2
        xb = tp % 2
        b = t % 2
        mb = (2, 4)[b]
        if jx == 0:
            P.dma("sp", "aos%d" % xb, aos[xb][:], ao_v[:, 2 * tp:2 * tp + 2, :], (), [("aos", xb)])
            P.dma("act", "xs2%d" % xb, xs2[xb][:], x_v[:, 2 * tp:2 * tp + 2, :], (), [("xs2", xb)])
        pT = pbb[b].rearrange("p (k t) -> p k t", k=8)
        for kc in range(8):
            TR(pT[:, kc, :], aos[xb][:, jx, kc * 128:(kc + 1) * 128], [("aos", xb)], [("pb", b)])
        CP("act", aoT[b][:], pT, [("pb", b)], [("aoT", b)])
        for nh in range(2):
            for kc in range(8):
                MM(pb[mb + nh][:, :], aoT[b][:, kc, :], Wo[:, kc, nh * 512:(nh + 1) * 512], kc == 0, kc == 7,
                   [("aoT", b), "Wo"], [("pb", mb + nh)])

    def c0_back(t):
        tp, jx = t // 2, t % 2
        xb = tp % 2
        mb = (2, 4)[t % 2]
        for nh in range(2):
            ACT(junk[:, 0:512], pb[mb + nh][:, :], AF.Square, [("pb", mb + nh)], [("ssm", t, nh), "junk"],
                accum_out=st3[:, 2 + nh, t:t + 1])
        TT("pool", st3[:, 0, t:t + 1], st3[:, 2, t:t + 1], st3[:, 3, t:t + 1], ALU.add, [("ssm", t, 0), ("ssm", t, 1)], [("ssmt", t)])
        norm_scale(t, st3[:, 0, t:t + 1], [("ssmt", t)], "rsm")
        for nh in range(2):
            STT(tm[:, jx, nh * 512:(nh + 1) * 512], pb[mb + nh][:, :], st3[:, 1, t:t + 1], gA[:, nh * 512:(nh + 1) * 512],
                ALU.mult, ALU.mult, [("pb", mb + nh), ("rsm", t), "gA"], [("tm", jx, nh)])
        if jx == 1:
            TT("pool", tm[:], tm[:], xs2[xb][:], ALU.add, [("tm", 0, 0), ("tm", 0, 1), ("tm", 1, 0), ("tm", 1, 1), ("xs2", xb)],
               ["x1", ("tm", 0, 0), ("tm", 0, 1), ("tm", 1, 0), ("tm", 1, 1)])
            P.dma("sp", "x1o", out_v[:, 2 * tp:2 * tp + 2, :], tm[:], ["x1", ("tm", 0, 0), ("tm", 0, 1), ("tm", 1, 0), ("tm", 1, 1)], [("out", tp)])

    c0_front(0)
    for t in range(NT):
        if t + 1 < NT:
            c0_front(t + 1)
        c0_back(t)

    P.barrier(keep_chans=("wd",), keep_res=[("Wdn", 0), ("Wdn", 1)])
    wg_r = [("Wg", k) for k in range(4)]
    wu_r = [("Wu", k) for k in range(4)]
    wd_r = [("Wdn", k) for k in range(2)]
    tm_all = [("tm", 0, 0), ("tm", 0, 1), ("tm", 1, 0), ("tm", 1, 1)]
    for blk in range(8):
        for tt in range(4):
            t = blk * 4 + tt
            b = t % 2
            tp, jx = t // 2, t % 2
            xb = tp % 2
            if jx == 0:
                P.dma("sp", "xs2%d" % xb, xs2[xb][:], out_v[:, 2 * tp:2 * tp + 2, :], [("out", tp)], [("xs2", xb)])
            xin = xs2[xb][:, jx, :]
            ACT(junk[:], xin, AF.Square, [("xs2", xb)], [("ss2", t), "junk"], accum_out=st3[:, 4, t:t + 1])
            TS1("pool", st3[:, 5, t:t + 1], st3[:, 4, t:t + 1], 1024.0 * EPS, ALU.add, [("ss2", t)], [("rs20", t)])
            TT("pool", st3[:, 5, t:t + 1], st3[:, 5, t:t + 1], mhalf[:, 0:1], ALU.pow, [("rs20", t), "mhalf"], [("rs2", t)])
            TS("dve", h2n[:], xin, st3[:, 5, t:t + 1], 32.0, ALU.mult, ALU.mult, [("xs2", xb), ("rs2", t)], ["h2n"])
            pT = pbb[b].rearrange("p (k t) -> p k t", k=8)
            for kc in range(8):
                TR(pT[:, kc, :], h2n[:, kc * 128:(kc + 1) * 128], ["h2n"], [("pb", b)])
            TT("dve", h2T[:, :, tt * 128:(tt + 1) * 128], pT, g2_bc, ALU.mult, [("pb", b), "g2T"], [("h2T", tt)])
        h2r = [("h2T", tt) for tt in range(4)]
        for fc in range(FC):
            p2 = fc % 2
            gb, ub = (4, 5) if p2 == 0 else (6, 7)
            for kc in range(8):
                MM(pb[gb][:, :], Wg[:, kc, fc * 128:(fc + 1) * 128], h2T[:, kc, :], kc == 0, kc == 7, wg_r + h2r, [("pb", gb)])
            for kc in range(8):
                MM(pb[ub][:, :], Wu[:, kc, fc * 128:(fc + 1) * 128], h2T[:, kc, :], kc == 0, kc == 7, wu_r + h2r, [("pb", ub)])
            ACT(sg[p2][:], pb[gb][:, :], AF.Silu, [("pb", gb)], [("sg", p2)])
            TT("dve", actT[:, fc, :], sg[p2][:], pb[ub][:, :], ALU.mult, [("sg", p2), ("pb", ub)], [("actT", fc)])
        ar = [("actT", fc) for fc in range(FC)]
        for tt in range(4):
            t = blk * 4 + tt
            tp, jx = t // 2, t % 2
            for nh in range(2):
                for fc in range(FC):
                    MM(pb[2 + nh][:, :], actT[:, fc, tt * 128:(tt + 1) * 128], Wdn[:, fc, nh * 512:(nh + 1) * 512],
                       fc == 0, fc == FC - 1, ar + wd_r, [("pb", 2 + nh)])
            for nh in range(2):
                ACT(junk[:, 0:512], pb[2 + nh][:, :], AF.Square, [("pb", 2 + nh)], [("ssy", t, nh), "junk"],
                    accum_out=st3[:, 2 + nh, 32 + t:33 + t])
            TT("pool", st3[:, 0, 32 + t:33 + t], st3[:, 2, 32 + t:33 + t], st3[:, 3, 32 + t:33 + t], ALU.add,
               [("ssy", t, 0), ("ssy", t, 1)], [("ssyt", t)])
            norm_scale(32 + t, st3[:, 0, 32 + t:33 + t], [("ssyt", t)], "rsy")
            for nh in range(2):
                STT(tm[:, jx, nh * 512:(nh + 1) * 512], pb[2 + nh][:, :], st3[:, 1, 32 + t:33 + t], gF[:, nh * 512:(nh + 1) * 512],
                    ALU.mult, ALU.mult, [("pb", 2 + nh), ("rsy", 32 + t), "gF"], [("tm", jx, nh)])
            if jx == 1:
                P.dma("pool", "acc", out_v[:, 2 * tp:2 * tp + 2, :], tm[:], tm_all + [("out", tp)],
                      [("out", tp)] + tm_all, accum_op=ALU.add)
    st = P.emit()
    return nc, st


def _consts():
    c = np.zeros((128, NCONST), np.float32)
    inv = 1.0 / (10000.0 ** (np.arange(0, 64, 2, dtype=np.float32) / 64.0))
    pos = (np.arange(NT)[None, :, None] * 128 + np.arange(128)[:, None, None]).astype(np.float32)
    ang = pos * inv[None, None, :]
    c[:, 0:1024] = np.cos(ang).astype(np.float32).reshape(128, 1024)
    c[:, 1024:2048] = np.sin(ang).astype(np.float32).reshape(128, 1024)
    hi = (np.arange(128) >= 64).astype(np.int64)[:, None]
    m = np.arange(128)[None, :] - 62
    c[:, 2048:2176] = np.where(m <= hi, 1e30, -1e30)
    c[:, 2176:2304] = np.where(m == hi, 2e30, np.where(m == hi - 1, 1e30, -3e30))
    n = np.arange(256)[:, None]
    j = np.arange(64)[None, :]
    ov = np.clip(np.minimum(16 * n + 32, 64 * j + 64) - np.maximum(16 * n, 64 * j), 0, None) / 32.0
    ov[255] = 0.0
    c[:, 2304:2432] = ov.reshape(2, 128, 64).transpose(1, 0, 2).reshape(128, 128)
    return c


def _layout(inp):
    f = lambda k: np.asarray(inp[k], np.float32)[0]
    w_in = f("w_in")
    cols = []
    for h in range(4):
        cols += [np.arange(h * 128, h * 128 + 128), 512 + np.arange(h * 128, h * 128 + 128), 1024 + np.arange(h * 128, h * 128 + 128)]
    for g in range(2):
        cols += [1536 + g * 256 + np.arange(256)]
        for base in (2048, 2304, 2560, 2176, 2432, 2688):
            cols += [base + g * 64 + np.arange(64)]
        cols += [2816 + g * 12 + np.arange(12)]
    cols = np.concatenate(cols)
    r8 = lambda w: np.ascontiguousarray(w.reshape(8, 128, -1).transpose(1, 0, 2))
    m = {
        "consts": _consts(),
        "gpreT": np.ascontiguousarray(f("attn_pre_norm").reshape(8, 128).T),
        "g2T": np.ascontiguousarray(f("ffn_pre_norm").reshape(8, 128).T),
        "w_in_u": r8(w_in[:, cols]),
        "lam4": np.stack([f("lambda_q1"), f("lambda_k1"), f("lambda_q2"), f("lambda_k2")]),
        "subln": f("diff_subln")[None, :],
        "kw1": np.ascontiguousarray(f("k_cmp_w1").reshape(32, 64, 256).transpose(1, 0, 2)),
        "vw1": np.ascontiguousarray(f("v_cmp_w1").reshape(32, 64, 256).transpose(1, 0, 2)),
        "kposT": np.ascontiguousarray(f("k_cmp_pos").T),
        "vposT": np.ascontiguousarray(f("v_cmp_pos").T),
        "kw2": np.ascontiguousarray(f("k_cmp_w2").reshape(2, 128, 64).transpose(1, 0, 2)),
        "vw2": np.ascontiguousarray(f("v_cmp_w2").reshape(2, 128, 64).transpose(1, 0, 2)),
        "w_out_r": r8(f("w_out")),
        "gpostA": f("attn_post_norm")[None, :],
        "gpostF": f("ffn_post_norm")[None, :],
        "wg": r8(f("w_gate")),
        "wu": r8(f("w_up")),
        "wd": np.ascontiguousarray(f("w_down").reshape(FC, 128, D).transpose(1, 0, 2)),
    }
    return m


_CACHE = {}


def kernel(**inputs):
    if "nc" not in _CACHE:
        _CACHE["nc"] = build()[0]
    nc = _CACHE["nc"]
    shared = _layout(inputs)
    xs = np.asarray(inputs["x"], np.float32)
    in_maps = [dict(shared, x=np.ascontiguousarray(xs[b])) for b in range(8)]
    res = run_bass_kernel_spmd(nc, in_maps, core_ids=list(range(8)))
    return np.stack([np.asarray(r["out"], np.float32) for r in res.results], axis=0)
```

```python
import numpy as np
import concourse.bass as bass
import concourse.mybir as mybir
from concourse.bass_utils import run_bass_kernel_spmd

F32 = mybir.dt.float32
BF16 = mybir.dt.bfloat16
ALU = mybir.AluOpType
AF = mybir.ActivationFunctionType
AX = mybir.AxisListType

S, D, NT, KC = 4096, 1024, 32, 8
DFF, FC = 2816, 22
NEG = -30000.0
EPS = 1e-6
LAM_INIT = 0.2
NCONST = 2432
ENGS = ("pe", "act", "dve", "pool", "sp")


class _Op:
    __slots__ = ("eng", "fn", "deps", "sig", "ticket", "chan", "is_dma")

    def __init__(self, eng, fn, deps, chan=None):
        self.eng, self.fn, self.deps = eng, fn, deps
        self.sig, self.ticket, self.chan = False, 0, chan
        self.is_dma = chan is not None


class Prog:
    def __init__(self, nc):
        self.nc = nc
        self.ops = []
        self.last_w = {}
        self.readers = {}
        self.chan_last = {}
        self.eng_last = {}

    def _deps(self, reads, writes):
        d = set()
        for r in reads:
            w = self.last_w.get(r)
            if w is not None:
                d.add(w)
        for w_ in writes:
            w = self.last_w.get(w_)
            if w is not None:
                d.add(w)
            d.update(self.readers.get(w_, ()))
        return d

    def _commit(self, idx, reads, writes):
        for r in reads:
            self.readers.setdefault(r, []).append(idx)
        for w_ in writes:
            self.last_w[w_] = idx
            self.readers[w_] = []

    def op(self, eng, fn, reads=(), writes=()):
        d = self._deps(reads, writes)
        idx = len(self.ops)
        if eng == "pe":
            d = {x for x in d if self.ops[x].eng != "pe" or self.ops[x].is_dma}
        self.ops.append(_Op(eng, fn, d))
        self._commit(idx, reads, writes)
        self.eng_last[eng] = idx
        return idx

    def dma(self, eng, chan, out, in_, reads=(), writes=(), after_all=False, **kw):
        d = self._deps(reads, writes)
        if after_all:
            d.update(self.eng_last.values())
        prev = self.chan_last.get(chan)
        if prev is not None:
            d.add(prev)
        idx = len(self.ops)
        self.ops.append(_Op(eng, lambda e: e.dma_start(out=out, in_=in_, **kw), d, chan=chan))
        self.chan_last[chan] = idx
        self._commit(idx, reads, writes)
        return idx

    def barrier(self, keep_chans=(), keep_res=()):
        deps = set(self.eng_last.values()) | {v for c, v in self.chan_last.items() if c not in keep_chans}
        kept = {r: self.last_w[r] for r in keep_res if r in self.last_w}
        for eng in ENGS:
            idx = len(self.ops)
            self.ops.append(_Op(eng, None, set(deps)))
            self.eng_last[eng] = idx
        self.last_w.clear()
        self.readers.clear()
        self.last_w.update(kept)

    def emit(self):
        nc, ops = self.nc, self.ops
        for o in ops:
            for d in o.deps:
                ops[d].sig = True
        cnt = {e: 0 for e in ENGS}
        chan_cnt = {}
        for o in ops:
            if o.is_dma:
                o.sig = True
                chan_cnt[o.chan] = chan_cnt.get(o.chan, 0) + 16
                o.ticket = chan_cnt[o.chan]
            elif o.fn is None:
                o.sig = False
            elif o.sig:
                cnt[o.eng] += 1
                o.ticket = cnt[o.eng]
        sems = {e: nc.alloc_semaphore("s_" + e) for e in ENGS if e != "sp"}
        csems = {c: nc.alloc_semaphore("c_" + str(c)) for c in chan_cnt}
        per_eng = {e: [] for e in ENGS}
        for i, o in enumerate(ops):
            per_eng[o.eng].append(i)

        def run(engname):
            def f(e):
                waited = {}
                for i in per_eng[engname]:
                    o = ops[i]
                    need = {}
                    for d in o.deps:
                        od = ops[d]
                        if od.fn is None and not od.is_dma:
                            continue
                        key = ("c", od.chan) if od.is_dma else ("e", od.eng)
                        if need.get(key, 0) < od.ticket:
                            need[key] = od.ticket
                    for key, val in need.items():
                        if waited.get(key, 0) >= val:
                            continue
                        waited[key] = val
                        e.wait_ge(csems[key[1]] if key[0] == "c" else sems[key[1]], val)
                    if o.fn is None:
                        continue
                    ins = o.fn(e)
                    if o.is_dma:
                        ins.then_inc(csems[o.chan], 16)
                    elif o.sig:
                        ins.then_inc(sems[o.eng], 1)
                if engname == "sp":
                    for c, v in chan_cnt.items():
                        e.wait_ge(csems[c], v)
                    for en in ("pe", "act", "dve", "pool"):
                        if cnt[en]:
                            e.wait_ge(sems[en], cnt[en])
            return f

        with nc.Block() as block:
            block.tensor(run("pe"))
            block.scalar(run("act"))
            block.vector(run("dve"))
            block.gpsimd(run("pool"))
            block.sync(run("sp"))
        return {e: len(per_eng[e]) for e in ENGS}


class Arena:
    def __init__(self, nc, base, limit):
        self.nc, self.o, self.limit = nc, base, limit

    def a(self, name, shape, dt):
        nb = 2 if dt == BF16 else 4
        size = int(np.prod(shape[1:])) * nb
        off = (self.o + 31) // 32 * 32
        self.o = off + size
        assert self.o <= self.limit, (name, self.o, self.limit)
        return self.nc.alloc_sbuf_tensor_at(name, list(shape), dt, offset=off)


def build(stage="all", dbg=False):
    nc = bass.Bass("TRN2", target_bir_lowering=False)
    P = Prog(nc)

    def din(name, shape, dt=F32):
        return nc.dram_tensor(name, list(shape), dt, kind="ExternalInput").ap()

    x = din("x", [S, D])
    consts = din("consts", [128, NCONST])
    gpreT_d = din("gpreT", [128, 8])
    g2T_d = din("g2T", [128, 8])
    w_in_u = din("w_in_u", [128, 8, 2840])
    lam4 = din("lam4", [4, 64])
    subln = din("subln", [1, 128])
    cw1 = [din("kw1", [64, 32, 256]), din("vw1", [64, 32, 256])]
    cposT = [din("kposT", [64, 32]), din("vposT", [64, 32])]
    cw2 = [din("kw2", [128, 2, 64]), din("vw2", [128, 2, 64])]
    w_out_r = din("w_out_r", [128, 8, D])
    gpostA_d = din("gpostA", [1, D])
    gpostF_d = din("gpostF", [1, D])
    wg_d = din("wg", [128, 8, DFF])
    wu_d = din("wu", [128, 8, DFF])
    wd_d = din("wd", [128, FC, D])
    out = nc.dram_tensor("out", [S, D], F32, kind="ExternalOutput").ap()
    ao_s = nc.dram_tensor("ao_s", [S, D], BF16, kind="ExternalOutput" if dbg else "Internal").ap()
    ao_v = ao_s.rearrange("(t p) c -> p t c", p=128)

    BASE = 16512
    TOP = 229344
    CA0 = Arena(nc, BASE, BASE + 3 * 1024)
    PC_OFF = BASE + 3 * 1024
    CA = Arena(nc, PC_OFF, BASE + 26 * 1024)
    HT_OFF = BASE + 26 * 1024
    UA = Arena(nc, HT_OFF + 65536, TOP)
    hT = nc.alloc_sbuf_tensor_at("hT", [128, 8, S], BF16, offset=HT_OFF)

    pb = [nc.alloc_psum_tensor("pb%d" % i, [128, 512], F32) for i in range(8)]
    pbb = [p[:, :].bitcast(BF16) for p in pb]

    def MM(o, lhsT, rhs, start, stop, r, w, sg=False):
        if sg:
            P.op("pe", lambda e: e.matmul(o, lhsT=lhsT, rhs=rhs, start=start, stop=stop, skip_group_check=True), r, w)
        else:
            P.op("pe", lambda e: e.matmul(o, lhsT=lhsT, rhs=rhs, start=start, stop=stop), r, w)

    def ACT(o, i, func, r, w, **kw):
        P.op("act", lambda e: e.activation(out=o, in_=i, func=func, **kw), r, w)

    def TS(eng, o, i, s1, s2, op0, op1, r, w):
        P.op(eng, lambda e: e.tensor_scalar(out=o, in0=i, scalar1=s1, scalar2=s2, op0=op0, op1=op1), r, w)

    def TS1(eng, o, i, s1, op0, r, w):
        P.op(eng, lambda e: e.tensor_scalar(out=o, in0=i, scalar1=s1, scalar2=None, op0=op0), r, w)

    def TT(eng, o, a, b, op, r, w):
        P.op(eng, lambda e: e.tensor_tensor(out=o, in0=a, in1=b, op=op), r, w)

    def STT(o, a, sc, b, op0, op1, r, w):
        P.op("dve", lambda e: e.scalar_tensor_tensor(out=o, in0=a, scalar=sc, in1=b, op0=op0, op1=op1), r, w)

    def CP(eng, o, i, r, w):
        if eng == "act":
            P.op("act", lambda e: e.copy(out=o, in_=i), r, w)
        else:
            P.op(eng, lambda e: e.tensor_copy(out=o, in_=i), r, w)

    def MS(eng, o, val, w):
        P.op(eng, lambda e: e.memset(o, val), (), w)

    def RECIP(o, i, r, w):
        P.op("dve", lambda e: e.reciprocal(out=o, in_=i), r, w)

    def TR(o, i, r, w):
        P.op("pe", lambda e: e.transpose(out=o, in_=i, identity=ident[:]), list(r) + ["ident"], w)

    def ASEL(o, i, pattern, cmp, fill, base, cm, r, w):
        P.op("pool", lambda e: e.affine_select(out=o, in_=i, pattern=pattern, compare_op=cmp, fill=fill,
                                               base=base, channel_multiplier=cm), r, w)

    cst = CA.a("cst", [128, NCONST], F32)
    P.dma("sp", "cst", cst[:], consts, (), ["cst"])
    cosT = cst[:, 0:1024].rearrange("p (t f) -> p t f", f=32)
    sinT = cst[:, 1024:2048].rearrange("p (t f) -> p t f", f=32)
    CAP0, FLO0, OV0 = 2048, 2176, 2304
    gpreT = CA0.a("gpreT", [128, 8], F32)
    g2T = CA0.a("g2T", [128, 8], F32)
    P.dma("sp", "gv", gpreT[:], gpreT_d, (), ["gpreT"])
    P.dma("sp", "gv", g2T[:], g2T_d, (), ["g2T"])
    ident = CA0.a("ident", [128, 128], BF16)
    maskC = CA.a("maskC", [128, 512], BF16)
    maskW = CA.a("maskW", [128, 512], BF16)
    mhalf = CA0.a("mhalf", [128, 4], F32)
    zf = UA.a("zf", [128, 512], F32)
    MS("pool", zf[:], 0.0, ["zf"])
    MS("pool", mhalf[:], -0.5, ["mhalf"])
    ASEL(ident[:], zf[:, 0:128], [[-1, 128]], ALU.not_equal, 1.0, 0, 1, ["zf"], ["ident"])
    ASEL(maskC[:], zf[:], [[0, 4], [1, 128]], ALU.is_ge, NEG, 0, -1, ["zf"], ["maskC"])
    ASEL(maskW[:], zf[:], [[0, 4], [-1, 128]], ALU.is_gt, NEG, 0, 1, ["zf"], ["maskW"])
    maskC4 = maskC[:, :].rearrange("p (h q) -> p h q", h=4)
    maskW4 = maskW[:, :].rearrange("p (h q) -> p h q", h=4)
    lamt = UA.a("lamt", [128, 4, 64], F32)
    P.dma("sp", "gv", lamt[:], lam4.partition_broadcast(128), (), ["lamt"])
    lprod = UA.a("lprod", [128, 2, 64], F32)
    lsum = CA.a("lsum", [128, 2], F32)
    lexp = CA.a("lexp", [128, 2], F32)
    neglam = CA.a("neglam", [128, 1], F32)
    TT("dve", lprod[:], lamt[:, 0:4:2, :], lamt[:, 1:4:2, :], ALU.mult, ["lamt"], ["lprod"])
    P.op("dve", lambda e: e.reduce_sum(out=lsum[:], in_=lprod[:], axis=AX.X), ["lprod"], ["lsum"])
    ACT(lexp[:], lsum[:], AF.Exp, ["lsum"], ["lexp"])
    TT("dve", neglam[:], lexp[:, 1:2], lexp[:, 0:1], ALU.subtract, ["lexp"], ["neglam0"])
    TS1("dve", neglam[:], neglam[:], -LAM_INIT, ALU.add, ["neglam0"], ["neglam"])
    subtab = CA.a("subtab", [128, 128], F32)
    P.dma("sp", "gv", subtab[:], subln[0].partition_broadcast(128), (), ["subtab0"])
    TS1("dve", subtab[:], subtab[:], (1.0 - LAM_INIT) * float(np.sqrt(128.0)), ALU.mult, ["subtab0"], ["subtab"])
    posT = [CA.a("posT%d" % i, [64, 32], BF16) for i in range(2)]
    w2 = [CA.a("w2_%d" % i, [128, 2, 64], BF16) for i in range(2)]
    for i in range(2):
        P.dma("pool", "gvp", posT[i][:], cposT[i], (), [("posT", i)])
        P.dma("pool", "gvp", w2[i][:], cw2[i], (), [("w2", i)])
    VCA = CA.a("VCA", [128, 2, 130], BF16)
    MS("dve", VCA[:, :, 64:65], 1.0, ["VCA1"])
    CP("dve", VCA[:, :, 65:129], cst[:, OV0:OV0 + 128].rearrange("p (n j) -> p n j", n=2), ["cst"], ["VCAov"])
    maskcmp = CA.a("maskcmp", [128, 33, 128], BF16)
    onesb = UA.a("onesb", [128, 128], BF16)
    MS("pool", onesb[:], 1.0, ["onesb"])
    for i in range(17):
        ASEL(maskcmp[:, i, :], onesb[:], [[1, 128]], ALU.is_ge, 0.0, 128 * i - 31, -16, ["onesb"], [("mcmp", i)])
    for i in range(16, 32):
        ASEL(maskcmp[:, 17 + i - 16, :], onesb[:], [[1, 128]], ALU.is_ge, 0.0, 128 * i - 31 - 2048, -16,
             ["onesb"], [("mcmp", 17 + i - 16)])
    ssA = CA.a("ssA", [128, 32], F32)
    rsA = CA.a("rsA", [128, 32], F32)
    junk = CA0.a("junk", [128, D], BF16)
    jctr = [0, 0]

    def junk_slot(width):
        if width == 1024:
            return junk[:, :], ["j%d" % i for i in range(8)]
        if width == 512:
            hh = jctr[0] % 2
            jctr[0] += 1
            return junk[:, hh * 512:(hh + 1) * 512], ["j%d" % i for i in range(4 * hh, 4 * hh + 4)]
        sl = jctr[1] % 8
        jctr[1] += 1
        return junk[:, sl * 128:(sl + 1) * 128], ["j%d" % sl]

    P.barrier()
    UA.o = HT_OFF + 65536
    import os
    n_diff = int(os.environ.get("DBG_HEADS", 4)) if stage != "none" else 0
    n_chunks = int(os.environ.get("DBG_CH", 16))
    XA = Arena(nc, TOP - 21 * 1024, TOP)
    xs = [XA.a("xs%d" % i, [128, 2, D], F32) for i in range(2)]
    xn = [XA.a("xn%d" % i, [128, D], BF16) for i in range(2)]
    gpre_bc = gpreT[:, :].unsqueeze(2).to_broadcast([128, 8, 128])
    x_v = x.rearrange("(t p) d -> p t d", p=128)

    def A_tile(t):
        tp, jx = t // 2, t % 2
        xb = tp % 2
        b = t % 2
        if jx == 0:
            P.dma("sp", "xs%d" % xb, xs[xb][:], x_v[:, 2 * tp:2 * tp + 2, :], (), [("xs", xb)])
        xin = xs[xb][:, jx, :]
        jo, jr = junk_slot(1024)
        ACT(jo, xin, AF.Square, [("xs", xb)], [("ssA", t)] + jr, accum_out=ssA[:, t:t + 1])
        TS1("pool", rsA[:, t:t + 1], ssA[:, t:t + 1], 1024.0 * EPS, ALU.add, [("ssA", t)], [("rsA0", t)])
        TT("pool", rsA[:, t:t + 1], rsA[:, t:t + 1], mhalf[:, 0:1], ALU.pow, [("rsA0", t), "mhalf"], [("rsA", t)])
        TS("dve", xn[b][:], xin, rsA[:, t:t + 1], 32.0, ALU.mult, ALU.mult, [("xs", xb), ("rsA", t)], [("xn", b)])
        pT = pbb[6 + b].rearrange("p (k t) -> p k t", k=8)
        for kc in range(8):
            TR(pT[:, kc, :], xn[b][:, kc * 128:(kc + 1) * 128], [("xn", b)], [("pb", 6 + b)])
        TT("dve", hT[:, :, t * 128:(t + 1) * 128], pT, gpre_bc, ALU.mult, [("pb", 6 + b), "gpreT"], [("hT", t)])

    if n_diff == 0:
        for t in range(NT):
            A_tile(t)

    hT_all = [("hT", t) for t in range(NT)]

    import os

    class Step:
        __slots__ = ("pre", "s", "pv", "post")

        def __init__(self, s, pv, pre=None, post=None):
            self.s, self.pv, self.pre, self.post = s, pv, pre, post

    def run_steps(steps, L=2):
        n = len(steps)
        nxt = 0
        for k in range(n):
            while nxt < n and nxt <= k + L:
                st = steps[nxt]
                if st.pre is not None:
                    if nxt > k:
                        break
                    st.pre()
                st.s()
                nxt += 1
            steps[k].pv()
            if steps[k].post:
                steps[k].post()

    SBANKS = (3, 4, 1)
    sctr = [0]

    PA0 = Arena(nc, PC_OFF, HT_OFF)
    gF = PA0.a("gF", [128, D], F32)
    xs2 = [PA0.a("xs2_%d" % i, [128, 2, D], F32) for i in range(2)]
    st3 = PA0.a("st3", [128, 6, 64], F32)
    PA = Arena(nc, HT_OFF, TOP)
    Wg = PA.a("Wg", [128, 8, DFF], BF16)
    Wu = PA.a("Wu", [128, 8, DFF], BF16)
    assert PA.o <= HT_OFF + 65536 + 31424

    UA_mark = UA.o
    Wd_ = [UA.a("Wdh%d" % i, [128, 8, 384], BF16) for i in range(2)]
    QK = UA.a("QK", [128, 3, S], BF16)
    Vh = UA.a("Vh", [128, NT, 130], BF16)
    AOd = UA.a("AOd", [128, NT, 128], BF16)
    qk = [UA.a("qk%d" % i, [128, 256], F32) for i in range(2)]
    rp = [UA.a("rp%d" % i, [128, 256], BF16) for i in range(2)]
    tmpd = [UA.a("tmpd%d" % i, [128, 2, 128], F32) for i in range(2)]
    tmpp = [UA.a("tmpp%d" % i, [128, 2, 128], F32) for i in range(2)]
    pTs = [UA.a("pTs%d" % i, [128, 2, 256], BF16) for i in range(4)]
    rcd = UA.a("rcd", [128, 16, 4], F32)
    cf1 = UA.a("cf1", [128, 16, 2], F32)
    asb = [UA.a("asb%d" % i, [128, 128], F32) for i in range(2)]
    a2sb = [UA.a("a2sb%d" % i, [128, 128], F32) for i in range(2)]
    ssd = UA.a("ssd", [128, 4 * NT], F32)
    rsd = UA.a("rsd", [128, 4 * NT], F32)
    if n_diff:
        MS("dve", Vh[:, :, 128:129], 1.0, ["Vones"])
        MS("pool", QK[64:128, 0, :], 0.0, ["Qz0"])
        MS("pool", QK[0:64, 2, :], 0.0, ["Qz1"])
        P.dma("pool", "wdh0", Wd_[0][:], w_in_u[:, :, 0:384], (), [("Wdh", 0)])

    def diff_inproj(h):
        wb = h % 2
        if h + 1 < n_diff:
            P.dma("pool", "wdh%d" % ((h + 1) % 2), Wd_[(h + 1) % 2][:], w_in_u[:, :, (h + 1) * 384:(h + 2) * 384], (),
                  [("Wdh", (h + 1) % 2)])
        def dfront(t):
            tb = t % 2
            pin = pb[tb]
            for kc in range(8):
                MM(pin[:, 0:384], hT[:, kc, t * 128:(t + 1) * 128], Wd_[wb][:, kc, :], kc == 0, kc == 7,
                   [("hT", t), ("Wdh", wb)], [("pb", tb)])
            ACT(qk[tb][:], pin[:, 0:256], AF.Copy, [("pb", tb)], [("qk", tb)])
            ACT(Vh[:, t, 0:128], pin[:, 256:384], AF.Copy, [("pb", tb)], [("Vh", t)])
            v4 = qk[tb][:, :].rearrange("p (a h d) -> p a h d", a=4, h=2)
            r4 = rp[tb][:, :].rearrange("p (a h d) -> p a h d", a=4, h=2)
            cb = cosT[:, t, :].unsqueeze(1).to_broadcast([128, 4, 32])
            sb = sinT[:, t, :].unsqueeze(1).to_broadcast([128, 4, 32])
            td = tmpd[tb][:, :, :].rearrange("p a (h d) -> p a h d", h=4)
            tp = tmpp[tb][:, :, :].rearrange("p a (h d) -> p a h d", h=4)
            TT("dve", td[:, 0], v4[:, :, 0, :], cb, ALU.mult, [("qk", tb), "cst"], [("td0", tb)])
            TT("dve", td[:, 1], v4[:, :, 1, :], sb, ALU.mult, [("qk", tb), "cst"], [("td1", tb)])
            TT("dve", r4[:, :, 0, :], td[:, 0], td[:, 1], ALU.subtract, [("td0", tb), ("td1", tb)], [("rpA", tb)])
            TT("pool", tp[:, 0], v4[:, :, 1, :], cb, ALU.mult, [("qk", tb), "cst"], [("tp0", tb)])
            TT("pool", tp[:, 1], v4[:, :, 0, :], sb, ALU.mult, [("qk", tb), "cst"], [("tp1", tb)])
            TT("pool", r4[:, :, 1, :], tp[:, 0], tp[:, 1], ALU.add, [("tp0", tb), ("tp1", tb)], [("rpB", tb)])

        def dback(t):
            tb = t % 2
            pT = pbb[2].rearrange("p (k t) -> p k t", k=8)
            TR(pT[:, 0, :], rp[tb][:, 0:128], [("rpA", tb), ("rpB", tb)], [("pb", 2)])
            TR(pT[:, 1, :], rp[tb][:, 128:256], [("rpA", tb), ("rpB", tb)], [("pb", 2)])
            CP("act", QK[0:64, 0, t * 128:(t + 1) * 128], pT[0:64, 0, :], [("pb", 2)], [("QKa", t)])
            CP("act", QK[64:128, 2, t * 128:(t + 1) * 128], pT[64:128, 0, :], [("pb", 2)], [("QKb", t)])
            CP("act", QK[:, 1, t * 128:(t + 1) * 128], pT[:, 1, :], [("pb", 2)], [("QK", t)])

        fuseA = (h == 0)
        if fuseA:
            A_tile(0)
            A_tile(1)
        dfront(0)
        for t in range(NT):
            if fuseA and t + 2 < NT:
                A_tile(t + 2)
            if t + 1 < NT:
                dfront(t + 1)
            dback(t)

    def mk_diff_step(h, c, kt):
        j = kt - 2 * c
        k = sctr[0]
        sctr[0] += 1
        sbk, pbuf = SBANKS[k % 3], k % 4
        obank = (5, 6) if c % 2 == 0 else (7, 2)
        O = [pb[obank[m]][:, 0:258].rearrange("p (j e) -> p j e", e=129) for m in range(2)]
        Sv = pb[sbk][:, :].rearrange("p (m q) -> p m q", m=2)
        qres = [("QKa", 2 * c), ("QKa", 2 * c + 1), ("QKb", 2 * c), ("QKb", 2 * c + 1), "Qz0", "Qz1"]

        def s_():
            kT = QK[:, 1, kt * 128:(kt + 1) * 128]
            for m in range(2):
                qp = 2 * m
                if j < 0:
                    MM(Sv[:, m, :], kT, QK[:, qp, c * 256:(c + 1) * 256], True, True, [("QK", kt)] + qres, [("pb", sbk)])
                else:
                    MM(Sv[:, m, 128 * j:128 * j + 128], ident[:], maskC[:, 0:128], True, False, ["ident", "maskC"], [("pb", sbk)])
                    MM(Sv[:, m, 128 * j:128 * j + 128], kT, QK[:, qp, c * 256 + 128 * j:c * 256 + 128 * j + 128], False, True,
                       [("QK", kt)] + qres, [("pb", sbk)])
                    if j == 0:
                        MM(Sv[:, m, 128:256], kT, QK[:, qp, c * 256 + 128:c * 256 + 256], True, True,
                           [("QK", kt)] + qres, [("pb", sbk)])
            q0 = 128 * j if j > 0 else 0
            ACT(pTs[pbuf][:, :, q0:256], Sv[:, :, q0:256], AF.Exp, [("pb", sbk)], [("pTs", pbuf)], scale=0.125)

        def pv_():
            for m in range(2):
                for jj in range(max(j, 0), 2):
                    MM(O[m][:, jj, :], pTs[pbuf][:, m, 128 * jj:128 * jj + 128], Vh[:, kt, 0:129],
                       kt == 0 and jj == 0, kt == 2 * c + jj, [("pTs", pbuf), ("Vh", kt), "Vones"], [("pb", obank[m])], sg=True)

        def post_():
            for m in range(2):
                RECIP(rcd[:, c, 2 * m:2 * m + 2], O[m][:, :, 128], [("pb", obank[m])], [("rcd", c, m)])
            TS1("dve", cf1[:, c, :], rcd[:, c, 2:4], neglam[:, 0:1], ALU.mult, [("rcd", c, 1), "neglam"], [("cf1", c)])
            for jj in range(2):
                t = 2 * c + jj
                col = h * NT + t
                ab = t % 2
                TS1("dve", asb[ab][:], O[0][:, jj, 0:128], rcd[:, c, jj:jj + 1], ALU.mult,
                    [("pb", obank[0]), ("rcd", c, 0)], [("asb", ab)])
                STT(a2sb[ab][:], O[1][:, jj, 0:128], cf1[:, c, jj:jj + 1], asb[ab][:], ALU.mult, ALU.add,
                    [("pb", obank[1]), ("cf1", c), ("asb", ab)], [("a2sb", ab)])
                jo, jr = junk_slot(128)
                ACT(jo, a2sb[ab][:], AF.Square, [("a2sb", ab)], [("ssd", col)] + jr, accum_out=ssd[:, col:col + 1])
                TS1("pool", rsd[:, col:col + 1], ssd[:, col:col + 1], 128.0 * EPS, ALU.add, [("ssd", col)], [("rsd0", col)])
                TT("pool", rsd[:, col:col + 1], rsd[:, col:col + 1], mhalf[:, 0:1], ALU.pow, [("rsd0", col), "mhalf"], [("rsd", col)])
                STT(AOd[:, t, :], a2sb[ab][:], rsd[:, col:col + 1], subtab[:], ALU.mult, ALU.mult,
                    [("a2sb", ab), ("rsd", col), "subtab"], [("AOd", t)])
            if c == n_chunks - 1:
                P.dma("sp", "aod", ao_v[:, :, h * 128:(h + 1) * 128], AOd[:], [("AOd", t) for t in range(NT)], [("ao_s", h)])

        pre = (lambda: diff_inproj(h)) if (c == 0 and kt == 0) else None
        post = post_ if kt == 2 * c + 1 else None
        return Step(s_, pv_, pre, post)

    run_steps([mk_diff_step(h, c, kt) for h in range(n_diff) for c in range(n_chunks) for kt in range(2 * c + 2)])
    UA.o = UA_mark

    n_nsa = 2 if stage in ("all", "nsa") else 0
    n_q = int(os.environ.get("DBG_NQ", NT))
    P.barrier()
    UA_mark = UA.o
    Wn = UA.a("Wn", [128, 8, 652], BF16)
    W1 = UA.a("W1", [64, 16, 256], BF16)
    qn = [UA.a("qn%d" % i, [128, 448], F32) for i in range(2)]
    rn = [UA.a("rn%d" % i, [128, 512], BF16) for i in range(2)]
    tnd = [UA.a("tnd%d" % i, [128, 2, 224], F32) for i in range(2)]
    tnp = [UA.a("tnp%d" % i, [128, 2, 224], F32) for i in range(2)]
    assert UA.o - (HT_OFF + 65536) >= 31424
    nqT = UA.a("nqT", [128, 4, S], BF16)
    KV4 = UA.a("KV4", [128, 4, S], BF16)
    VS = UA.a("VS", [128, NT, 66], BF16)
    VW = UA.a("VW", [128, NT, 66], BF16)
    gat = UA.a("gat", [128, NT, 12], F32)
    AOn = [UA.a("AOn%d" % i, [128, 2, 256], BF16) for i in range(2)]
    kcmpT = UA.a("kcmpT", [64, 256], BF16)
    hsb = UA.a("hsb", [128, 2, 256], BF16)
    bsb = UA.a("bsb", [128, 2], F32)
    gex = UA.a("gex", [128, 12], F32)
    pcs = [UA.a("pcs%d" % i, [128, 4, 128], BF16) for i in range(2)]
    pss = [UA.a("pss%d" % i, [128, 4, 128], BF16) for i in range(4)]
    NB = CA.a("NB", [128, 128], BF16)
    imp = CA.a("imp", [128, 64], F32)
    imp2 = CA.a("imp2", [128, 64], F32)
    mrp = CA.a("mrp", [128, 64], F32)
    m8 = UA.a("m8", [128, 16], F32)
    rc4 = [UA.a("rc4_%d" % i, [128, 4], F32) for i in range(2)]
    cco = [UA.a("cco%d" % i, [128, 3, 4], F32) for i in range(2)]
    acc = [UA.a("acc%d" % i, [128, 4, 64], F32) for i in range(2)]
    tac = UA.a("tac", [128, 4, 64], F32)
    if n_nsa:
        MS("dve", VS[:, :, 64:65], 1.0, ["VSones"])
        MS("dve", VW[:, :, 64:65], 1.0, ["VWones"])
        MS("dve", NB[:, 0:64], 0.0, ["NB0"])
        MS("dve", hsb[:, :, 255:256], 0.0, ["hsbpad"])
        Et = nqT[0:64, 0, :]
        MS("pool", Et, 1.0, ["Et"])
        ASEL(Et, Et, [[1, S]], ALU.is_ge, 0.0, 0, -64, ["Et"], ["Et"])
        ASEL(Et, Et, [[-1, S]], ALU.is_ge, 0.0, 63, 64, ["Et"], ["Et"])
        CP("dve", KV4[64:128, 1, :], Et, ["Et"] + [("nqT", t) for t in range(NT)], ["E"])
    kv_all = [("KV4", t) for t in range(NT)]
    vca_r = ["VCAv", "VCA1", "VCAov"]

    def nsa_pre(g):
        P.dma("pool", "wn", Wn[:], w_in_u[:, :, 1536 + g * 652:1536 + (g + 1) * 652], (), ["Wn"])
        def nfront(t):
            tb = t % 2
            pa, pbk, pbi = pb[tb], pb[(2, 5)[tb]], (2, 5)[tb]
            for kc in range(8):
                MM(pa[:, :], hT[:, kc, t * 128:(t + 1) * 128], Wn[:, kc, 0:512], kc == 0, kc == 7,
                   [("hT", t), "Wn"], [("pb", tb)])
            for kc in range(8):
                MM(pbk[:, 0:140], hT[:, kc, t * 128:(t + 1) * 128], Wn[:, kc, 512:652], kc == 0, kc == 7,
                   [("hT", t), "Wn"], [("pb", pbi)])
            ACT(qn[tb][:], pa[:, 0:448], AF.Copy, [("pb", tb)], [("qn", tb)])
            ACT(rn[tb][:, 448:512], pa[:, 448:512], AF.Copy, [("pb", tb)], [("rnV", tb)])
            ACT(VS[:, t, 0:64], pbk[:, 0:64], AF.Copy, [("pb", pbi)], [("VS", t)])
            ACT(VW[:, t, 0:64], pbk[:, 64:128], AF.Copy, [("pb", pbi)], [("VW", t)])
            ACT(gex[:], pbk[:, 128:140], AF.Exp, [("pb", pbi)], ["gex"], scale=-1.0)
            TS1("dve", gex[:], gex[:], 1.0, ALU.add, ["gex"], ["gex1"])
            RECIP(gat[:, t, :], gex[:], ["gex1"], [("gat", t)])
            v4 = qn[tb][:, :].rearrange("p (a h d) -> p a h d", a=7, h=2)
            r4 = rn[tb][:, 0:448].rearrange("p (a h d) -> p a h d", a=7, h=2)
            cb = cosT[:, t, :].unsqueeze(1).to_broadcast([128, 7, 32])
            sb = sinT[:, t, :].unsqueeze(1).to_broadcast([128, 7, 32])
            td = tnd[tb][:, :, :].rearrange("p a (h d) -> p a h d", h=7)
            tp = tnp[tb][:, :, :].rearrange("p a (h d) -> p a h d", h=7)
            TT("dve", td[:, 0], v4[:, :, 0, :], cb, ALU.mult, [("qn", tb), "cst"], [("nd0", tb)])
            TT("dve", td[:, 1], v4[:, :, 1, :], sb, ALU.mult, [("qn", tb), "cst"], [("nd1", tb)])
            TT("dve", r4[:, :, 0, :], td[:, 0], td[:, 1], ALU.subtract, [("nd0", tb), ("nd1", tb)], [("rnA", tb)])
            TT("pool", tp[:, 0], v4[:, :, 1, :], cb, ALU.mult, [("qn", tb), "cst"], [("np0", tb)])
            TT("pool", tp[:, 1], v4[:, :, 0, :], sb, ALU.mult, [("qn", tb), "cst"], [("np1", tb)])
            TT("pool", r4[:, :, 1, :], tp[:, 0], tp[:, 1], ALU.add, [("np0", tb), ("np1", tb)], [("rnB", tb)])

        def nback(t):
            tb = t % 2
            pT = pbb[3 + tb].rearrange("p (k t) -> p k t", k=8)
            for k8 in range(8):
                TR(pT[0:64, k8, :], rn[tb][:, k8 * 64:(k8 + 1) * 64], [("rnA", tb), ("rnB", tb), ("rnV", tb)], [("pb", 3 + tb)])
            CP("dve", nqT[0:64, :, t * 128:(t + 1) * 128], pT[0:64, 0:4, :], [("pb", 3 + tb)], [("nqT", t)])
            CP("dve", KV4[0:64, :, t * 128:(t + 1) * 128], pT[0:64, 4:8, :], [("pb", 3 + tb)], [("KV4", t)])

        nfront(0)
        for t in range(NT):
            if t + 1 < NT:
                nfront(t + 1)
            nback(t)

        for kvi in range(2):
            plane = 0 if kvi == 0 else 3
            hid = pb[3][:, :].rearrange("p (j n) -> p j n", j=2)
            for half in range(2):
                P.dma("pool", "w1", W1[:], cw1[kvi][:, half * 16:(half + 1) * 16, :], (), ["W1"])
                for jh in range(2):
                    for l16 in range(16):
                        l = half * 16 + l16
                        MM(hid[:, jh, 0:255], W1[:, l16, jh * 128:(jh + 1) * 128], KV4[0:64, plane, l:l + 16 * 254 + 1:16],
                           l == 0 and jh == 0, l == 31, ["W1"] + kv_all, [("pb", 3)], sg=True)
                for jh in range(2):
                    for l16 in range(16):
                        l = half * 16 + l16
                        MM(pb[4][:, jh:jh + 1], W1[:, l16, jh * 128:(jh + 1) * 128], posT[kvi][:, l:l + 1],
                           l == 0 and jh == 0, l == 31, ["W1", ("posT", kvi)], [("pb", 4)], sg=True)
            CP("dve", bsb[:], pb[4][:, 0:2], [("pb", 4)], ["bsb"])
            for jh in range(2):
                ACT(hsb[:, jh, 0:255], hid[:, jh, 0:255], AF.Silu, [("pb", 3), "bsb", "hsbpad"], [("hsb", jh)],
                    bias=bsb[:, jh:jh + 1])
            if kvi == 0:
                for jh in range(2):
                    MM(pb[5][0:64, 0:256], w2[0][:, jh, :], hsb[:, jh, :], jh == 0, jh == 1,
                       [("hsb", 0), ("hsb", 1), ("w2", 0)], [("pb", 5)])
                CP("dve", kcmpT[:], pb[5][0:64, 0:256], [("pb", 5)], ["kcmpT"])
            else:
                pv = pb[6][:, 0:128].rearrange("p (n d) -> p n d", n=2)
                for nt in range(2):
                    for jh in range(2):
                        MM(pv[:, nt, :], hsb[:, jh, nt * 128:(nt + 1) * 128], w2[1][:, jh, :], jh == 0, jh == 1,
                           [("hsb", 0), ("hsb", 1), ("w2", 1)], [("pb", 6)])
                CP("dve", VCA[:, :, 0:64], pv, [("pb", 6)], ["VCAv"])
        if g == n_nsa - 1 and stage == "all":
            for k in range(4):
                sl = slice(k * 704, (k + 1) * 704)
                P.dma("pool", "wg", Wg[:, :, sl], wg_d[:, :, sl], (), [("Wg", k)], after_all=True)
                P.dma("pool", "wu", Wu[:, :, sl], wu_d[:, :, sl], (), [("Wu", k)], after_all=True)

    def mk_cmp_step(g, i, nt, first, last):
        pi = i % 2
        k = sctr[0]
        sctr[0] += 1
        sbk = SBANKS[k % 3]
        pc = pss[k % 4]
        pcr = ("pss", k % 4)
        Sc = pb[sbk][:, :].rearrange("p (h q) -> p h q", h=4)
        q64 = nqT[0:64, :, i * 128:(i + 1) * 128]
        U = [pb[5 + b][:, 0:258].rearrange("p (j e) -> p j e", e=129) for b in range(2)]

        def s_():
            MM(Sc, kcmpT[:, nt * 128:(nt + 1) * 128], q64, True, True, ["kcmpT", ("nqT", i)], [("pb", sbk)])
            ACT(pc[:], Sc, AF.Exp, [("pb", sbk)], [pcr], scale=0.125)
            midx = None
            if nt == 0 and i <= 16:
                midx = i
            if nt == 1:
                midx = 17 + i - 16
            if midx is not None:
                TT("pool", pc[:], pc[:], maskcmp[:, midx, :].unsqueeze(1).to_broadcast([128, 4, 128]),
                   ALU.mult, [pcr, ("mcmp", midx)], [pcr])

        def pv_():
            for hh in range(4):
                MM(U[hh // 2][:, hh % 2, :], pc[:, hh, :], VCA[:, nt, 0:129], first and hh % 2 == 0, last,
                   [pcr] + vca_r, [("pb", 5 + hh // 2)], sg=True)

        def post_():
            for b in range(2):
                TS1("dve", rc4[pi][:, 2 * b:2 * b + 2], U[b][:, :, 64], 1e-30, ALU.add, [("pb", 5 + b)], [("rc4a", pi, b)])
            RECIP(rc4[pi][:], rc4[pi][:], [("rc4a", pi, 0), ("rc4a", pi, 1)], [("rc4", pi)])
            TS1("dve", imp[:], U[0][:, 0, 65:129], rc4[pi][:, 0:1], ALU.mult, [("pb", 5), ("rc4", pi)], ["imp"])
            for hh in range(1, 4):
                STT(imp[:], U[hh // 2][:, hh % 2, 65:129], rc4[pi][:, hh:hh + 1], imp[:], ALU.mult, ALU.add,
                    [("pb", 5 + hh // 2), ("rc4", pi), "imp"], ["imp"])
            c0 = 62 - 2 * i
            TT("dve", imp2[:], imp[:], cst[:, CAP0 + c0:CAP0 + c0 + 64], ALU.min, ["imp", "cst"], ["imp2"])
            TT("dve", imp2[:], imp2[:], cst[:, FLO0 + c0:FLO0 + c0 + 64], ALU.max, ["imp2", "cst"], ["imp2"])
            MS("dve", imp2[:, 0:1], 3e30, ["imp2"])
            P.op("dve", lambda e: e.max(out=m8[:, 0:8], in_=imp2[:]), ["imp2"], ["m8a"])
            P.op("dve", lambda e: e.match_replace(out=mrp[:], in_to_replace=m8[:, 0:8], in_values=imp2[:], imm_value=-3e30),
                 ["imp2", "m8a"], ["mrp"])
            P.op("dve", lambda e: e.max(out=m8[:, 8:16], in_=mrp[:]), ["mrp"], ["m8b"])
            TS("dve", NB[:, 64:128], imp2[:], m8[:, 15:16], NEG, ALU.is_lt, ALU.mult, ["imp2", "m8b"], ["NB"])
            pTn = pbb[0][:, 0:128]
            TR(pTn, NB[:, :], ["NB", "NB0"], [("pb", 0)])
            CP("act", nqT[64:128, :, i * 128:(i + 1) * 128], pTn[64:128, :].unsqueeze(1).to_broadcast([64, 4, 128]),
               [("pb", 0)], [("nqTb", i)])
            TT("dve", cco[pi][:, 0, :], rc4[pi][:], gat[:, i, 0:12:3], ALU.mult, [("rc4", pi), ("gat", i)], [("cco", pi, 0)])
            for b in range(2):
                TT("dve", acc[pi][:, 2 * b:2 * b + 2, :], U[b][:, :, 0:64],
                   cco[pi][:, 0, 2 * b:2 * b + 2].unsqueeze(2).to_broadcast([128, 2, 64]), ALU.mult,
                   [("pb", 5 + b), ("cco", pi, 0)], [("acc", pi)])

        return Step(s_, pv_, None, post_ if last else None)

    def mk_br_step(g, i, br, kt, kts):
        pi = i % 2
        k = sctr[0]
        sctr[0] += 1
        sbk, pbuf = SBANKS[k % 3], k % 4
        Ss = pb[sbk][:, :].rearrange("p (h q) -> p h q", h=4)
        q64 = nqT[0:64, :, i * 128:(i + 1) * 128]
        q128 = nqT[:, :, i * 128:(i + 1) * 128]
        if br == 2:
            obk, Vt, vres = 2, VW, "VW"
        else:
            obk, Vt, vres = 7, VS, "VS"
        Ob = pb[obk][:, 0:260].rearrange("p (h e) -> p h e", e=65)

        def s_():
            first = True
            if kt == i:
                MM(Ss, ident[:], maskC4, True, False, ["ident", "maskC"], [("pb", sbk)])
                first = False
            elif br == 2 and kt == i - 4:
                MM(Ss, ident[:], maskW4, True, False, ["ident", "maskW"], [("pb", sbk)])
                first = False
            if br == 2:
                MM(Ss, KV4[0:64, 2, kt * 128:(kt + 1) * 128], q64, first, True, [("KV4", kt), ("nqT", i)], [("pb", sbk)])
            else:
                MM(Ss, KV4[:, 1, kt * 128:(kt + 1) * 128], q128, first, True,
                   [("KV4", kt), "E", ("nqTb", i), ("nqT", i)], [("pb", sbk)])
            ACT(pss[pbuf][:], Ss, AF.Exp, [("pb", sbk)], [("pss", pbuf)], scale=0.125)

        def pv_():
            for hh in range(4):
                MM(Ob[:, hh, :], pss[pbuf][:, hh, :], Vt[:, kt, 0:65], kt == kts[0] and hh == 0, kt == kts[-1],
                   [("pss", pbuf), (vres, kt), vres + "ones"], [("pb", obk)], sg=True)

        def post_():
            RECIP(cco[pi][:, br, :], Ob[:, :, 64], [("pb", obk)], [("ccoa", pi, br)])
            TT("dve", cco[pi][:, br, :], cco[pi][:, br, :], gat[:, i, br:12:3], ALU.mult, [("ccoa", pi, br), ("gat", i)], [("cco", pi, br)])
            TT("dve", tac[:], Ob[:, :, 0:64], cco[pi][:, br, :].unsqueeze(2).to_broadcast([128, 4, 64]), ALU.mult,
               [("pb", obk), ("cco", pi, br)], ["tac"])
            TT("pool", acc[pi][:], acc[pi][:], tac[:], ALU.add, ["tac", ("acc", pi)], [("acc", pi)])
            if br == 1:
                ab = (i // 2) % 2
                CP("pool", AOn[ab][:, i % 2, :], acc[pi][:, :, :].rearrange("p h d -> p (h d)"), [("acc", pi)], [("AOn", ab)])
                if i % 2 == 1:
                    P.dma("act", "aon%d" % ab, ao_v[:, i - 1:i + 1, 512 + g * 256:512 + (g + 1) * 256], AOn[ab][:],
                          [("AOn", ab)], [("ao_n", g, i)])

        return Step(s_, pv_, None, post_ if kt == kts[-1] else None)

    def cmp_steps(g, i):
        nts = [0] if i < 16 else [0, 1]
        return [mk_cmp_step(g, i, nt, nt == nts[0], nt == nts[-1]) for nt in nts]

    nsa_steps = []
    for g in range(n_nsa):
        gsteps = []
        if n_q:
            gsteps += cmp_steps(g, 0)
        for i in range(n_q):
            kts_w = list(range(max(0, i - 4), i + 1))
            gsteps += [mk_br_step(g, i, 2, kt, kts_w) for kt in kts_w]
            if i + 1 < n_q:
                gsteps += cmp_steps(g, i + 1)
            kts_s = list(range(0, i + 1))
            gsteps += [mk_br_step(g, i, 1, kt, kts_s) for kt in kts_s]
        if gsteps:
            gsteps[0].pre = (lambda g=g: nsa_pre(g))
        else:
            nsa_pre(g)
        nsa_steps += gsteps
    run_steps(nsa_steps)
    UA.o = UA_mark

    if stage in ("diff", "nsa", "none"):
        st = P.emit()
        return nc, st

    P.barrier()
    Wdn = PA.a("Wdn", [128, FC, D], BF16)
    tm = PA.a("tm", [128, 2, D], F32)
    h2n = PA.a("h2n", [128, D], BF16)
    h2T_off = (PA.o + 31) // 32 * 32
    h2T = PA.a("h2T", [128, 8, 512], BF16)
    ov_base = PA.o
    Wo = PA.a("Wo", [128, 8, D], BF16)
    gA = PA.a("gA", [128, D], F32)
    aos = [PA.a("aos%d" % i, [128, 2, D], BF16) for i in range(2)]
    aoT = [nc.alloc_sbuf_tensor_at("aoT%d" % i, [128, 8, 128], BF16, offset=h2T_off + 2048 * i) for i in range(2)]
    PA.o = ov_base
    sg = [PA.a("sg%d" % i, [128, 512], F32) for i in range(2)]
    actT = PA.a("actT", [128, FC, 512], BF16)
    P.dma("pool", "wo", Wo[:], w_out_r, (), ["Wo"])
    P.dma("sp", "gv", gA[:], gpostA_d[0].partition_broadcast(128), (), ["gA0"])
    P.dma("sp", "gv", gF[:], gpostF_d[0].partition_broadcast(128), (), ["gF0"])
    TS1("pool", gA[:], gA[:], 32.0, ALU.mult, ["gA0"], ["gA"])
    TS1("pool", gF[:], gF[:], 32.0, ALU.mult, ["gF0"], ["gF"])
    for k in range(2):
        sl = slice(k * 11, (k + 1) * 11)
        P.dma("pool", "wd", Wdn[:, sl, :], wd_d[:, sl, :], (), [("Wdn", k)])
    g2_bc = g2T[:, :].unsqueeze(2).to_broadcast([128, 8, 128])
    out_v = out.rearrange("(t p) d -> p t d", p=128)

    def norm_scale(col, ssrc_ap, res_in, tag):
        TS1("pool", st3[:, 1, col:col + 1], ssrc_ap, 1024.0 * EPS, ALU.add, res_in, [(tag + "0", col)])
        TT("pool", st3[:, 1, col:col + 1], st3[:, 1, col:col + 1], mhalf[:, 0:1], ALU.pow, [(tag + "0", col), "mhalf"], [(tag, col)])

    def c0_front(t):
        tp, jx = t // 2, t % 2
        xb = tp % 2
        b = t % 2
        mb = (2, 4)[b]
        if jx == 0:
            P.dma("sp", "aos%d" % xb, aos[xb][:], ao_v[:, 2 * tp:2 * tp + 2, :], (), [("aos", xb)])
            P.dma("act", "xs2%d" % xb, xs2[xb][:], x_v[:, 2 * tp:2 * tp + 2, :], (), [("xs2", xb)])
        pT = pbb[b].rearrange("p (k t) -> p k t", k=8)
        for kc in range(8):
            TR(pT[:, kc, :], aos[xb][:, jx, kc * 128:(kc + 1) * 128], [("aos", xb)], [("pb", b)])
        CP("act", aoT[b][:], pT, [("pb", b)], [("aoT", b)])
        for nh in range(2):
            for kc in range(8):
                MM(pb[mb + nh][:, :], aoT[b][:, kc, :], Wo[:, kc, nh * 512:(nh + 1) * 512], kc == 0, kc == 7,
                   [("aoT", b), "Wo"], [("pb", mb + nh)])

    def c0_back(t):
        tp, jx = t // 2, t % 2
        xb = tp % 2
        mb = (2, 4)[t % 2]
        for nh in range(2):
            jo, jr = junk_slot(512)
            ACT(jo, pb[mb + nh][:, :], AF.Square, [("pb", mb + nh)], [("ssm", t, nh)] + jr,
                accum_out=st3[:, 2 + nh, t:t + 1])
        TT("pool", st3[:, 0, t:t + 1], st3[:, 2, t:t + 1], st3[:, 3, t:t + 1], ALU.add, [("ssm", t, 0), ("ssm", t, 1)], [("ssmt", t)])
        norm_scale(t, st3[:, 0, t:t + 1], [("ssmt", t)], "rsm")
        for nh in range(2):
            STT(tm[:, jx, nh * 512:(nh + 1) * 512], pb[mb + nh][:, :], st3[:, 1, t:t + 1], gA[:, nh * 512:(nh + 1) * 512],
                ALU.mult, ALU.mult, [("pb", mb + nh), ("rsm", t), "gA"], [("tm", jx, nh)])
        if jx == 1:
            TT("pool", tm[:], tm[:], xs2[xb][:], ALU.add, [("tm", 0, 0), ("tm", 0, 1), ("tm", 1, 0), ("tm", 1, 1), ("xs2", xb)],
               ["x1", ("tm", 0, 0), ("tm", 0, 1), ("tm", 1, 0), ("tm", 1, 1)])
            P.dma("sp", "x1o", out_v[:, 2 * tp:2 * tp + 2, :], tm[:], ["x1", ("tm", 0, 0), ("tm", 0, 1), ("tm", 1, 0), ("tm", 1, 1)], [("out", tp)])

    c0_front(0)
    for t in range(NT):
        if t + 1 < NT:
            c0_front(t + 1)
        c0_back(t)

    P.barrier(keep_chans=("wd",), keep_res=[("Wdn", 0), ("Wdn", 1)])
    wg_r = [("Wg", k) for k in range(4)]
    wu_r = [("Wu", k) for k in range(4)]
    wd_r = [("Wdn", k) for k in range(2)]
    tm_all = [("tm", 0, 0), ("tm", 0, 1), ("tm", 1, 0), ("tm", 1, 1)]
    for blk in range(8):
        for tt in range(4):
            t = blk * 4 + tt
            b = t % 2
            tp, jx = t // 2, t % 2
            xb = tp % 2
            if jx == 0:
                P.dma("sp", "xs2%d" % xb, xs2[xb][:], out_v[:, 2 * tp:2 * tp + 2, :], [("out", tp)], [("xs2", xb)])
            xin = xs2[xb][:, jx, :]
            jo, jr = junk_slot(1024)
            ACT(jo, xin, AF.Square, [("xs2", xb)], [("ss2", t)] + jr, accum_out=st3[:, 4, t:t + 1])
            TS1("pool", st3[:, 5, t:t + 1], st3[:, 4, t:t + 1], 1024.0 * EPS, ALU.add, [("ss2", t)], [("rs20", t)])
            TT("pool", st3[:, 5, t:t + 1], st3[:, 5, t:t + 1], mhalf[:, 0:1], ALU.pow, [("rs20", t), "mhalf"], [("rs2", t)])
            TS("dve", h2n[:], xin, st3[:, 5, t:t + 1], 32.0, ALU.mult, ALU.mult, [("xs2", xb), ("rs2", t)], ["h2n"])
            pT = pbb[b].rearrange("p (k t) -> p k t", k=8)
            for kc in range(8):
                TR(pT[:, kc, :], h2n[:, kc * 128:(kc + 1) * 128], ["h2n"], [("pb", b)])
            TT("dve", h2T[:, :, tt * 128:(tt + 1) * 128], pT, g2_bc, ALU.mult, [("pb", b), "g2T"], [("h2T", tt)])
        h2r = [("h2T", tt) for tt in range(4)]
        for fc in range(FC):
            p2 = fc % 2
            gb, ub = (4, 5) if p2 == 0 else (6, 7)
            for kc in range(8):
                MM(pb[gb][:, :], Wg[:, kc, fc * 128:(fc + 1) * 128], h2T[:, kc, :], kc == 0, kc == 7, wg_r + h2r, [("pb", gb)])
            for kc in range(8):
                MM(pb[ub][:, :], Wu[:, kc, fc * 128:(fc + 1) * 128], h2T[:, kc, :], kc == 0, kc == 7, wu_r + h2r, [("pb", ub)])
            ACT(sg[p2][:], pb[gb][:, :], AF.Silu, [("pb", gb)], [("sg", p2)])
            TT("dve", actT[:, fc, :], sg[p2][:], pb[ub][:, :], ALU.mult, [("sg", p2), ("pb", ub)], [("actT", fc)])
        ar = [("actT", fc) for fc in range(FC)]
        for tt in range(4):
            t = blk * 4 + tt
            tp, jx = t // 2, t % 2
            for nh in range(2):
                for fc in range(FC):
                    MM(pb[2 + nh][:, :], actT[:, fc, tt * 128:(tt + 1) * 128], Wdn[:, fc, nh * 512:(nh + 1) * 512],
                       fc == 0, fc == FC - 1, ar + wd_r, [("pb", 2 + nh)])
            for nh in range(2):
                jo, jr = junk_slot(512)
                ACT(jo, pb[2 + nh][:, :], AF.Square, [("pb", 2 + nh)], [("ssy", t, nh)] + jr,
                    accum_out=st3[:, 2 + nh, 32 + t:33 + t])
            TT("pool", st3[:, 0, 32 + t:33 + t], st3[:, 2, 32 + t:33 + t], st3[:, 3, 32 + t:33 + t], ALU.add,
               [("ssy", t, 0), ("ssy", t, 1)], [("ssyt", t)])
            norm_scale(32 + t, st3[:, 0, 32 + t:33 + t], [("ssyt", t)], "rsy")
            for nh in range(2):
                STT(tm[:, jx, nh * 512:(nh + 1) * 512], pb[2 + nh][:, :], st3[:, 1, 32 + t:33 + t], gF[:, nh * 512:(nh + 1) * 512],
                    ALU.mult, ALU.mult, [("pb", 2 + nh), ("rsy", 32 + t), "gF"], [("tm", jx, nh)])
            if jx == 1:
                P.dma("pool", "acc", out_v[:, 2 * tp:2 * tp + 2, :], tm[:], tm_all + [("out", tp)],
                      [("out", tp)] + tm_all, accum_op=ALU.add)
    st = P.emit()
    return nc, st


def _consts():
    c = np.zeros((128, NCONST), np.float32)
    inv = 1.0 / (10000.0 ** (np.arange(0, 64, 2, dtype=np.float32) / 64.0))
    pos = (np.arange(NT)[None, :, None] * 128 + np.arange(128)[:, None, None]).astype(np.float32)
    ang = pos * inv[None, None, :]
    c[:, 0:1024] = np.cos(ang).astype(np.float32).reshape(128, 1024)
    c[:, 1024:2048] = np.sin(ang).astype(np.float32).reshape(128, 1024)
    hi = (np.arange(128) >= 64).astype(np.int64)[:, None]
    m = np.arange(128)[None, :] - 62
    c[:, 2048:2176] = np.where(m <= hi, 1e30, -1e30)
    c[:, 2176:2304] = np.where(m == hi, 2e30, np.where(m == hi - 1, 1e30, -3e30))
    n = np.arange(256)[:, None]
    j = np.arange(64)[None, :]
    ov = np.clip(np.minimum(16 * n + 32, 64 * j + 64) - np.maximum(16 * n, 64 * j), 0, None) / 32.0
    ov[255] = 0.0
    c[:, 2304:2432] = ov.reshape(2, 128, 64).transpose(1, 0, 2).reshape(128, 128)
    return c


def _layout(inp):
    f = lambda k: np.asarray(inp[k], np.float32)[0]
    w_in = f("w_in")
    cols = []
    for h in range(4):
        cols += [np.arange(h * 128, h * 128 + 128), 512 + np.arange(h * 128, h * 128 + 128), 1024 + np.arange(h * 128, h * 128 + 128)]
    for g in range(2):
        cols += [1536 + g * 256 + np.arange(256)]
        for base in (2048, 2304, 2560, 2176, 2432, 2688):
            cols += [base + g * 64 + np.arange(64)]
        cols += [2816 + g * 12 + np.arange(12)]
    cols = np.concatenate(cols)
    r8 = lambda w: np.ascontiguousarray(w.reshape(8, 128, -1).transpose(1, 0, 2))
    m = {
        "consts": _consts(),
        "gpreT": np.ascontiguousarray(f("attn_pre_norm").reshape(8, 128).T),
        "g2T": np.ascontiguousarray(f("ffn_pre_norm").reshape(8, 128).T),
        "w_in_u": r8(w_in[:, cols]),
        "lam4": np.stack([f("lambda_q1"), f("lambda_k1"), f("lambda_q2"), f("lambda_k2")]),
        "subln": f("diff_subln")[None, :],
        "kw1": np.ascontiguousarray(f("k_cmp_w1").reshape(32, 64, 256).transpose(1, 0, 2)),
        "vw1": np.ascontiguousarray(f("v_cmp_w1").reshape(32, 64, 256).transpose(1, 0, 2)),
        "kposT": np.ascontiguousarray(f("k_cmp_pos").T),
        "vposT": np.ascontiguousarray(f("v_cmp_pos").T),
        "kw2": np.ascontiguousarray(f("k_cmp_w2").reshape(2, 128, 64).transpose(1, 0, 2)),
        "vw2": np.ascontiguousarray(f("v_cmp_w2").reshape(2, 128, 64).transpose(1, 0, 2)),
        "w_out_r": r8(f("w_out")),
        "gpostA": f("attn_post_norm")[None, :],
        "gpostF": f("ffn_post_norm")[None, :],
        "wg": r8(f("w_gate")),
        "wu": r8(f("w_up")),
        "wd": np.ascontiguousarray(f("w_down").reshape(FC, 128, D).transpose(1, 0, 2)),
    }
    return m


_CACHE = {}


def kernel(**inputs):
    if "nc" not in _CACHE:
        _CACHE["nc"] = build()[0]
    nc = _CACHE["nc"]
    shared = _layout(inputs)
    xs = np.asarray(inputs["x"], np.float32)
    in_maps = [dict(shared, x=np.ascontiguousarray(xs[b])) for b in range(8)]
    res = run_bass_kernel_spmd(nc, in_maps, core_ids=list(range(8)))
    return np.stack([np.asarray(r["out"], np.float32) for r in res.results], axis=0)
```

```python
import numpy as np
import concourse.bass as bass
import concourse.mybir as mybir
from concourse.bass_utils import run_bass_kernel_spmd

F32 = mybir.dt.float32
BF16 = mybir.dt.bfloat16
ALU = mybir.AluOpType
AF = mybir.ActivationFunctionType
AX = mybir.AxisListType

S, D, NT, KC = 4096, 1024, 32, 8
DFF, FC = 2816, 22
NEG = -30000.0
EPS = 1e-6
LAM_INIT = 0.2
NCONST = 2432
ENGS = ("pe", "act", "dve", "pool", "sp")


class _Op:
    __slots__ = ("eng", "fn", "deps", "sig", "ticket", "chan", "is_dma")

    def __init__(self, eng, fn, deps, chan=None):
        self.eng, self.fn, self.deps = eng, fn, deps
        self.sig, self.ticket, self.chan = False, 0, chan
        self.is_dma = chan is not None


class Prog:
    def __init__(self, nc):
        self.nc = nc
        self.ops = []
        self.last_w = {}
        self.readers = {}
        self.chan_last = {}
        self.eng_last = {}

    def _deps(self, reads, writes):
        d = set()
        for r in reads:
            w = self.last_w.get(r)
            if w is not None:
                d.add(w)
        for w_ in writes:
            w = self.last_w.get(w_)
            if w is not None:
                d.add(w)
            d.update(self.readers.get(w_, ()))
        return d

    def _commit(self, idx, reads, writes):
        for r in reads:
            self.readers.setdefault(r, []).append(idx)
        for w_ in writes:
            self.last_w[w_] = idx
            self.readers[w_] = []

    def op(self, eng, fn, reads=(), writes=()):
        d = self._deps(reads, writes)
        idx = len(self.ops)
        if eng == "pe":
            d = {x for x in d if self.ops[x].eng != "pe" or self.ops[x].is_dma}
        self.ops.append(_Op(eng, fn, d))
        self._commit(idx, reads, writes)
        self.eng_last[eng] = idx
        return idx

    def dma(self, eng, chan, out, in_, reads=(), writes=(), after_all=False, **kw):
        d = self._deps(reads, writes)
        if after_all:
            d.update(self.eng_last.values())
        prev = self.chan_last.get(chan)
        if prev is not None:
            d.add(prev)
        idx = len(self.ops)
        self.ops.append(_Op(eng, lambda e: e.dma_start(out=out, in_=in_, **kw), d, chan=chan))
        self.chan_last[chan] = idx
        self._commit(idx, reads, writes)
        return idx

    def barrier(self, keep_chans=(), keep_res=()):
        deps = set(self.eng_last.values()) | {v for c, v in self.chan_last.items() if c not in keep_chans}
        kept = {r: self.last_w[r] for r in keep_res if r in self.last_w}
        for eng in ENGS:
            idx = len(self.ops)
            self.ops.append(_Op(eng, None, set(deps)))
            self.eng_last[eng] = idx
        self.last_w.clear()
        self.readers.clear()
        self.last_w.update(kept)

    def emit(self):
        nc, ops = self.nc, self.ops
        for o in ops:
            for d in o.deps:
                ops[d].sig = True
        cnt = {e: 0 for e in ENGS}
        chan_cnt = {}
        for o in ops:
            if o.is_dma:
                o.sig = True
                chan_cnt[o.chan] = chan_cnt.get(o.chan, 0) + 16
                o.ticket = chan_cnt[o.chan]
            elif o.fn is None:
                o.sig = False
            elif o.sig:
                cnt[o.eng] += 1
                o.ticket = cnt[o.eng]
        sems = {e: nc.alloc_semaphore("s_" + e) for e in ENGS if e != "sp"}
        csems = {c: nc.alloc_semaphore("c_" + str(c)) for c in chan_cnt}
        per_eng = {e: [] for e in ENGS}
        for i, o in enumerate(ops):
            per_eng[o.eng].append(i)

        def run(engname):
            def f(e):
                waited = {}
                for i in per_eng[engname]:
                    o = ops[i]
                    need = {}
                    for d in o.deps:
                        od = ops[d]
                        if od.fn is None and not od.is_dma:
                            continue
                        key = ("c", od.chan) if od.is_dma else ("e", od.eng)
                        if need.get(key, 0) < od.ticket:
                            need[key] = od.ticket
                    for key, val in need.items():
                        if waited.get(key, 0) >= val:
                            continue
                        waited[key] = val
                        e.wait_ge(csems[key[1]] if key[0] == "c" else sems[key[1]], val)
                    if o.fn is None:
                        continue
                    ins = o.fn(e)
                    if o.is_dma:
                        ins.then_inc(csems[o.chan], 16)
                    elif o.sig:
                        ins.then_inc(sems[o.eng], 1)
                if engname == "sp":
                    for c, v in chan_cnt.items():
                        e.wait_ge(csems[c], v)
                    for en in ("pe", "act", "dve", "pool"):
                        if cnt[en]:
                            e.wait_ge(sems[en], cnt[en])
            return f

        with nc.Block() as block:
            block.tensor(run("pe"))
            block.scalar(run("act"))
            block.vector(run("dve"))
            block.gpsimd(run("pool"))
            block.sync(run("sp"))
        return {e: len(per_eng[e]) for e in ENGS}


class Arena:
    def __init__(self, nc, base, limit):
        self.nc, self.o, self.limit = nc, base, limit

    def a(self, name, shape, dt):
        nb = 2 if dt == BF16 else 4
        size = int(np.prod(shape[1:])) * nb
        off = (self.o + 31) // 32 * 32
        self.o = off + size
        assert self.o <= self.limit, (name, self.o, self.limit)
        return self.nc.alloc_sbuf_tensor_at(name, list(shape), dt, offset=off)


def build(stage="all", dbg=False):
    nc = bass.Bass("TRN2", target_bir_lowering=False)
    P = Prog(nc)

    def din(name, shape, dt=F32):
        return nc.dram_tensor(name, list(shape), dt, kind="ExternalInput").ap()

    x = din("x", [S, D])
    consts = din("consts", [128, NCONST])
    gpreT_d = din("gpreT", [128, 8])
    g2T_d = din("g2T", [128, 8])
    w_in_u = din("w_in_u", [128, 8, 2840])
    lam4 = din("lam4", [4, 64])
    subln = din("subln", [1, 128])
    cw1 = [din("kw1", [64, 32, 256]), din("vw1", [64, 32, 256])]
    cposT = [din("kposT", [64, 32]), din("vposT", [64, 32])]
    cw2 = [din("kw2", [128, 2, 64]), din("vw2", [128, 2, 64])]
    w_out_r = din("w_out_r", [128, 8, D])
    gpostA_d = din("gpostA", [1, D])
    gpostF_d = din("gpostF", [1, D])
    wg_d = din("wg", [128, 8, DFF])
    wu_d = din("wu", [128, 8, DFF])
    wd_d = din("wd", [128, FC, D])
    out = nc.dram_tensor("out", [S, D], F32, kind="ExternalOutput").ap()
    ao_s = nc.dram_tensor("ao_s", [S, D], BF16, kind="ExternalOutput" if dbg else "Internal").ap()
    ao_v = ao_s.rearrange("(t p) c -> p t c", p=128)

    BASE = 16512
    TOP = 229344
    CA0 = Arena(nc, BASE, BASE + 3 * 1024)
    PC_OFF = BASE + 3 * 1024
    CA = Arena(nc, PC_OFF, BASE + 26 * 1024)
    HT_OFF = BASE + 26 * 1024
    UA = Arena(nc, HT_OFF + 65536, TOP)
    hT = nc.alloc_sbuf_tensor_at("hT", [128, 8, S], BF16, offset=HT_OFF)

    pb = [nc.alloc_psum_tensor("pb%d" % i, [128, 512], F32) for i in range(8)]
    pbb = [p[:, :].bitcast(BF16) for p in pb]

    def MM(o, lhsT, rhs, start, stop, r, w, sg=False):
        if sg:
            P.op("pe", lambda e: e.matmul(o, lhsT=lhsT, rhs=rhs, start=start, stop=stop, skip_group_check=True), r, w)
        else:
            P.op("pe", lambda e: e.matmul(o, lhsT=lhsT, rhs=rhs, start=start, stop=stop), r, w)

    def ACT(o, i, func, r, w, **kw):
        P.op("act", lambda e: e.activation(out=o, in_=i, func=func, **kw), r, w)

    def TS(eng, o, i, s1, s2, op0, op1, r, w):
        P.op(eng, lambda e: e.tensor_scalar(out=o, in0=i, scalar1=s1, scalar2=s2, op0=op0, op1=op1), r, w)

    def TS1(eng, o, i, s1, op0, r, w):
        P.op(eng, lambda e: e.tensor_scalar(out=o, in0=i, scalar1=s1, scalar2=None, op0=op0), r, w)

    def TT(eng, o, a, b, op, r, w):
        P.op(eng, lambda e: e.tensor_tensor(out=o, in0=a, in1=b, op=op), r, w)

    def STT(o, a, sc, b, op0, op1, r, w):
        P.op("dve", lambda e: e.scalar_tensor_tensor(out=o, in0=a, scalar=sc, in1=b, op0=op0, op1=op1), r, w)

    def CP(eng, o, i, r, w):
        if eng == "act":
            P.op("act", lambda e: e.copy(out=o, in_=i), r, w)
        else:
            P.op(eng, lambda e: e.tensor_copy(out=o, in_=i), r, w)

    def MS(eng, o, val, w):
        P.op(eng, lambda e: e.memset(o, val), (), w)

    def RECIP(o, i, r, w):
        P.op("dve", lambda e: e.reciprocal(out=o, in_=i), r, w)

    def TR(o, i, r, w):
        P.op("pe", lambda e: e.transpose(out=o, in_=i, identity=ident[:]), list(r) + ["ident"], w)

    def ASEL(o, i, pattern, cmp, fill, base, cm, r, w):
        P.op("pool", lambda e: e.affine_select(out=o, in_=i, pattern=pattern, compare_op=cmp, fill=fill,
                                               base=base, channel_multiplier=cm), r, w)

    cst = CA.a("cst", [128, NCONST], F32)
    P.dma("sp", "cst", cst[:], consts, (), ["cst"])
    cosT = cst[:, 0:1024].rearrange("p (t f) -> p t f", f=32)
    sinT = cst[:, 1024:2048].rearrange("p (t f) -> p t f", f=32)
    CAP0, FLO0, OV0 = 2048, 2176, 2304
    gpreT = CA0.a("gpreT", [128, 8], F32)
    g2T = CA0.a("g2T", [128, 8], F32)
    P.dma("sp", "gv", gpreT[:], gpreT_d, (), ["gpreT"])
    P.dma("sp", "gv", g2T[:], g2T_d, (), ["g2T"])
    ident = CA0.a("ident", [128, 128], BF16)
    maskC = CA.a("maskC", [128, 512], BF16)
    maskW = CA.a("maskW", [128, 512], BF16)
    mhalf = CA0.a("mhalf", [128, 4], F32)
    zf = UA.a("zf", [128, 512], F32)
    MS("pool", zf[:], 0.0, ["zf"])
    MS("pool", mhalf[:], -0.5, ["mhalf"])
    ASEL(ident[:], zf[:, 0:128], [[-1, 128]], ALU.not_equal, 1.0, 0, 1, ["zf"], ["ident"])
    ASEL(maskC[:], zf[:], [[0, 4], [1, 128]], ALU.is_ge, NEG, 0, -1, ["zf"], ["maskC"])
    ASEL(maskW[:], zf[:], [[0, 4], [-1, 128]], ALU.is_gt, NEG, 0, 1, ["zf"], ["maskW"])
    maskC4 = maskC[:, :].rearrange("p (h q) -> p h q", h=4)
    maskW4 = maskW[:, :].rearrange("p (h q) -> p h q", h=4)
    lamt = UA.a("lamt", [128, 4, 64], F32)
    P.dma("sp", "gv", lamt[:], lam4.partition_broadcast(128), (), ["lamt"])
    lprod = UA.a("lprod", [128, 2, 64], F32)
    lsum = CA.a("lsum", [128, 2], F32)
    lexp = CA.a("lexp", [128, 2], F32)
    neglam = CA.a("neglam", [128, 1], F32)
    TT("dve", lprod[:], lamt[:, 0:4:2, :], lamt[:, 1:4:2, :], ALU.mult, ["lamt"], ["lprod"])
    P.op("dve", lambda e: e.reduce_sum(out=lsum[:], in_=lprod[:], axis=AX.X), ["lprod"], ["lsum"])
    ACT(lexp[:], lsum[:], AF.Exp, ["lsum"], ["lexp"])
    TT("dve", neglam[:], lexp[:, 1:2], lexp[:, 0:1], ALU.subtract, ["lexp"], ["neglam0"])
    TS1("dve", neglam[:], neglam[:], -LAM_INIT, ALU.add, ["neglam0"], ["neglam"])
    subtab = CA.a("subtab", [128, 128], F32)
    P.dma("sp", "gv", subtab[:], subln[0].partition_broadcast(128), (), ["subtab0"])
    TS1("dve", subtab[:], subtab[:], (1.0 - LAM_INIT) * float(np.sqrt(128.0)), ALU.mult, ["subtab0"], ["subtab"])
    posT = [CA.a("posT%d" % i, [64, 32], BF16) for i in range(2)]
    w2 = [CA.a("w2_%d" % i, [128, 2, 64], BF16) for i in range(2)]
    for i in range(2):
        P.dma("pool", "gvp", posT[i][:], cposT[i], (), [("posT", i)])
        P.dma("pool", "gvp", w2[i][:], cw2[i], (), [("w2", i)])
    VCA = CA.a("VCA", [128, 2, 130], BF16)
    MS("dve", VCA[:, :, 64:65], 1.0, ["VCA1"])
    CP("dve", VCA[:, :, 65:129], cst[:, OV0:OV0 + 128].rearrange("p (n j) -> p n j", n=2), ["cst"], ["VCAov"])
    maskcmp = CA.a("maskcmp", [128, 33, 128], BF16)
    onesb = UA.a("onesb", [128, 128], BF16)
    MS("pool", onesb[:], 1.0, ["onesb"])
    for i in range(17):
        ASEL(maskcmp[:, i, :], onesb[:], [[1, 128]], ALU.is_ge, 0.0, 128 * i - 31, -16, ["onesb"], [("mcmp", i)])
    for i in range(16, 32):
        ASEL(maskcmp[:, 17 + i - 16, :], onesb[:], [[1, 128]], ALU.is_ge, 0.0, 128 * i - 31 - 2048, -16,
             ["onesb"], [("mcmp", 17 + i - 16)])
    ssA = CA.a("ssA", [128, 32], F32)
    rsA = CA.a("rsA", [128, 32], F32)
    junk = CA0.a("junk", [128, D], BF16)
    jctr = [0, 0]

    def junk_slot(width):
        if width == 1024:
            return junk[:, :], ["j%d" % i for i in range(8)]
        if width == 512:
            hh = jctr[0] % 2
            jctr[0] += 1
            return junk[:, hh * 512:(hh + 1) * 512], ["j%d" % i for i in range(4 * hh, 4 * hh + 4)]
        sl = jctr[1] % 8
        jctr[1] += 1
        return junk[:, sl * 128:(sl + 1) * 128], ["j%d" % sl]

    P.barrier()
    UA.o = HT_OFF + 65536
    UA_mark = UA.o
    xs = [UA.a("xs%d" % i, [128, 2, D], F32) for i in range(2)]
    xn = [UA.a("xn%d" % i, [128, D], BF16) for i in range(2)]
    gpre_bc = gpreT[:, :].unsqueeze(2).to_broadcast([128, 8, 128])
    x_v = x.rearrange("(t p) d -> p t d", p=128)
    for tp in range(NT // 2):
        xb = tp % 2
        P.dma("sp", "xs%d" % xb, xs[xb][:], x_v[:, 2 * tp:2 * tp + 2, :], (), [("xs", xb)])
        for jx in range(2):
            t = 2 * tp + jx
            b = t % 2
            xin = xs[xb][:, jx, :]
            jo, jr = junk_slot(1024)
            ACT(jo, xin, AF.Square, [("xs", xb)], [("ssA", t)] + jr, accum_out=ssA[:, t:t + 1])
            TS1("pool", rsA[:, t:t + 1], ssA[:, t:t + 1], 1024.0 * EPS, ALU.add, [("ssA", t)], [("rsA0", t)])
            TT("pool", rsA[:, t:t + 1], rsA[:, t:t + 1], mhalf[:, 0:1], ALU.pow, [("rsA0", t), "mhalf"], [("rsA", t)])
            TS("dve", xn[b][:], xin, rsA[:, t:t + 1], 32.0, ALU.mult, ALU.mult, [("xs", xb), ("rsA", t)], [("xn", b)])
            pT = pbb[b].rearrange("p (k t) -> p k t", k=8)
            for kc in range(8):
                TR(pT[:, kc, :], xn[b][:, kc * 128:(kc + 1) * 128], [("xn", b)], [("pb", b)])
            TT("dve", hT[:, :, t * 128:(t + 1) * 128], pT, gpre_bc, ALU.mult, [("pb", b), "gpreT"], [("hT", t)])
    UA.o = UA_mark
    P.barrier()

    hT_all = [("hT", t) for t in range(NT)]

    import os

    class Step:
        __slots__ = ("pre", "s", "pv", "post")

        def __init__(self, s, pv, pre=None, post=None):
            self.s, self.pv, self.pre, self.post = s, pv, pre, post

    def run_steps(steps, L=2):
        n = len(steps)
        nxt = 0
        for k in range(n):
            while nxt < n and nxt <= k + L:
                st = steps[nxt]
                if st.pre is not None:
                    if nxt > k:
                        break
                    st.pre()
                st.s()
                nxt += 1
            steps[k].pv()
            if steps[k].post:
                steps[k].post()

    SBANKS = (3, 4, 1)
    sctr = [0]

    PA0 = Arena(nc, PC_OFF, HT_OFF)
    gF = PA0.a("gF", [128, D], F32)
    xs2 = [PA0.a("xs2_%d" % i, [128, 2, D], F32) for i in range(2)]
    st3 = PA0.a("st3", [128, 6, 64], F32)
    PA = Arena(nc, HT_OFF, TOP)
    Wg = PA.a("Wg", [128, 8, DFF], BF16)
    Wu = PA.a("Wu", [128, 8, DFF], BF16)
    assert PA.o <= HT_OFF + 65536 + 31424

    n_diff = int(os.environ.get("DBG_HEADS", 4)) if stage != "none" else 0
    n_chunks = int(os.environ.get("DBG_CH", 16))
    UA_mark = UA.o
    Wd_ = [UA.a("Wdh%d" % i, [128, 8, 384], BF16) for i in range(2)]
    QK = UA.a("QK", [128, 3, S], BF16)
    Vh = UA.a("Vh", [128, NT, 130], BF16)
    AOd = UA.a("AOd", [128, NT, 128], BF16)
    qk = [UA.a("qk%d" % i, [128, 256], F32) for i in range(2)]
    rp = [UA.a("rp%d" % i, [128, 256], BF16) for i in range(2)]
    tmpd = [UA.a("tmpd%d" % i, [128, 2, 128], F32) for i in range(2)]
    tmpp = [UA.a("tmpp%d" % i, [128, 2, 128], F32) for i in range(2)]
    pTs = [UA.a("pTs%d" % i, [128, 2, 256], BF16) for i in range(4)]
    rcd = UA.a("rcd", [128, 16, 4], F32)
    cf1 = UA.a("cf1", [128, 16, 2], F32)
    asb = [UA.a("asb%d" % i, [128, 128], F32) for i in range(2)]
    a2sb = [UA.a("a2sb%d" % i, [128, 128], F32) for i in range(2)]
    ssd = UA.a("ssd", [128, 4 * NT], F32)
    rsd = UA.a("rsd", [128, 4 * NT], F32)
    if n_diff:
        MS("dve", Vh[:, :, 128:129], 1.0, ["Vones"])
        MS("pool", QK[64:128, 0, :], 0.0, ["Qz0"])
        MS("pool", QK[0:64, 2, :], 0.0, ["Qz1"])
        P.dma("pool", "wdh0", Wd_[0][:], w_in_u[:, :, 0:384], (), [("Wdh", 0)])

    def diff_inproj(h):
        wb = h % 2
        if h + 1 < n_diff:
            P.dma("pool", "wdh%d" % ((h + 1) % 2), Wd_[(h + 1) % 2][:], w_in_u[:, :, (h + 1) * 384:(h + 2) * 384], (),
                  [("Wdh", (h + 1) % 2)])
        def dfront(t):
            tb = t % 2
            pin = pb[tb]
            for kc in range(8):
                MM(pin[:, 0:384], hT[:, kc, t * 128:(t + 1) * 128], Wd_[wb][:, kc, :], kc == 0, kc == 7,
                   [("hT", t), ("Wdh", wb)], [("pb", tb)])
            ACT(qk[tb][:], pin[:, 0:256], AF.Copy, [("pb", tb)], [("qk", tb)])
            ACT(Vh[:, t, 0:128], pin[:, 256:384], AF.Copy, [("pb", tb)], [("Vh", t)])
            v4 = qk[tb][:, :].rearrange("p (a h d) -> p a h d", a=4, h=2)
            r4 = rp[tb][:, :].rearrange("p (a h d) -> p a h d", a=4, h=2)
            cb = cosT[:, t, :].unsqueeze(1).to_broadcast([128, 4, 32])
            sb = sinT[:, t, :].unsqueeze(1).to_broadcast([128, 4, 32])
            td = tmpd[tb][:, :, :].rearrange("p a (h d) -> p a h d", h=4)
            tp = tmpp[tb][:, :, :].rearrange("p a (h d) -> p a h d", h=4)
            TT("dve", td[:, 0], v4[:, :, 0, :], cb, ALU.mult, [("qk", tb), "cst"], [("td0", tb)])
            TT("dve", td[:, 1], v4[:, :, 1, :], sb, ALU.mult, [("qk", tb), "cst"], [("td1", tb)])
            TT("dve", r4[:, :, 0, :], td[:, 0], td[:, 1], ALU.subtract, [("td0", tb), ("td1", tb)], [("rpA", tb)])
            TT("pool", tp[:, 0], v4[:, :, 1, :], cb, ALU.mult, [("qk", tb), "cst"], [("tp0", tb)])
            TT("pool", tp[:, 1], v4[:, :, 0, :], sb, ALU.mult, [("qk", tb), "cst"], [("tp1", tb)])
            TT("pool", r4[:, :, 1, :], tp[:, 0], tp[:, 1], ALU.add, [("tp0", tb), ("tp1", tb)], [("rpB", tb)])

        def dback(t):
            tb = t % 2
            pT = pbb[2].rearrange("p (k t) -> p k t", k=8)
            TR(pT[:, 0, :], rp[tb][:, 0:128], [("rpA", tb), ("rpB", tb)], [("pb", 2)])
            TR(pT[:, 1, :], rp[tb][:, 128:256], [("rpA", tb), ("rpB", tb)], [("pb", 2)])
            CP("act", QK[0:64, 0, t * 128:(t + 1) * 128], pT[0:64, 0, :], [("pb", 2)], [("QKa", t)])
            CP("act", QK[64:128, 2, t * 128:(t + 1) * 128], pT[64:128, 0, :], [("pb", 2)], [("QKb", t)])
            CP("act", QK[:, 1, t * 128:(t + 1) * 128], pT[:, 1, :], [("pb", 2)], [("QK", t)])

        dfront(0)
        for t in range(NT):
            if t + 1 < NT:
                dfront(t + 1)
            dback(t)

    def mk_diff_step(h, c, kt):
        j = kt - 2 * c
        k = sctr[0]
        sctr[0] += 1
        sbk, pbuf = SBANKS[k % 3], k % 4
        obank = (5, 6) if c % 2 == 0 else (7, 2)
        O = [pb[obank[m]][:, 0:258].rearrange("p (j e) -> p j e", e=129) for m in range(2)]
        Sv = pb[sbk][:, :].rearrange("p (m q) -> p m q", m=2)
        qres = [("QKa", 2 * c), ("QKa", 2 * c + 1), ("QKb", 2 * c), ("QKb", 2 * c + 1), "Qz0", "Qz1"]

        def s_():
            kT = QK[:, 1, kt * 128:(kt + 1) * 128]
            for m in range(2):
                qp = 2 * m
                if j < 0:
                    MM(Sv[:, m, :], kT, QK[:, qp, c * 256:(c + 1) * 256], True, True, [("QK", kt)] + qres, [("pb", sbk)])
                else:
                    MM(Sv[:, m, 128 * j:128 * j + 128], ident[:], maskC[:, 0:128], True, False, ["ident", "maskC"], [("pb", sbk)])
                    MM(Sv[:, m, 128 * j:128 * j + 128], kT, QK[:, qp, c * 256 + 128 * j:c * 256 + 128 * j + 128], False, True,
                       [("QK", kt)] + qres, [("pb", sbk)])
                    if j == 0:
                        MM(Sv[:, m, 128:256], kT, QK[:, qp, c * 256 + 128:c * 256 + 256], True, True,
                           [("QK", kt)] + qres, [("pb", sbk)])
            q0 = 128 * j if j > 0 else 0
            ACT(pTs[pbuf][:, :, q0:256], Sv[:, :, q0:256], AF.Exp, [("pb", sbk)], [("pTs", pbuf)], scale=0.125)

        def pv_():
            for m in range(2):
                for jj in range(max(j, 0), 2):
                    MM(O[m][:, jj, :], pTs[pbuf][:, m, 128 * jj:128 * jj + 128], Vh[:, kt, 0:129],
                       kt == 0 and jj == 0, kt == 2 * c + jj, [("pTs", pbuf), ("Vh", kt), "Vones"], [("pb", obank[m])], sg=True)

        def post_():
            for m in range(2):
                RECIP(rcd[:, c, 2 * m:2 * m + 2], O[m][:, :, 128], [("pb", obank[m])], [("rcd", c, m)])
            TS1("dve", cf1[:, c, :], rcd[:, c, 2:4], neglam[:, 0:1], ALU.mult, [("rcd", c, 1), "neglam"], [("cf1", c)])
            for jj in range(2):
                t = 2 * c + jj
                col = h * NT + t
                ab = t % 2
                TS1("dve", asb[ab][:], O[0][:, jj, 0:128], rcd[:, c, jj:jj + 1], ALU.mult,
                    [("pb", obank[0]), ("rcd", c, 0)], [("asb", ab)])
                STT(a2sb[ab][:], O[1][:, jj, 0:128], cf1[:, c, jj:jj + 1], asb[ab][:], ALU.mult, ALU.add,
                    [("pb", obank[1]), ("cf1", c), ("asb", ab)], [("a2sb", ab)])
                jo, jr = junk_slot(128)
                ACT(jo, a2sb[ab][:], AF.Square, [("a2sb", ab)], [("ssd", col)] + jr, accum_out=ssd[:, col:col + 1])
                TS1("pool", rsd[:, col:col + 1], ssd[:, col:col + 1], 128.0 * EPS, ALU.add, [("ssd", col)], [("rsd0", col)])
                TT("pool", rsd[:, col:col + 1], rsd[:, col:col + 1], mhalf[:, 0:1], ALU.pow, [("rsd0", col), "mhalf"], [("rsd", col)])
                STT(AOd[:, t, :], a2sb[ab][:], rsd[:, col:col + 1], subtab[:], ALU.mult, ALU.mult,
                    [("a2sb", ab), ("rsd", col), "subtab"], [("AOd", t)])
            if c == n_chunks - 1:
                P.dma("sp", "aod", ao_v[:, :, h * 128:(h + 1) * 128], AOd[:], [("AOd", t) for t in range(NT)], [("ao_s", h)])

        pre = (lambda: diff_inproj(h)) if (c == 0 and kt == 0) else None
        post = post_ if kt == 2 * c + 1 else None
        return Step(s_, pv_, pre, post)

    run_steps([mk_diff_step(h, c, kt) for h in range(n_diff) for c in range(n_chunks) for kt in range(2 * c + 2)])
    UA.o = UA_mark

    n_nsa = 2 if stage in ("all", "nsa") else 0
    n_q = int(os.environ.get("DBG_NQ", NT))
    P.barrier()
    UA_mark = UA.o
    Wn = UA.a("Wn", [128, 8, 652], BF16)
    W1 = UA.a("W1", [64, 16, 256], BF16)
    qn = [UA.a("qn%d" % i, [128, 448], F32) for i in range(2)]
    rn = [UA.a("rn%d" % i, [128, 512], BF16) for i in range(2)]
    tnd = [UA.a("tnd%d" % i, [128, 2, 224], F32) for i in range(2)]
    tnp = [UA.a("tnp%d" % i, [128, 2, 224], F32) for i in range(2)]
    assert UA.o - (HT_OFF + 65536) >= 31424
    nqT = UA.a("nqT", [128, 4, S], BF16)
    KV4 = UA.a("KV4", [128, 4, S], BF16)
    VS = UA.a("VS", [128, NT, 66], BF16)
    VW = UA.a("VW", [128, NT, 66], BF16)
    gat = UA.a("gat", [128, NT, 12], F32)
    AOn = [UA.a("AOn%d" % i, [128, 2, 256], BF16) for i in range(2)]
    kcmpT = UA.a("kcmpT", [64, 256], BF16)
    hsb = UA.a("hsb", [128, 2, 256], BF16)
    bsb = UA.a("bsb", [128, 2], F32)
    gex = UA.a("gex", [128, 12], F32)
    pcs = [UA.a("pcs%d" % i, [128, 4, 128], BF16) for i in range(2)]
    pss = [UA.a("pss%d" % i, [128, 4, 128], BF16) for i in range(4)]
    NB = CA.a("NB", [128, 128], BF16)
    imp = CA.a("imp", [128, 64], F32)
    imp2 = CA.a("imp2", [128, 64], F32)
    mrp = CA.a("mrp", [128, 64], F32)
    m8 = UA.a("m8", [128, 16], F32)
    rc4 = [UA.a("rc4_%d" % i, [128, 4], F32) for i in range(2)]
    cco = [UA.a("cco%d" % i, [128, 3, 4], F32) for i in range(2)]
    acc = [UA.a("acc%d" % i, [128, 4, 64], F32) for i in range(2)]
    tac = UA.a("tac", [128, 4, 64], F32)
    if n_nsa:
        MS("dve", VS[:, :, 64:65], 1.0, ["VSones"])
        MS("dve", VW[:, :, 64:65], 1.0, ["VWones"])
        MS("dve", NB[:, 0:64], 0.0, ["NB0"])
        MS("dve", hsb[:, :, 255:256], 0.0, ["hsbpad"])
        Et = nqT[0:64, 0, :]
        MS("pool", Et, 1.0, ["Et"])
        ASEL(Et, Et, [[1, S]], ALU.is_ge, 0.0, 0, -64, ["Et"], ["Et"])
        ASEL(Et, Et, [[-1, S]], ALU.is_ge, 0.0, 63, 64, ["Et"], ["Et"])
        CP("dve", KV4[64:128, 1, :], Et, ["Et"] + [("nqT", t) for t in range(NT)], ["E"])
    kv_all = [("KV4", t) for t in range(NT)]
    vca_r = ["VCAv", "VCA1", "VCAov"]

    def nsa_pre(g):
        P.dma("pool", "wn", Wn[:], w_in_u[:, :, 1536 + g * 652:1536 + (g + 1) * 652], (), ["Wn"])
        def nfront(t):
            tb = t % 2
            pa, pbk, pbi = pb[tb], pb[(2, 5)[tb]], (2, 5)[tb]
            for kc in range(8):
                MM(pa[:, :], hT[:, kc, t * 128:(t + 1) * 128], Wn[:, kc, 0:512], kc == 0, kc == 7,
                   [("hT", t), "Wn"], [("pb", tb)])
            for kc in range(8):
                MM(pbk[:, 0:140], hT[:, kc, t * 128:(t + 1) * 128], Wn[:, kc, 512:652], kc == 0, kc == 7,
                   [("hT", t), "Wn"], [("pb", pbi)])
            ACT(qn[tb][:], pa[:, 0:448], AF.Copy, [("pb", tb)], [("qn", tb)])
            ACT(rn[tb][:, 448:512], pa[:, 448:512], AF.Copy, [("pb", tb)], [("rnV", tb)])
            ACT(VS[:, t, 0:64], pbk[:, 0:64], AF.Copy, [("pb", pbi)], [("VS", t)])
            ACT(VW[:, t, 0:64], pbk[:, 64:128], AF.Copy, [("pb", pbi)], [("VW", t)])
            ACT(gex[:], pbk[:, 128:140], AF.Exp, [("pb", pbi)], ["gex"], scale=-1.0)
            TS1("dve", gex[:], gex[:], 1.0, ALU.add, ["gex"], ["gex"])
            RECIP(gat[:, t, :], gex[:], ["gex"], [("gat", t)])
            v4 = qn[tb][:, :].rearrange("p (a h d) -> p a h d", a=7, h=2)
            r4 = rn[tb][:, 0:448].rearrange("p (a h d) -> p a h d", a=7, h=2)
            cb = cosT[:, t, :].unsqueeze(1).to_broadcast([128, 7, 32])
            sb = sinT[:, t, :].unsqueeze(1).to_broadcast([128, 7, 32])
            td = tnd[tb][:, :, :].rearrange("p a (h d) -> p a h d", h=7)
            tp = tnp[tb][:, :, :].rearrange("p a (h d) -> p a h d", h=7)
            TT("dve", td[:, 0], v4[:, :, 0, :], cb, ALU.mult, [("qn", tb), "cst"], [("nd0", tb)])
            TT("dve", td[:, 1], v4[:, :, 1, :], sb, ALU.mult, [("qn", tb), "cst"], [("nd1", tb)])
            TT("dve", r4[:, :, 0, :], td[:, 0], td[:, 1], ALU.subtract, [("nd0", tb), ("nd1", tb)], [("rnA", tb)])
            TT("pool", tp[:, 0], v4[:, :, 1, :], cb, ALU.mult, [("qn", tb), "cst"], [("np0", tb)])
            TT("pool", tp[:, 1], v4[:, :, 0, :], sb, ALU.mult, [("qn", tb), "cst"], [("np1", tb)])
            TT("pool", r4[:, :, 1, :], tp[:, 0], tp[:, 1], ALU.add, [("np0", tb), ("np1", tb)], [("rnB", tb)])

        def nback(t):
            tb = t % 2
            pT = pbb[3 + tb].rearrange("p (k t) -> p k t", k=8)
            for k8 in range(8):
                TR(pT[0:64, k8, :], rn[tb][:, k8 * 64:(k8 + 1) * 64], [("rnA", tb), ("rnB", tb), ("rnV", tb)], [("pb", 3 + tb)])
            CP("dve", nqT[0:64, :, t * 128:(t + 1) * 128], pT[0:64, 0:4, :], [("pb", 3 + tb)], [("nqT", t)])
            CP("dve", KV4[0:64, :, t * 128:(t + 1) * 128], pT[0:64, 4:8, :], [("pb", 3 + tb)], [("KV4", t)])

        nfront(0)
        for t in range(NT):
            if t + 1 < NT:
                nfront(t + 1)
            nback(t)

        for kvi in range(2):
            plane = 0 if kvi == 0 else 3
            hid = pb[3][:, :].rearrange("p (j n) -> p j n", j=2)
            for half in range(2):
                P.dma("pool", "w1", W1[:], cw1[kvi][:, half * 16:(half + 1) * 16, :], (), ["W1"])
                for jh in range(2):
                    for l16 in range(16):
                        l = half * 16 + l16
                        MM(hid[:, jh, 0:255], W1[:, l16, jh * 128:(jh + 1) * 128], KV4[0:64, plane, l:l + 16 * 254 + 1:16],
                           l == 0 and jh == 0, l == 31, ["W1"] + kv_all, [("pb", 3)], sg=True)
                for jh in range(2):
                    for l16 in range(16):
                        l = half * 16 + l16
                        MM(pb[4][:, jh:jh + 1], W1[:, l16, jh * 128:(jh + 1) * 128], posT[kvi][:, l:l + 1],
                           l == 0 and jh == 0, l == 31, ["W1", ("posT", kvi)], [("pb", 4)], sg=True)
            CP("dve", bsb[:], pb[4][:, 0:2], [("pb", 4)], ["bsb"])
            for jh in range(2):
                ACT(hsb[:, jh, 0:255], hid[:, jh, 0:255], AF.Silu, [("pb", 3), "bsb", "hsbpad"], [("hsb", jh)],
                    bias=bsb[:, jh:jh + 1])
            if kvi == 0:
                for jh in range(2):
                    MM(pb[5][0:64, 0:256], w2[0][:, jh, :], hsb[:, jh, :], jh == 0, jh == 1,
                       [("hsb", 0), ("hsb", 1), ("w2", 0)], [("pb", 5)])
                CP("dve", kcmpT[:], pb[5][0:64, 0:256], [("pb", 5)], ["kcmpT"])
            else:
                pv = pb[6][:, 0:128].rearrange("p (n d) -> p n d", n=2)
                for nt in range(2):
                    for jh in range(2):
                        MM(pv[:, nt, :], hsb[:, jh, nt * 128:(nt + 1) * 128], w2[1][:, jh, :], jh == 0, jh == 1,
                           [("hsb", 0), ("hsb", 1), ("w2", 1)], [("pb", 6)])
                CP("dve", VCA[:, :, 0:64], pv, [("pb", 6)], ["VCAv"])
        if g == n_nsa - 1 and stage == "all":
            for k in range(4):
                sl = slice(k * 704, (k + 1) * 704)
                P.dma("pool", "wg", Wg[:, :, sl], wg_d[:, :, sl], (), [("Wg", k)], after_all=True)
                P.dma("pool", "wu", Wu[:, :, sl], wu_d[:, :, sl], (), [("Wu", k)], after_all=True)

    def mk_cmp_step(g, i, nt, first, last):
        pi = i % 2
        k = sctr[0]
        sctr[0] += 1
        sbk = SBANKS[k % 3]
        pc = pss[k % 4]
        pcr = ("pss", k % 4)
        Sc = pb[sbk][:, :].rearrange("p (h q) -> p h q", h=4)
        q64 = nqT[0:64, :, i * 128:(i + 1) * 128]
        U = [pb[5 + b][:, 0:258].rearrange("p (j e) -> p j e", e=129) for b in range(2)]

        def s_():
            MM(Sc, kcmpT[:, nt * 128:(nt + 1) * 128], q64, True, True, ["kcmpT", ("nqT", i)], [("pb", sbk)])
            ACT(pc[:], Sc, AF.Exp, [("pb", sbk)], [pcr], scale=0.125)
            midx = None
            if nt == 0 and i <= 16:
                midx = i
            if nt == 1:
                midx = 17 + i - 16
            if midx is not None:
                TT("pool", pc[:], pc[:], maskcmp[:, midx, :].unsqueeze(1).to_broadcast([128, 4, 128]),
                   ALU.mult, [pcr, ("mcmp", midx)], [pcr])

        def pv_():
            for hh in range(4):
                MM(U[hh // 2][:, hh % 2, :], pc[:, hh, :], VCA[:, nt, 0:129], first and hh % 2 == 0, last,
                   [pcr] + vca_r, [("pb", 5 + hh // 2)], sg=True)

        def post_():
            for b in range(2):
                TS1("dve", rc4[pi][:, 2 * b:2 * b + 2], U[b][:, :, 64], 1e-30, ALU.add, [("pb", 5 + b)], [("rc4", pi)])
            RECIP(rc4[pi][:], rc4[pi][:], [("rc4", pi)], [("rc4", pi)])
            TS1("dve", imp[:], U[0][:, 0, 65:129], rc4[pi][:, 0:1], ALU.mult, [("pb", 5), ("rc4", pi)], ["imp"])
            for hh in range(1, 4):
                STT(imp[:], U[hh // 2][:, hh % 2, 65:129], rc4[pi][:, hh:hh + 1], imp[:], ALU.mult, ALU.add,
                    [("pb", 5 + hh // 2), ("rc4", pi), "imp"], ["imp"])
            c0 = 62 - 2 * i
            TT("dve", imp2[:], imp[:], cst[:, CAP0 + c0:CAP0 + c0 + 64], ALU.min, ["imp", "cst"], ["imp2"])
            TT("dve", imp2[:], imp2[:], cst[:, FLO0 + c0:FLO0 + c0 + 64], ALU.max, ["imp2", "cst"], ["imp2"])
            MS("dve", imp2[:, 0:1], 3e30, ["imp2"])
            P.op("dve", lambda e: e.max(out=m8[:, 0:8], in_=imp2[:]), ["imp2"], ["m8a"])
            P.op("dve", lambda e: e.match_replace(out=mrp[:], in_to_replace=m8[:, 0:8], in_values=imp2[:], imm_value=-3e30),
                 ["imp2", "m8a"], ["mrp"])
            P.op("dve", lambda e: e.max(out=m8[:, 8:16], in_=mrp[:]), ["mrp"], ["m8b"])
            TS("dve", NB[:, 64:128], imp2[:], m8[:, 15:16], NEG, ALU.is_lt, ALU.mult, ["imp2", "m8b"], ["NB"])
            pTn = pbb[0][:, 0:128]
            TR(pTn, NB[:, :], ["NB", "NB0"], [("pb", 0)])
            CP("act", nqT[64:128, :, i * 128:(i + 1) * 128], pTn[64:128, :].unsqueeze(1).to_broadcast([64, 4, 128]),
               [("pb", 0)], [("nqTb", i)])
            TT("dve", cco[pi][:, 0, :], rc4[pi][:], gat[:, i, 0:12:3], ALU.mult, [("rc4", pi), ("gat", i)], [("cco", pi, 0)])
            for b in range(2):
                TT("dve", acc[pi][:, 2 * b:2 * b + 2, :], U[b][:, :, 0:64],
                   cco[pi][:, 0, 2 * b:2 * b + 2].unsqueeze(2).to_broadcast([128, 2, 64]), ALU.mult,
                   [("pb", 5 + b), ("cco", pi, 0)], [("acc", pi)])

        return Step(s_, pv_, None, post_ if last else None)

    def mk_br_step(g, i, br, kt, kts):
        pi = i % 2
        k = sctr[0]
        sctr[0] += 1
        sbk, pbuf = SBANKS[k % 3], k % 4
        Ss = pb[sbk][:, :].rearrange("p (h q) -> p h q", h=4)
        q64 = nqT[0:64, :, i * 128:(i + 1) * 128]
        q128 = nqT[:, :, i * 128:(i + 1) * 128]
        if br == 2:
            obk, Vt, vres = 2, VW, "VW"
        else:
            obk, Vt, vres = 7, VS, "VS"
        Ob = pb[obk][:, 0:260].rearrange("p (h e) -> p h e", e=65)

        def s_():
            first = True
            if kt == i:
                MM(Ss, ident[:], maskC4, True, False, ["ident", "maskC"], [("pb", sbk)])
                first = False
            elif br == 2 and kt == i - 4:
                MM(Ss, ident[:], maskW4, True, False, ["ident", "maskW"], [("pb", sbk)])
                first = False
            if br == 2:
                MM(Ss, KV4[0:64, 2, kt * 128:(kt + 1) * 128], q64, first, True, [("KV4", kt), ("nqT", i)], [("pb", sbk)])
            else:
                MM(Ss, KV4[:, 1, kt * 128:(kt + 1) * 128], q128, first, True,
                   [("KV4", kt), "E", ("nqTb", i), ("nqT", i)], [("pb", sbk)])
            ACT(pss[pbuf][:], Ss, AF.Exp, [("pb", sbk)], [("pss", pbuf)], scale=0.125)

        def pv_():
            for hh in range(4):
                MM(Ob[:, hh, :], pss[pbuf][:, hh, :], Vt[:, kt, 0:65], kt == kts[0] and hh == 0, kt == kts[-1],
                   [("pss", pbuf), (vres, kt), vres + "ones"], [("pb", obk)], sg=True)

        def post_():
            RECIP(cco[pi][:, br, :], Ob[:, :, 64], [("pb", obk)], [("cco", pi, br)])
            TT("dve", cco[pi][:, br, :], cco[pi][:, br, :], gat[:, i, br:12:3], ALU.mult, [("cco", pi, br), ("gat", i)], [("cco", pi, br)])
            TT("dve", tac[:], Ob[:, :, 0:64], cco[pi][:, br, :].unsqueeze(2).to_broadcast([128, 4, 64]), ALU.mult,
               [("pb", obk), ("cco", pi, br)], ["tac"])
            TT("pool", acc[pi][:], acc[pi][:], tac[:], ALU.add, ["tac", ("acc", pi)], [("acc", pi)])
            if br == 1:
                ab = (i // 2) % 2
                CP("pool", AOn[ab][:, i % 2, :], acc[pi][:, :, :].rearrange("p h d -> p (h d)"), [("acc", pi)], [("AOn", ab)])
                if i % 2 == 1:
                    P.dma("act", "aon%d" % ab, ao_v[:, i - 1:i + 1, 512 + g * 256:512 + (g + 1) * 256], AOn[ab][:],
                          [("AOn", ab)], [("ao_n", g, i)])

        return Step(s_, pv_, None, post_ if kt == kts[-1] else None)

    def cmp_steps(g, i):
        nts = [0] if i < 16 else [0, 1]
        return [mk_cmp_step(g, i, nt, nt == nts[0], nt == nts[-1]) for nt in nts]

    nsa_steps = []
    for g in range(n_nsa):
        gsteps = []
        if n_q:
            gsteps += cmp_steps(g, 0)
        for i in range(n_q):
            kts_w = list(range(max(0, i - 4), i + 1))
            gsteps += [mk_br_step(g, i, 2, kt, kts_w) for kt in kts_w]
            if i + 1 < n_q:
                gsteps += cmp_steps(g, i + 1)
            kts_s = list(range(0, i + 1))
            gsteps += [mk_br_step(g, i, 1, kt, kts_s) for kt in kts_s]
        if gsteps:
            gsteps[0].pre = (lambda g=g: nsa_pre(g))
        else:
            nsa_pre(g)
        nsa_steps += gsteps
    run_steps(nsa_steps)
    UA.o = UA_mark

    if stage in ("diff", "nsa", "none"):
        st = P.emit()
        return nc, st

    P.barrier()
    Wdn = PA.a("Wdn", [128, FC, D], BF16)
    tm = PA.a("tm", [128, 2, D], F32)
    h2n = PA.a("h2n", [128, D], BF16)
    h2T_off = (PA.o + 31) // 32 * 32
    h2T = PA.a("h2T", [128, 8, 512], BF16)
    ov_base = PA.o
    Wo = PA.a("Wo", [128, 8, D], BF16)
    gA = PA.a("gA", [128, D], F32)
    aos = [PA.a("aos%d" % i, [128, 2, D], BF16) for i in range(2)]
    aoT = [nc.alloc_sbuf_tensor_at("aoT%d" % i, [128, 8, 128], BF16, offset=h2T_off + 2048 * i) for i in range(2)]
    PA.o = ov_base
    sg = [PA.a("sg%d" % i, [128, 512], F32) for i in range(2)]
    actT = PA.a("actT", [128, FC, 512], BF16)
    P.dma("pool", "wo", Wo[:], w_out_r, (), ["Wo"])
    P.dma("sp", "gv", gA[:], gpostA_d[0].partition_broadcast(128), (), ["gA0"])
    P.dma("sp", "gv", gF[:], gpostF_d[0].partition_broadcast(128), (), ["gF0"])
    TS1("pool", gA[:], gA[:], 32.0, ALU.mult, ["gA0"], ["gA"])
    TS1("pool", gF[:], gF[:], 32.0, ALU.mult, ["gF0"], ["gF"])
    for k in range(2):
        sl = slice(k * 11, (k + 1) * 11)
        P.dma("pool", "wd", Wdn[:, sl, :], wd_d[:, sl, :], (), [("Wdn", k)])
    g2_bc = g2T[:, :].unsqueeze(2).to_broadcast([128, 8, 128])
    out_v = out.rearrange("(t p) d -> p t d", p=128)

    def norm_scale(col, ssrc_ap, res_in, tag):
        TS1("pool", st3[:, 1, col:col + 1], ssrc_ap, 1024.0 * EPS, ALU.add, res_in, [(tag + "0", col)])
        TT("pool", st3[:, 1, col:col + 1], st3[:, 1, col:col + 1], mhalf[:, 0:1], ALU.pow, [(tag + "0", col), "mhalf"], [(tag, col)])

    def c0_front(t):
        tp, jx = t // 2, t % 2
        xb = tp % 2
        b = t % 2
        mb = (2, 4)[b]
        if jx == 0:
            P.dma("sp", "aos%d" % xb, aos[xb][:], ao_v[:, 2 * tp:2 * tp + 2, :], (), [("aos", xb)])
            P.dma("act", "xs2%d" % xb, xs2[xb][:], x_v[:, 2 * tp:2 * tp + 2, :], (), [("xs2", xb)])
        pT = pbb[b].rearrange("p (k t) -> p k t", k=8)
        for kc in range(8):
            TR(pT[:, kc, :], aos[xb][:, jx, kc * 128:(kc + 1) * 128], [("aos", xb)], [("pb", b)])
        CP("act", aoT[b][:], pT, [("pb", b)], [("aoT", b)])
        for nh in range(2):
            for kc in range(8):
                MM(pb[mb + nh][:, :], aoT[b][:, kc, :], Wo[:, kc, nh * 512:(nh + 1) * 512], kc == 0, kc == 7,
                   [("aoT", b), "Wo"], [("pb", mb + nh)])

    def c0_back(t):
        tp, jx = t // 2, t % 2
        xb = tp % 2
        mb = (2, 4)[t % 2]
        for nh in range(2):
            jo, jr = junk_slot(512)
            ACT(jo, pb[mb + nh][:, :], AF.Square, [("pb", mb + nh)], [("ssm", t, nh)] + jr,
                accum_out=st3[:, 2 + nh, t:t + 1])
        TT("pool", st3[:, 0, t:t + 1], st3[:, 2, t:t + 1], st3[:, 3, t:t + 1], ALU.add, [("ssm", t, 0), ("ssm", t, 1)], [("ssmt", t)])
        norm_scale(t, st3[:, 0, t:t + 1], [("ssmt", t)], "rsm")
        for nh in range(2):
            STT(tm[:, jx, nh * 512:(nh + 1) * 512], pb[mb + nh][:, :], st3[:, 1, t:t + 1], gA[:, nh * 512:(nh + 1) * 512],
                ALU.mult, ALU.mult, [("pb", mb + nh), ("rsm", t), "gA"], [("tm", jx, nh)])
        if jx == 1:
            TT("pool", tm[:], tm[:], xs2[xb][:], ALU.add, [("tm", 0, 0), ("tm", 0, 1), ("tm", 1, 0), ("tm", 1, 1), ("xs2", xb)],
               ["x1", ("tm", 0, 0), ("tm", 0, 1), ("tm", 1, 0), ("tm", 1, 1)])
            P.dma("sp", "x1o", out_v[:, 2 * tp:2 * tp + 2, :], tm[:], ["x1", ("tm", 0, 0), ("tm", 0, 1), ("tm", 1, 0), ("tm", 1, 1)], [("out", tp)])

    c0_front(0)
    for t in range(NT):
        if t + 1 < NT:
            c0_front(t + 1)
        c0_back(t)

    P.barrier(keep_chans=("wd",), keep_res=[("Wdn", 0), ("Wdn", 1)])
    wg_r = [("Wg", k) for k in range(4)]
    wu_r = [("Wu", k) for k in range(4)]
    wd_r = [("Wdn", k) for k in range(2)]
    tm_all = [("tm", 0, 0), ("tm", 0, 1), ("tm", 1, 0), ("tm", 1, 1)]
    for blk in range(8):
        for tt in range(4):
            t = blk * 4 + tt
            b = t % 2
            tp, jx = t // 2, t % 2
            xb = tp % 2
            if jx == 0:
                P.dma("sp", "xs2%d" % xb, xs2[xb][:], out_v[:, 2 * tp:2 * tp + 2, :], [("out", tp)], [("xs2", xb)])
            xin = xs2[xb][:, jx, :]
            jo, jr = junk_slot(1024)
            ACT(jo, xin, AF.Square, [("xs2", xb)], [("ss2", t)] + jr, accum_out=st3[:, 4, t:t + 1])
            TS1("pool", st3[:, 5, t:t + 1], st3[:, 4, t:t + 1], 1024.0 * EPS, ALU.add, [("ss2", t)], [("rs20", t)])
            TT("pool", st3[:, 5, t:t + 1], st3[:, 5, t:t + 1], mhalf[:, 0:1], ALU.pow, [("rs20", t), "mhalf"], [("rs2", t)])
            TS("dve", h2n[:], xin, st3[:, 5, t:t + 1], 32.0, ALU.mult, ALU.mult, [("xs2", xb), ("rs2", t)], ["h2n"])
            pT = pbb[b].rearrange("p (k t) -> p k t", k=8)
            for kc in range(8):
                TR(pT[:, kc, :], h2n[:, kc * 128:(kc + 1) * 128], ["h2n"], [("pb", b)])
            TT("dve", h2T[:, :, tt * 128:(tt + 1) * 128], pT, g2_bc, ALU.mult, [("pb", b), "g2T"], [("h2T", tt)])
        h2r = [("h2T", tt) for tt in range(4)]
        for fc in range(FC):
            p2 = fc % 2
            gb, ub = (4, 5) if p2 == 0 else (6, 7)
            for kc in range(8):
                MM(pb[gb][:, :], Wg[:, kc, fc * 128:(fc + 1) * 128], h2T[:, kc, :], kc == 0, kc == 7, wg_r + h2r, [("pb", gb)])
            for kc in range(8):
                MM(pb[ub][:, :], Wu[:, kc, fc * 128:(fc + 1) * 128], h2T[:, kc, :], kc == 0, kc == 7, wu_r + h2r, [("pb", ub)])
            ACT(sg[p2][:], pb[gb][:, :], AF.Silu, [("pb", gb)], [("sg", p2)])
            TT("dve", actT[:, fc, :], sg[p2][:], pb[ub][:, :], ALU.mult, [("sg", p2), ("pb", ub)], [("actT", fc)])
        ar = [("actT", fc) for fc in range(FC)]
        for tt in range(4):
            t = blk * 4 + tt
            tp, jx = t // 2, t % 2
            for nh in range(2):
                for fc in range(FC):
                    MM(pb[2 + nh][:, :], actT[:, fc, tt * 128:(tt + 1) * 128], Wdn[:, fc, nh * 512:(nh + 1) * 512],
                       fc == 0, fc == FC - 1, ar + wd_r, [("pb", 2 + nh)])
            for nh in range(2):
                jo, jr = junk_slot(512)
                ACT(jo, pb[2 + nh][:, :], AF.Square, [("pb", 2 + nh)], [("ssy", t, nh)] + jr,
                    accum_out=st3[:, 2 + nh, 32 + t:33 + t])
            TT("pool", st3[:, 0, 32 + t:33 + t], st3[:, 2, 32 + t:33 + t], st3[:, 3, 32 + t:33 + t], ALU.add,
               [("ssy", t, 0), ("ssy", t, 1)], [("ssyt", t)])
            norm_scale(32 + t, st3[:, 0, 32 + t:33 + t], [("ssyt", t)], "rsy")
            for nh in range(2):
                STT(tm[:, jx, nh * 512:(nh + 1) * 512], pb[2 + nh][:, :], st3[:, 1, 32 + t:33 + t], gF[:, nh * 512:(nh + 1) * 512],
                    ALU.mult, ALU.mult, [("pb", 2 + nh), ("rsy", 32 + t), "gF"], [("tm", jx, nh)])
            if jx == 1:
                P.dma("pool", "acc", out_v[:, 2 * tp:2 * tp + 2, :], tm[:], tm_all + [("out", tp)],
                      [("out", tp)] + tm_all, accum_op=ALU.add)
    st = P.emit()
    return nc, st


def _consts():
    c = np.zeros((128, NCONST), np.float32)
    inv = 1.0 / (10000.0 ** (np.arange(0, 64, 2, dtype=np.float32) / 64.0))
    pos = (np.arange(NT)[None, :, None] * 128 + np.arange(128)[:, None, None]).astype(np.float32)
    ang = pos * inv[None, None, :]
    c[:, 0:1024] = np.cos(ang).astype(np.float32).reshape(128, 1024)
    c[:, 1024:2048] = np.sin(ang).astype(np.float32).reshape(128, 1024)
    hi = (np.arange(128) >= 64).astype(np.int64)[:, None]
    m = np.arange(128)[None, :] - 62
    c[:, 2048:2176] = np.where(m <= hi, 1e30, -1e30)
    c[:, 2176:2304] = np.where(m == hi, 2e30, np.where(m == hi - 1, 1e30, -3e30))
    n = np.arange(256)[:, None]
    j = np.arange(64)[None, :]
    ov = np.clip(np.minimum(16 * n + 32, 64 * j + 64) - np.maximum(16 * n, 64 * j), 0, None) / 32.0
    ov[255] = 0.0
    c[:, 2304:2432] = ov.reshape(2, 128, 64).transpose(1, 0, 2).reshape(128, 128)
    return c


def _layout(inp):
    f = lambda k: np.asarray(inp[k], np.float32)[0]
    w_in = f("w_in")
    cols = []
    for h in range(4):
        cols += [np.arange(h * 128, h * 128 + 128), 512 + np.arange(h * 128, h * 128 + 128), 1024 + np.arange(h * 128, h * 128 + 128)]
    for g in range(2):
        cols += [1536 + g * 256 + np.arange(256)]
        for base in (2048, 2304, 2560, 2176, 2432, 2688):
            cols += [base + g * 64 + np.arange(64)]
        cols += [2816 + g * 12 + np.arange(12)]
    cols = np.concatenate(cols)
    r8 = lambda w: np.ascontiguousarray(w.reshape(8, 128, -1).transpose(1, 0, 2))
    m = {
        "consts": _consts(),
        "gpreT": np.ascontiguousarray(f("attn_pre_norm").reshape(8, 128).T),
        "g2T": np.ascontiguousarray(f("ffn_pre_norm").reshape(8, 128).T),
        "w_in_u": r8(w_in[:, cols]),
        "lam4": np.stack([f("lambda_q1"), f("lambda_k1"), f("lambda_q2"), f("lambda_k2")]),
        "subln": f("diff_subln")[None, :],
        "kw1": np.ascontiguousarray(f("k_cmp_w1").reshape(32, 64, 256).transpose(1, 0, 2)),
        "vw1": np.ascontiguousarray(f("v_cmp_w1").reshape(32, 64, 256).transpose(1, 0, 2)),
        "kposT": np.ascontiguousarray(f("k_cmp_pos").T),
        "vposT": np.ascontiguousarray(f("v_cmp_pos").T),
        "kw2": np.ascontiguousarray(f("k_cmp_w2").reshape(2, 128, 64).transpose(1, 0, 2)),
        "vw2": np.ascontiguousarray(f("v_cmp_w2").reshape(2, 128, 64).transpose(1, 0, 2)),
        "w_out_r": r8(f("w_out")),
        "gpostA": f("attn_post_norm")[None, :],
        "gpostF": f("ffn_post_norm")[None, :],
        "wg": r8(f("w_gate")),
        "wu": r8(f("w_up")),
        "wd": np.ascontiguousarray(f("w_down").reshape(FC, 128, D).transpose(1, 0, 2)),
    }
    return m


_CACHE = {}


def kernel(**inputs):
    if "nc" not in _CACHE:
        _CACHE["nc"] = build()[0]
    nc = _CACHE["nc"]
    shared = _layout(inputs)
    xs = np.asarray(inputs["x"], np.float32)
    in_maps = [dict(shared, x=np.ascontiguousarray(xs[b])) for b in range(8)]
    res = run_bass_kernel_spmd(nc, in_maps, core_ids=list(range(8)))
    return np.stack([np.asarray(r["out"], np.float32) for r in res.results], axis=0)
```

```python
import numpy as np
import concourse.bass as bass
import concourse.mybir as mybir
from concourse.bass_utils import run_bass_kernel_spmd

F32 = mybir.dt.float32
BF16 = mybir.dt.bfloat16
ALU = mybir.AluOpType
AF = mybir.ActivationFunctionType
AX = mybir.AxisListType

S, D, NT, KC = 4096, 1024, 32, 8
DFF, FC = 2816, 22
NEG = -30000.0
EPS = 1e-6
LAM_INIT = 0.2
NCONST = 2432
ENGS = ("pe", "act", "dve", "pool", "sp")


class _Op:
    __slots__ = ("eng", "fn", "deps", "sig", "ticket", "chan", "is_dma")

    def __init__(self, eng, fn, deps, chan=None):
        self.eng, self.fn, self.deps = eng, fn, deps
        self.sig, self.ticket, self.chan = False, 0, chan
        self.is_dma = chan is not None


class Prog:
    def __init__(self, nc):
        self.nc = nc
        self.ops = []
        self.last_w = {}
        self.readers = {}
        self.chan_last = {}
        self.eng_last = {}

    def _deps(self, reads, writes):
        d = set()
        for r in reads:
            w = self.last_w.get(r)
            if w is not None:
                d.add(w)
        for w_ in writes:
            w = self.last_w.get(w_)
            if w is not None:
                d.add(w)
            d.update(self.readers.get(w_, ()))
        return d

    def _commit(self, idx, reads, writes):
        for r in reads:
            self.readers.setdefault(r, []).append(idx)
        for w_ in writes:
            self.last_w[w_] = idx
            self.readers[w_] = []

    def op(self, eng, fn, reads=(), writes=()):
        d = self._deps(reads, writes)
        idx = len(self.ops)
        if eng == "pe":
            d = {x for x in d if self.ops[x].eng != "pe" or self.ops[x].is_dma}
        self.ops.append(_Op(eng, fn, d))
        self._commit(idx, reads, writes)
        self.eng_last[eng] = idx
        return idx

    def dma(self, eng, chan, out, in_, reads=(), writes=(), after_all=False, **kw):
        d = self._deps(reads, writes)
        if after_all:
            d.update(self.eng_last.values())
        prev = self.chan_last.get(chan)
        if prev is not None:
            d.add(prev)
        idx = len(self.ops)
        self.ops.append(_Op(eng, lambda e: e.dma_start(out=out, in_=in_, **kw), d, chan=chan))
        self.chan_last[chan] = idx
        self._commit(idx, reads, writes)
        return idx

    def barrier(self, keep_chans=(), keep_res=()):
        deps = set(self.eng_last.values()) | {v for c, v in self.chan_last.items() if c not in keep_chans}
        kept = {r: self.last_w[r] for r in keep_res if r in self.last_w}
        for eng in ENGS:
            idx = len(self.ops)
            self.ops.append(_Op(eng, None, set(deps)))
            self.eng_last[eng] = idx
        self.last_w.clear()
        self.readers.clear()
        self.last_w.update(kept)

    def emit(self):
        nc, ops = self.nc, self.ops
        for o in ops:
            best = {}
            for d in o.deps:
                od = ops[d]
                if od.is_dma:
                    od.sig = True
                elif od.fn is not None and best.get(od.eng, -1) < d:
                    best[od.eng] = d
            for d in best.values():
                ops[d].sig = True
        cnt = {e: 0 for e in ENGS}
        chan_cnt = {}
        for o in ops:
            if o.is_dma:
                o.sig = True
                chan_cnt[o.chan] = chan_cnt.get(o.chan, 0) + 16
                o.ticket = chan_cnt[o.chan]
            elif o.fn is None:
                o.sig = False
            elif o.sig:
                cnt[o.eng] += 1
                o.ticket = cnt[o.eng]
        sems = {e: nc.alloc_semaphore("s_" + e) for e in ENGS if e != "sp"}
        csems = {c: nc.alloc_semaphore("c_" + str(c)) for c in chan_cnt}
        per_eng = {e: [] for e in ENGS}
        for i, o in enumerate(ops):
            per_eng[o.eng].append(i)

        def run(engname):
            def f(e):
                waited = {}
                for i in per_eng[engname]:
                    o = ops[i]
                    need = {}
                    for d in o.deps:
                        od = ops[d]
                        if od.fn is None and not od.is_dma:
                            continue
                        key = ("c", od.chan) if od.is_dma else ("e", od.eng)
                        if need.get(key, 0) < od.ticket:
                            need[key] = od.ticket
                    for key, val in need.items():
                        if waited.get(key, 0) >= val:
                            continue
                        waited[key] = val
                        e.wait_ge(csems[key[1]] if key[0] == "c" else sems[key[1]], val)
                    if o.fn is None:
                        continue
                    ins = o.fn(e)
                    if o.is_dma:
                        ins.then_inc(csems[o.chan], 16)
                    elif o.sig:
                        ins.then_inc(sems[o.eng], 1)
                if engname == "sp":
                    for c, v in chan_cnt.items():
                        e.wait_ge(csems[c], v)
                    for en in ("pe", "act", "dve", "pool"):
                        if cnt[en]:
                            e.wait_ge(sems[en], cnt[en])
            return f

        with nc.Block() as block:
            block.tensor(run("pe"))
            block.scalar(run("act"))
            block.vector(run("dve"))
            block.gpsimd(run("pool"))
            block.sync(run("sp"))
        return {e: len(per_eng[e]) for e in ENGS}


class Arena:
    def __init__(self, nc, base, limit):
        self.nc, self.o, self.limit = nc, base, limit

    def a(self, name, shape, dt):
        nb = 2 if dt == BF16 else 4
        size = int(np.prod(shape[1:])) * nb
        off = (self.o + 31) // 32 * 32
        self.o = off + size
        assert self.o <= self.limit, (name, self.o, self.limit)
        return self.nc.alloc_sbuf_tensor_at(name, list(shape), dt, offset=off)


def build(stage="all", dbg=False):
    nc = bass.Bass("TRN2", target_bir_lowering=False)
    P = Prog(nc)

    def din(name, shape, dt=F32):
        return nc.dram_tensor(name, list(shape), dt, kind="ExternalInput").ap()

    x = din("x", [S, D])
    consts = din("consts", [128, NCONST])
    gpreT_d = din("gpreT", [128, 8])
    g2T_d = din("g2T", [128, 8])
    w_in_u = din("w_in_u", [128, 8, 2840])
    lam4 = din("lam4", [4, 64])
    subln = din("subln", [1, 128])
    cw1 = [din("kw1", [64, 32, 256]), din("vw1", [64, 32, 256])]
    cposT = [din("kposT", [64, 32]), din("vposT", [64, 32])]
    cw2 = [din("kw2", [128, 2, 64]), din("vw2", [128, 2, 64])]
    w_out_r = din("w_out_r", [128, 8, D])
    gpostA_d = din("gpostA", [1, D])
    gpostF_d = din("gpostF", [1, D])
    wg_d = din("wg", [128, 8, DFF])
    wu_d = din("wu", [128, 8, DFF])
    wd_d = din("wd", [128, FC, D])
    out = nc.dram_tensor("out", [S, D], F32, kind="ExternalOutput").ap()
    ao_s = nc.dram_tensor("ao_s", [S, D], BF16, kind="ExternalOutput" if dbg else "Internal").ap()
    ao_v = ao_s.rearrange("(t p) c -> p t c", p=128)

    BASE = 16512
    TOP = 229344
    CA0 = Arena(nc, BASE, BASE + 3 * 1024)
    PC_OFF = BASE + 3 * 1024
    CA = Arena(nc, PC_OFF, BASE + 26 * 1024)
    HT_OFF = BASE + 26 * 1024
    UA = Arena(nc, HT_OFF + 65536, TOP)
    hT = nc.alloc_sbuf_tensor_at("hT", [128, 8, S], BF16, offset=HT_OFF)

    pb = [nc.alloc_psum_tensor("pb%d" % i, [128, 512], F32) for i in range(8)]
    pbb = [p[:, :].bitcast(BF16) for p in pb]

    def MM(o, lhsT, rhs, start, stop, r, w, sg=False):
        if sg:
            P.op("pe", lambda e: e.matmul(o, lhsT=lhsT, rhs=rhs, start=start, stop=stop, skip_group_check=True), r, w)
        else:
            P.op("pe", lambda e: e.matmul(o, lhsT=lhsT, rhs=rhs, start=start, stop=stop), r, w)

    def ACT(o, i, func, r, w, **kw):
        P.op("act", lambda e: e.activation(out=o, in_=i, func=func, **kw), r, w)

    def TS(eng, o, i, s1, s2, op0, op1, r, w):
        P.op(eng, lambda e: e.tensor_scalar(out=o, in0=i, scalar1=s1, scalar2=s2, op0=op0, op1=op1), r, w)

    def TS1(eng, o, i, s1, op0, r, w):
        P.op(eng, lambda e: e.tensor_scalar(out=o, in0=i, scalar1=s1, scalar2=None, op0=op0), r, w)

    def TT(eng, o, a, b, op, r, w):
        P.op(eng, lambda e: e.tensor_tensor(out=o, in0=a, in1=b, op=op), r, w)

    def STT(o, a, sc, b, op0, op1, r, w):
        P.op("dve", lambda e: e.scalar_tensor_tensor(out=o, in0=a, scalar=sc, in1=b, op0=op0, op1=op1), r, w)

    def CP(eng, o, i, r, w):
        if eng == "act":
            P.op("act", lambda e: e.copy(out=o, in_=i), r, w)
        else:
            P.op(eng, lambda e: e.tensor_copy(out=o, in_=i), r, w)

    def MS(eng, o, val, w):
        P.op(eng, lambda e: e.memset(o, val), (), w)

    def RECIP(o, i, r, w):
        P.op("dve", lambda e: e.reciprocal(out=o, in_=i), r, w)

    def TR(o, i, r, w):
        P.op("pe", lambda e: e.transpose(out=o, in_=i, identity=ident[:]), list(r) + ["ident"], w)

    def ASEL(o, i, pattern, cmp, fill, base, cm, r, w):
        P.op("pool", lambda e: e.affine_select(out=o, in_=i, pattern=pattern, compare_op=cmp, fill=fill,
                                               base=base, channel_multiplier=cm), r, w)

    cst = CA.a("cst", [128, NCONST], F32)
    P.dma("sp", "cst", cst[:], consts, (), ["cst"])
    cosT = cst[:, 0:1024].rearrange("p (t f) -> p t f", f=32)
    sinT = cst[:, 1024:2048].rearrange("p (t f) -> p t f", f=32)
    CAP0, FLO0, OV0 = 2048, 2176, 2304
    gpreT = CA0.a("gpreT", [128, 8], F32)
    g2T = CA0.a("g2T", [128, 8], F32)
    P.dma("sp", "gv", gpreT[:], gpreT_d, (), ["gpreT"])
    P.dma("sp", "gv", g2T[:], g2T_d, (), ["g2T"])
    ident = CA0.a("ident", [128, 128], BF16)
    maskC = CA.a("maskC", [128, 512], BF16)
    maskW = CA.a("maskW", [128, 512], BF16)
    mhalf = CA0.a("mhalf", [128, 4], F32)
    zf = UA.a("zf", [128, 512], F32)
    MS("pool", zf[:], 0.0, ["zf"])
    MS("pool", mhalf[:], -0.5, ["mhalf"])
    ASEL(ident[:], zf[:, 0:128], [[-1, 128]], ALU.not_equal, 1.0, 0, 1, ["zf"], ["ident"])
    ASEL(maskC[:], zf[:], [[0, 4], [1, 128]], ALU.is_ge, NEG, 0, -1, ["zf"], ["maskC"])
    ASEL(maskW[:], zf[:], [[0, 4], [-1, 128]], ALU.is_gt, NEG, 0, 1, ["zf"], ["maskW"])
    maskC4 = maskC[:, :].rearrange("p (h q) -> p h q", h=4)
    maskW4 = maskW[:, :].rearrange("p (h q) -> p h q", h=4)
    lamt = UA.a("lamt", [128, 4, 64], F32)
    P.dma("sp", "gv", lamt[:], lam4.partition_broadcast(128), (), ["lamt"])
    lprod = UA.a("lprod", [128, 2, 64], F32)
    lsum = CA.a("lsum", [128, 2], F32)
    lexp = CA.a("lexp", [128, 2], F32)
    neglam = CA.a("neglam", [128, 1], F32)
    TT("dve", lprod[:], lamt[:, 0:4:2, :], lamt[:, 1:4:2, :], ALU.mult, ["lamt"], ["lprod"])
    P.op("dve", lambda e: e.reduce_sum(out=lsum[:], in_=lprod[:], axis=AX.X), ["lprod"], ["lsum"])
    ACT(lexp[:], lsum[:], AF.Exp, ["lsum"], ["lexp"])
    TT("dve", neglam[:], lexp[:, 1:2], lexp[:, 0:1], ALU.subtract, ["lexp"], ["neglam0"])
    TS1("dve", neglam[:], neglam[:], -LAM_INIT, ALU.add, ["neglam0"], ["neglam"])
    subtab = CA.a("subtab", [128, 128], F32)
    P.dma("sp", "gv", subtab[:], subln[0].partition_broadcast(128), (), ["subtab0"])
    TS1("dve", subtab[:], subtab[:], (1.0 - LAM_INIT) * float(np.sqrt(128.0)), ALU.mult, ["subtab0"], ["subtab"])
    posT = [CA.a("posT%d" % i, [64, 32], BF16) for i in range(2)]
    w2 = [CA.a("w2_%d" % i, [128, 2, 64], BF16) for i in range(2)]
    for i in range(2):
        P.dma("pool", "gvp", posT[i][:], cposT[i], (), [("posT", i)])
        P.dma("pool", "gvp", w2[i][:], cw2[i], (), [("w2", i)])
    VCA = CA.a("VCA", [128, 2, 130], BF16)
    MS("dve", VCA[:, :, 64:65], 1.0, ["VCA1"])
    CP("dve", VCA[:, :, 65:129], cst[:, OV0:OV0 + 128].rearrange("p (n j) -> p n j", n=2), ["cst"], ["VCAov"])
    maskcmp = CA.a("maskcmp", [128, 33, 128], BF16)
    onesb = UA.a("onesb", [128, 128], BF16)
    MS("pool", onesb[:], 1.0, ["onesb"])
    for i in range(17):
        ASEL(maskcmp[:, i, :], onesb[:], [[1, 128]], ALU.is_ge, 0.0, 128 * i - 31, -16, ["onesb"], [("mcmp", i)])
    for i in range(16, 32):
        ASEL(maskcmp[:, 17 + i - 16, :], onesb[:], [[1, 128]], ALU.is_ge, 0.0, 128 * i - 31 - 2048, -16,
             ["onesb"], [("mcmp", 17 + i - 16)])
    ssA = CA.a("ssA", [128, 32], F32)
    rsA = CA.a("rsA", [128, 32], F32)
    junk = CA0.a("junk", [128, D], BF16)
    jctr = [0, 0]

    def junk_slot(width):
        if width == 1024:
            return junk[:, :], ["j%d" % i for i in range(8)]
        if width == 512:
            hh = jctr[0] % 2
            jctr[0] += 1
            return junk[:, hh * 512:(hh + 1) * 512], ["j%d" % i for i in range(4 * hh, 4 * hh + 4)]
        sl = jctr[1] % 8
        jctr[1] += 1
        return junk[:, sl * 128:(sl + 1) * 128], ["j%d" % sl]

    P.barrier()
    UA.o = HT_OFF + 65536
    UA_mark = UA.o
    xs = [UA.a("xs%d" % i, [128, 2, D], F32) for i in range(2)]
    xn = [UA.a("xn%d" % i, [128, D], BF16) for i in range(2)]
    gpre_bc = gpreT[:, :].unsqueeze(2).to_broadcast([128, 8, 128])
    x_v = x.rearrange("(t p) d -> p t d", p=128)
    for tp in range(NT // 2):
        xb = tp % 2
        P.dma("sp", "xs%d" % xb, xs[xb][:], x_v[:, 2 * tp:2 * tp + 2, :], (), [("xs", xb)])
        for jx in range(2):
            t = 2 * tp + jx
            b = t % 2
            xin = xs[xb][:, jx, :]
            jo, jr = junk_slot(1024)
            ACT(jo, xin, AF.Square, [("xs", xb)], [("ssA", t)] + jr, accum_out=ssA[:, t:t + 1])
            TS1("pool", rsA[:, t:t + 1], ssA[:, t:t + 1], 1024.0 * EPS, ALU.add, [("ssA", t)], [("rsA0", t)])
            TT("pool", rsA[:, t:t + 1], rsA[:, t:t + 1], mhalf[:, 0:1], ALU.pow, [("rsA0", t), "mhalf"], [("rsA", t)])
            TS("dve", xn[b][:], xin, rsA[:, t:t + 1], 32.0, ALU.mult, ALU.mult, [("xs", xb), ("rsA", t)], [("xn", b)])
            pT = pbb[b].rearrange("p (k t) -> p k t", k=8)
            for kc in range(8):
                TR(pT[:, kc, :], xn[b][:, kc * 128:(kc + 1) * 128], [("xn", b)], [("pb", b)])
            TT("dve", hT[:, :, t * 128:(t + 1) * 128], pT, gpre_bc, ALU.mult, [("pb", b), "gpreT"], [("hT", t)])
    UA.o = UA_mark
    P.barrier()

    hT_all = [("hT", t) for t in range(NT)]

    import os

    class Step:
        __slots__ = ("pre", "s", "pv", "post")

        def __init__(self, s, pv, pre=None, post=None):
            self.s, self.pv, self.pre, self.post = s, pv, pre, post

    def run_steps(steps, L=2):
        n = len(steps)
        nxt = 0
        for k in range(n):
            while nxt < n and nxt <= k + L:
                st = steps[nxt]
                if st.pre is not None:
                    if nxt > k:
                        break
                    st.pre()
                st.s()
                nxt += 1
            steps[k].pv()
            if steps[k].post:
                steps[k].post()

    SBANKS = (3, 4, 1)
    sctr = [0]

    PA0 = Arena(nc, PC_OFF, HT_OFF)
    gF = PA0.a("gF", [128, D], F32)
    xs2 = [PA0.a("xs2_%d" % i, [128, 2, D], F32) for i in range(2)]
    st3 = PA0.a("st3", [128, 6, 64], F32)
    PA = Arena(nc, HT_OFF, TOP)
    Wg = PA.a("Wg", [128, 8, DFF], BF16)
    Wu = PA.a("Wu", [128, 8, DFF], BF16)
    assert PA.o <= HT_OFF + 65536 + 31424

    n_diff = int(os.environ.get("DBG_HEADS", 4)) if stage != "none" else 0
    n_chunks = int(os.environ.get("DBG_CH", 16))
    UA_mark = UA.o
    Wd_ = [UA.a("Wdh%d" % i, [128, 8, 384], BF16) for i in range(2)]
    QK = UA.a("QK", [128, 3, S], BF16)
    Vh = UA.a("Vh", [128, NT, 130], BF16)
    AOd = UA.a("AOd", [128, NT, 128], BF16)
    qk = [UA.a("qk%d" % i, [128, 256], F32) for i in range(2)]
    rp = [UA.a("rp%d" % i, [128, 256], BF16) for i in range(2)]
    tmpd = [UA.a("tmpd%d" % i, [128, 2, 128], F32) for i in range(2)]
    tmpp = [UA.a("tmpp%d" % i, [128, 2, 128], F32) for i in range(2)]
    pTs = [UA.a("pTs%d" % i, [128, 2, 256], BF16) for i in range(4)]
    rcd = UA.a("rcd", [128, 16, 4], F32)
    cf1 = UA.a("cf1", [128, 16, 2], F32)
    asb = [UA.a("asb%d" % i, [128, 128], F32) for i in range(2)]
    a2sb = [UA.a("a2sb%d" % i, [128, 128], F32) for i in range(2)]
    ssd = UA.a("ssd", [128, 4 * NT], F32)
    rsd = UA.a("rsd", [128, 4 * NT], F32)
    if n_diff:
        MS("dve", Vh[:, :, 128:129], 1.0, ["Vones"])
        MS("pool", QK[64:128, 0, :], 0.0, ["Qz0"])
        MS("pool", QK[0:64, 2, :], 0.0, ["Qz1"])
        P.dma("pool", "wdh0", Wd_[0][:], w_in_u[:, :, 0:384], (), [("Wdh", 0)])

    def diff_inproj(h):
        wb = h % 2
        if h + 1 < n_diff:
            P.dma("pool", "wdh%d" % ((h + 1) % 2), Wd_[(h + 1) % 2][:], w_in_u[:, :, (h + 1) * 384:(h + 2) * 384], (),
                  [("Wdh", (h + 1) % 2)])
        def dfront(t):
            tb = t % 2
            pin = pb[tb]
            for kc in range(8):
                MM(pin[:, 0:384], hT[:, kc, t * 128:(t + 1) * 128], Wd_[wb][:, kc, :], kc == 0, kc == 7,
                   [("hT", t), ("Wdh", wb)], [("pb", tb)])
            ACT(qk[tb][:], pin[:, 0:256], AF.Copy, [("pb", tb)], [("qk", tb)])
            ACT(Vh[:, t, 0:128], pin[:, 256:384], AF.Copy, [("pb", tb)], [("Vh", t)])
            v4 = qk[tb][:, :].rearrange("p (a h d) -> p a h d", a=4, h=2)
            r4 = rp[tb][:, :].rearrange("p (a h d) -> p a h d", a=4, h=2)
            cb = cosT[:, t, :].unsqueeze(1).to_broadcast([128, 4, 32])
            sb = sinT[:, t, :].unsqueeze(1).to_broadcast([128, 4, 32])
            td = tmpd[tb][:, :, :].rearrange("p a (h d) -> p a h d", h=4)
            tp = tmpp[tb][:, :, :].rearrange("p a (h d) -> p a h d", h=4)
            TT("dve", td[:, 0], v4[:, :, 0, :], cb, ALU.mult, [("qk", tb), "cst"], [("td0", tb)])
            TT("dve", td[:, 1], v4[:, :, 1, :], sb, ALU.mult, [("qk", tb), "cst"], [("td1", tb)])
            TT("dve", r4[:, :, 0, :], td[:, 0], td[:, 1], ALU.subtract, [("td0", tb), ("td1", tb)], [("rpA", tb)])
            TT("pool", tp[:, 0], v4[:, :, 1, :], cb, ALU.mult, [("qk", tb), "cst"], [("tp0", tb)])
            TT("pool", tp[:, 1], v4[:, :, 0, :], sb, ALU.mult, [("qk", tb), "cst"], [("tp1", tb)])
            TT("pool", r4[:, :, 1, :], tp[:, 0], tp[:, 1], ALU.add, [("tp0", tb), ("tp1", tb)], [("rpB", tb)])

        def dback(t):
            tb = t % 2
            pT = pbb[2].rearrange("p (k t) -> p k t", k=8)
            TR(pT[:, 0, :], rp[tb][:, 0:128], [("rpA", tb), ("rpB", tb)], [("pb", 2)])
            TR(pT[:, 1, :], rp[tb][:, 128:256], [("rpA", tb), ("rpB", tb)], [("pb", 2)])
            CP("act", QK[0:64, 0, t * 128:(t + 1) * 128], pT[0:64, 0, :], [("pb", 2)], [("QKa", t)])
            CP("act", QK[64:128, 2, t * 128:(t + 1) * 128], pT[64:128, 0, :], [("pb", 2)], [("QKb", t)])
            CP("act", QK[:, 1, t * 128:(t + 1) * 128], pT[:, 1, :], [("pb", 2)], [("QK", t)])

        dfront(0)
        for t in range(NT):
            if t + 1 < NT:
                dfront(t + 1)
            dback(t)

    def mk_diff_step(h, c, kt):
        j = kt - 2 * c
        k = sctr[0]
        sctr[0] += 1
        sbk, pbuf = SBANKS[k % 3], k % 4
        obank = (5, 6) if c % 2 == 0 else (7, 2)
        O = [pb[obank[m]][:, 0:258].rearrange("p (j e) -> p j e", e=129) for m in range(2)]
        Sv = pb[sbk][:, :].rearrange("p (m q) -> p m q", m=2)
        qres = [("QKa", 2 * c), ("QKa", 2 * c + 1), ("QKb", 2 * c), ("QKb", 2 * c + 1), "Qz0", "Qz1"]

        def s_():
            kT = QK[:, 1, kt * 128:(kt + 1) * 128]
            for m in range(2):
                qp = 2 * m
                if j < 0:
                    MM(Sv[:, m, :], kT, QK[:, qp, c * 256:(c + 1) * 256], True, True, [("QK", kt)] + qres, [("pb", sbk)])
                else:
                    MM(Sv[:, m, 128 * j:128 * j + 128], ident[:], maskC[:, 0:128], True, False, ["ident", "maskC"], [("pb", sbk)])
                    MM(Sv[:, m, 128 * j:128 * j + 128], kT, QK[:, qp, c * 256 + 128 * j:c * 256 + 128 * j + 128], False, True,
                       [("QK", kt)] + qres, [("pb", sbk)])
                    if j == 0:
                        MM(Sv[:, m, 128:256], kT, QK[:, qp, c * 256 + 128:c * 256 + 256], True, True,
                           [("QK", kt)] + qres, [("pb", sbk)])
            q0 = 128 * j if j > 0 else 0
            ACT(pTs[pbuf][:, :, q0:256], Sv[:, :, q0:256], AF.Exp, [("pb", sbk)], [("pTs", pbuf)], scale=0.125)

        def pv_():
            for m in range(2):
                for jj in range(max(j, 0), 2):
                    MM(O[m][:, jj, :], pTs[pbuf][:, m, 128 * jj:128 * jj + 128], Vh[:, kt, 0:129],
                       kt == 0 and jj == 0, kt == 2 * c + jj, [("pTs", pbuf), ("Vh", kt), "Vones"], [("pb", obank[m])], sg=True)

        def post_():
            for m in range(2):
                RECIP(rcd[:, c, 2 * m:2 * m + 2], O[m][:, :, 128], [("pb", obank[m])], [("rcd", c, m)])
            TS1("dve", cf1[:, c, :], rcd[:, c, 2:4], neglam[:, 0:1], ALU.mult, [("rcd", c, 1), "neglam"], [("cf1", c)])
            for jj in range(2):
                t = 2 * c + jj
                col = h * NT + t
                ab = t % 2
                TS1("dve", asb[ab][:], O[0][:, jj, 0:128], rcd[:, c, jj:jj + 1], ALU.mult,
                    [("pb", obank[0]), ("rcd", c, 0)], [("asb", ab)])
                STT(a2sb[ab][:], O[1][:, jj, 0:128], cf1[:, c, jj:jj + 1], asb[ab][:], ALU.mult, ALU.add,
                    [("pb", obank[1]), ("cf1", c), ("asb", ab)], [("a2sb", ab)])
                jo, jr = junk_slot(128)
                ACT(jo, a2sb[ab][:], AF.Square, [("a2sb", ab)], [("ssd", col)] + jr, accum_out=ssd[:, col:col + 1])
                TS1("pool", rsd[:, col:col + 1], ssd[:, col:col + 1], 128.0 * EPS, ALU.add, [("ssd", col)], [("rsd0", col)])
                TT("pool", rsd[:, col:col + 1], rsd[:, col:col + 1], mhalf[:, 0:1], ALU.pow, [("rsd0", col), "mhalf"], [("rsd", col)])
                STT(AOd[:, t, :], a2sb[ab][:], rsd[:, col:col + 1], subtab[:], ALU.mult, ALU.mult,
                    [("a2sb", ab), ("rsd", col), "subtab"], [("AOd", t)])
            if c == n_chunks - 1:
                P.dma("sp", "aod", ao_v[:, :, h * 128:(h + 1) * 128], AOd[:], [("AOd", t) for t in range(NT)], [("ao_s", h)])

        pre = (lambda: diff_inproj(h)) if (c == 0 and kt == 0) else None
        post = post_ if kt == 2 * c + 1 else None
        return Step(s_, pv_, pre, post)

    run_steps([mk_diff_step(h, c, kt) for h in range(n_diff) for c in range(n_chunks) for kt in range(2 * c + 2)])
    UA.o = UA_mark

    n_nsa = 2 if stage in ("all", "nsa") else 0
    n_q = int(os.environ.get("DBG_NQ", NT))
    P.barrier()
    UA_mark = UA.o
    Wn = UA.a("Wn", [128, 8, 652], BF16)
    W1 = UA.a("W1", [64, 16, 256], BF16)
    qn = [UA.a("qn%d" % i, [128, 448], F32) for i in range(2)]
    rn = [UA.a("rn%d" % i, [128, 512], BF16) for i in range(2)]
    tnd = [UA.a("tnd%d" % i, [128, 2, 224], F32) for i in range(2)]
    tnp = [UA.a("tnp%d" % i, [128, 2, 224], F32) for i in range(2)]
    assert UA.o - (HT_OFF + 65536) >= 31424
    nqT = UA.a("nqT", [128, 4, S], BF16)
    KV4 = UA.a("KV4", [128, 4, S], BF16)
    VS = UA.a("VS", [128, NT, 66], BF16)
    VW = UA.a("VW", [128, NT, 66], BF16)
    gat = UA.a("gat", [128, NT, 12], F32)
    AOn = [UA.a("AOn%d" % i, [128, 2, 256], BF16) for i in range(2)]
    kcmpT = UA.a("kcmpT", [64, 256], BF16)
    hsb = UA.a("hsb", [128, 2, 256], BF16)
    bsb = UA.a("bsb", [128, 2], F32)
    gex = UA.a("gex", [128, 12], F32)
    pcs = [UA.a("pcs%d" % i, [128, 4, 128], BF16) for i in range(2)]
    pss = [UA.a("pss%d" % i, [128, 4, 128], BF16) for i in range(4)]
    NB = CA.a("NB", [128, 128], BF16)
    imp = CA.a("imp", [128, 64], F32)
    imp2 = CA.a("imp2", [128, 64], F32)
    mrp = CA.a("mrp", [128, 64], F32)
    m8 = UA.a("m8", [128, 16], F32)
    rc4 = [UA.a("rc4_%d" % i, [128, 4], F32) for i in range(2)]
    cco = [UA.a("cco%d" % i, [128, 3, 4], F32) for i in range(2)]
    acc = [UA.a("acc%d" % i, [128, 4, 64], F32) for i in range(2)]
    tac = UA.a("tac", [128, 4, 64], F32)
    if n_nsa:
        MS("dve", VS[:, :, 64:65], 1.0, ["VSones"])
        MS("dve", VW[:, :, 64:65], 1.0, ["VWones"])
        MS("dve", NB[:, 0:64], 0.0, ["NB0"])
        MS("dve", hsb[:, :, 255:256], 0.0, ["hsbpad"])
        Et = nqT[0:64, 0, :]
        MS("pool", Et, 1.0, ["Et"])
        ASEL(Et, Et, [[1, S]], ALU.is_ge, 0.0, 0, -64, ["Et"], ["Et"])
        ASEL(Et, Et, [[-1, S]], ALU.is_ge, 0.0, 63, 64, ["Et"], ["Et"])
        CP("dve", KV4[64:128, 1, :], Et, ["Et"] + [("nqT", t) for t in range(NT)], ["E"])
    kv_all = [("KV4", t) for t in range(NT)]
    vca_r = ["VCAv", "VCA1", "VCAov"]

    def nsa_pre(g):
        P.dma("pool", "wn", Wn[:], w_in_u[:, :, 1536 + g * 652:1536 + (g + 1) * 652], (), ["Wn"])
        def nfront(t):
            tb = t % 2
            pa, pbk, pbi = pb[tb], pb[(2, 5)[tb]], (2, 5)[tb]
            for kc in range(8):
                MM(pa[:, :], hT[:, kc, t * 128:(t + 1) * 128], Wn[:, kc, 0:512], kc == 0, kc == 7,
                   [("hT", t), "Wn"], [("pb", tb)])
            for kc in range(8):
                MM(pbk[:, 0:140], hT[:, kc, t * 128:(t + 1) * 128], Wn[:, kc, 512:652], kc == 0, kc == 7,
                   [("hT", t), "Wn"], [("pb", pbi)])
            ACT(qn[tb][:], pa[:, 0:448], AF.Copy, [("pb", tb)], [("qn", tb)])
            ACT(rn[tb][:, 448:512], pa[:, 448:512], AF.Copy, [("pb", tb)], [("rnV", tb)])
            ACT(VS[:, t, 0:64], pbk[:, 0:64], AF.Copy, [("pb", pbi)], [("VS", t)])
            ACT(VW[:, t, 0:64], pbk[:, 64:128], AF.Copy, [("pb", pbi)], [("VW", t)])
            ACT(gex[:], pbk[:, 128:140], AF.Exp, [("pb", pbi)], ["gex"], scale=-1.0)
            TS1("dve", gex[:], gex[:], 1.0, ALU.add, ["gex"], ["gex"])
            RECIP(gat[:, t, :], gex[:], ["gex"], [("gat", t)])
            v4 = qn[tb][:, :].rearrange("p (a h d) -> p a h d", a=7, h=2)
            r4 = rn[tb][:, 0:448].rearrange("p (a h d) -> p a h d", a=7, h=2)
            cb = cosT[:, t, :].unsqueeze(1).to_broadcast([128, 7, 32])
            sb = sinT[:, t, :].unsqueeze(1).to_broadcast([128, 7, 32])
            td = tnd[tb][:, :, :].rearrange("p a (h d) -> p a h d", h=7)
            tp = tnp[tb][:, :, :].rearrange("p a (h d) -> p a h d", h=7)
            TT("dve", td[:, 0], v4[:, :, 0, :], cb, ALU.mult, [("qn", tb), "cst"], [("nd0", tb)])
            TT("dve", td[:, 1], v4[:, :, 1, :], sb, ALU.mult, [("qn", tb), "cst"], [("nd1", tb)])
            TT("dve", r4[:, :, 0, :], td[:, 0], td[:, 1], ALU.subtract, [("nd0", tb), ("nd1", tb)], [("rnA", tb)])
            TT("pool", tp[:, 0], v4[:, :, 1, :], cb, ALU.mult, [("qn", tb), "cst"], [("np0", tb)])
            TT("pool", tp[:, 1], v4[:, :, 0, :], sb, ALU.mult, [("qn", tb), "cst"], [("np1", tb)])
            TT("pool", r4[:, :, 1, :], tp[:, 0], tp[:, 1], ALU.add, [("np0", tb), ("np1", tb)], [("rnB", tb)])

        def nback(t):
            tb = t % 2
            pT = pbb[3 + tb].rearrange("p (k t) -> p k t", k=8)
            for k8 in range(8):
                TR(pT[0:64, k8, :], rn[tb][:, k8 * 64:(k8 + 1) * 64], [("rnA", tb), ("rnB", tb), ("rnV", tb)], [("pb", 3 + tb)])
            CP("dve", nqT[0:64, :, t * 128:(t + 1) * 128], pT[0:64, 0:4, :], [("pb", 3 + tb)], [("nqT", t)])
            CP("dve", KV4[0:64, :, t * 128:(t + 1) * 128], pT[0:64, 4:8, :], [("pb", 3 + tb)], [("KV4", t)])

        nfront(0)
        for t in range(NT):
            if t + 1 < NT:
                nfront(t + 1)
            nback(t)

        for kvi in range(2):
            plane = 0 if kvi == 0 else 3
            hid = pb[3][:, :].rearrange("p (j n) -> p j n", j=2)
            for half in range(2):
                P.dma("pool", "w1", W1[:], cw1[kvi][:, half * 16:(half + 1) * 16, :], (), ["W1"])
                for jh in range(2):
                    for l16 in range(16):
                        l = half * 16 + l16
                        MM(hid[:, jh, 0:255], W1[:, l16, jh * 128:(jh + 1) * 128], KV4[0:64, plane, l:l + 16 * 254 + 1:16],
                           l == 0 and jh == 0, l == 31, ["W1"] + kv_all, [("pb", 3)], sg=True)
                for jh in range(2):
                    for l16 in range(16):
                        l = half * 16 + l16
                        MM(pb[4][:, jh:jh + 1], W1[:, l16, jh * 128:(jh + 1) * 128], posT[kvi][:, l:l + 1],
                           l == 0 and jh == 0, l == 31, ["W1", ("posT", kvi)], [("pb", 4)], sg=True)
            CP("dve", bsb[:], pb[4][:, 0:2], [("pb", 4)], ["bsb"])
            for jh in range(2):
                ACT(hsb[:, jh, 0:255], hid[:, jh, 0:255], AF.Silu, [("pb", 3), "bsb", "hsbpad"], [("hsb", jh)],
                    bias=bsb[:, jh:jh + 1])
            if kvi == 0:
                for jh in range(2):
                    MM(pb[5][0:64, 0:256], w2[0][:, jh, :], hsb[:, jh, :], jh == 0, jh == 1,
                       [("hsb", 0), ("hsb", 1), ("w2", 0)], [("pb", 5)])
                CP("dve", kcmpT[:], pb[5][0:64, 0:256], [("pb", 5)], ["kcmpT"])
            else:
                pv = pb[6][:, 0:128].rearrange("p (n d) -> p n d", n=2)
                for nt in range(2):
                    for jh in range(2):
                        MM(pv[:, nt, :], hsb[:, jh, nt * 128:(nt + 1) * 128], w2[1][:, jh, :], jh == 0, jh == 1,
                           [("hsb", 0), ("hsb", 1), ("w2", 1)], [("pb", 6)])
                CP("dve", VCA[:, :, 0:64], pv, [("pb", 6)], ["VCAv"])
        if g == n_nsa - 1 and stage == "all":
            for k in range(4):
                sl = slice(k * 704, (k + 1) * 704)
                P.dma("pool", "wg", Wg[:, :, sl], wg_d[:, :, sl], (), [("Wg", k)], after_all=True)
                P.dma("pool", "wu", Wu[:, :, sl], wu_d[:, :, sl], (), [("Wu", k)], after_all=True)

    def mk_cmp_step(g, i, nt, first, last):
        pi = i % 2
        k = sctr[0]
        sctr[0] += 1
        sbk = SBANKS[k % 3]
        pc = pss[k % 4]
        pcr = ("pss", k % 4)
        Sc = pb[sbk][:, :].rearrange("p (h q) -> p h q", h=4)
        q64 = nqT[0:64, :, i * 128:(i + 1) * 128]
        U = [pb[5 + b][:, 0:258].rearrange("p (j e) -> p j e", e=129) for b in range(2)]

        def s_():
            MM(Sc, kcmpT[:, nt * 128:(nt + 1) * 128], q64, True, True, ["kcmpT", ("nqT", i)], [("pb", sbk)])
            ACT(pc[:], Sc, AF.Exp, [("pb", sbk)], [pcr], scale=0.125)
            midx = None
            if nt == 0 and i <= 16:
                midx = i
            if nt == 1:
                midx = 17 + i - 16
            if midx is not None:
                TT("pool", pc[:], pc[:], maskcmp[:, midx, :].unsqueeze(1).to_broadcast([128, 4, 128]),
                   ALU.mult, [pcr, ("mcmp", midx)], [pcr])

        def pv_():
            for hh in range(4):
                MM(U[hh // 2][:, hh % 2, :], pc[:, hh, :], VCA[:, nt, 0:129], first and hh % 2 == 0, last,
                   [pcr] + vca_r, [("pb", 5 + hh // 2)], sg=True)

        def post_():
            for b in range(2):
                TS1("dve", rc4[pi][:, 2 * b:2 * b + 2], U[b][:, :, 64], 1e-30, ALU.add, [("pb", 5 + b)], [("rc4", pi)])
            RECIP(rc4[pi][:], rc4[pi][:], [("rc4", pi)], [("rc4", pi)])
            TS1("dve", imp[:], U[0][:, 0, 65:129], rc4[pi][:, 0:1], ALU.mult, [("pb", 5), ("rc4", pi)], ["imp"])
            for hh in range(1, 4):
                STT(imp[:], U[hh // 2][:, hh % 2, 65:129], rc4[pi][:, hh:hh + 1], imp[:], ALU.mult, ALU.add,
                    [("pb", 5 + hh // 2), ("rc4", pi), "imp"], ["imp"])
            c0 = 62 - 2 * i
            TT("dve", imp2[:], imp[:], cst[:, CAP0 + c0:CAP0 + c0 + 64], ALU.min, ["imp", "cst"], ["imp2"])
            TT("dve", imp2[:], imp2[:], cst[:, FLO0 + c0:FLO0 + c0 + 64], ALU.max, ["imp2", "cst"], ["imp2"])
            MS("dve", imp2[:, 0:1], 3e30, ["imp2"])
            P.op("dve", lambda e: e.max(out=m8[:, 0:8], in_=imp2[:]), ["imp2"], ["m8a"])
            P.op("dve", lambda e: e.match_replace(out=mrp[:], in_to_replace=m8[:, 0:8], in_values=imp2[:], imm_value=-3e30),
                 ["imp2", "m8a"], ["mrp"])
            P.op("dve", lambda e: e.max(out=m8[:, 8:16], in_=mrp[:]), ["mrp"], ["m8b"])
            TS("dve", NB[:, 64:128], imp2[:], m8[:, 15:16], NEG, ALU.is_lt, ALU.mult, ["imp2", "m8b"], ["NB"])
            pTn = pbb[0][:, 0:128]
            TR(pTn, NB[:, :], ["NB", "NB0"], [("pb", 0)])
            CP("act", nqT[64:128, :, i * 128:(i + 1) * 128], pTn[64:128, :].unsqueeze(1).to_broadcast([64, 4, 128]),
               [("pb", 0)], [("nqTb", i)])
            TT("dve", cco[pi][:, 0, :], rc4[pi][:], gat[:, i, 0:12:3], ALU.mult, [("rc4", pi), ("gat", i)], [("cco", pi, 0)])
            for b in range(2):
                TT("dve", acc[pi][:, 2 * b:2 * b + 2, :], U[b][:, :, 0:64],
                   cco[pi][:, 0, 2 * b:2 * b + 2].unsqueeze(2).to_broadcast([128, 2, 64]), ALU.mult,
                   [("pb", 5 + b), ("cco", pi, 0)], [("acc", pi)])

        return Step(s_, pv_, None, post_ if last else None)

    def mk_br_step(g, i, br, kt, kts):
        pi = i % 2
        k = sctr[0]
        sctr[0] += 1
        sbk, pbuf = SBANKS[k % 3], k % 4
        Ss = pb[sbk][:, :].rearrange("p (h q) -> p h q", h=4)
        q64 = nqT[0:64, :, i * 128:(i + 1) * 128]
        q128 = nqT[:, :, i * 128:(i + 1) * 128]
        if br == 2:
            obk, Vt, vres = 2, VW, "VW"
        else:
            obk, Vt, vres = 7, VS, "VS"
        Ob = pb[obk][:, 0:260].rearrange("p (h e) -> p h e", e=65)

        def s_():
            first = True
            if kt == i:
                MM(Ss, ident[:], maskC4, True, False, ["ident", "maskC"], [("pb", sbk)])
                first = False
            elif br == 2 and kt == i - 4:
                MM(Ss, ident[:], maskW4, True, False, ["ident", "maskW"], [("pb", sbk)])
                first = False
            if br == 2:
                MM(Ss, KV4[0:64, 2, kt * 128:(kt + 1) * 128], q64, first, True, [("KV4", kt), ("nqT", i)], [("pb", sbk)])
            else:
                MM(Ss, KV4[:, 1, kt * 128:(kt + 1) * 128], q128, first, True,
                   [("KV4", kt), "E", ("nqTb", i), ("nqT", i)], [("pb", sbk)])
            ACT(pss[pbuf][:], Ss, AF.Exp, [("pb", sbk)], [("pss", pbuf)], scale=0.125)

        def pv_():
            for hh in range(4):
                MM(Ob[:, hh, :], pss[pbuf][:, hh, :], Vt[:, kt, 0:65], kt == kts[0] and hh == 0, kt == kts[-1],
                   [("pss", pbuf), (vres, kt), vres + "ones"], [("pb", obk)], sg=True)

        def post_():
            RECIP(cco[pi][:, br, :], Ob[:, :, 64], [("pb", obk)], [("cco", pi, br)])
            TT("dve", cco[pi][:, br, :], cco[pi][:, br, :], gat[:, i, br:12:3], ALU.mult, [("cco", pi, br), ("gat", i)], [("cco", pi, br)])
            TT("dve", tac[:], Ob[:, :, 0:64], cco[pi][:, br, :].unsqueeze(2).to_broadcast([128, 4, 64]), ALU.mult,
               [("pb", obk), ("cco", pi, br)], ["tac"])
            TT("pool", acc[pi][:], acc[pi][:], tac[:], ALU.add, ["tac", ("acc", pi)], [("acc", pi)])
            if br == 1:
                ab = (i // 2) % 2
                CP("pool", AOn[ab][:, i % 2, :], acc[pi][:, :, :].rearrange("p h d -> p (h d)"), [("acc", pi)], [("AOn", ab)])
                if i % 2 == 1:
                    P.dma("act", "aon%d" % ab, ao_v[:, i - 1:i + 1, 512 + g * 256:512 + (g + 1) * 256], AOn[ab][:],
                          [("AOn", ab)], [("ao_n", g, i)])

        return Step(s_, pv_, None, post_ if kt == kts[-1] else None)

    def cmp_steps(g, i):
        nts = [0] if i < 16 else [0, 1]
        return [mk_cmp_step(g, i, nt, nt == nts[0], nt == nts[-1]) for nt in nts]

    nsa_steps = []
    for g in range(n_nsa):
        gsteps = []
        if n_q:
            gsteps += cmp_steps(g, 0)
        for i in range(n_q):
            kts_w = list(range(max(0, i - 4), i + 1))
            gsteps += [mk_br_step(g, i, 2, kt, kts_w) for kt in kts_w]
            if i + 1 < n_q:
                gsteps += cmp_steps(g, i + 1)
            kts_s = list(range(0, i + 1))
            gsteps += [mk_br_step(g, i, 1, kt, kts_s) for kt in kts_s]
        if gsteps:
            gsteps[0].pre = (lambda g=g: nsa_pre(g))
        else:
            nsa_pre(g)
        nsa_steps += gsteps
    run_steps(nsa_steps)
    UA.o = UA_mark

    if stage in ("diff", "nsa", "none"):
        st = P.emit()
        return nc, st

    P.barrier()
    Wdn = PA.a("Wdn", [128, FC, D], BF16)
    tm = PA.a("tm", [128, 2, D], F32)
    h2n = PA.a("h2n", [128, D], BF16)
    h2T_off = (PA.o + 31) // 32 * 32
    h2T = PA.a("h2T", [128, 8, 512], BF16)
    ov_base = PA.o
    Wo = PA.a("Wo", [128, 8, D], BF16)
    gA = PA.a("gA", [128, D], F32)
    aos = [PA.a("aos%d" % i, [128, 2, D], BF16) for i in range(2)]
    aoT = [nc.alloc_sbuf_tensor_at("aoT%d" % i, [128, 8, 128], BF16, offset=h2T_off + 2048 * i) for i in range(2)]
    PA.o = ov_base
    sg = [PA.a("sg%d" % i, [128, 512], F32) for i in range(2)]
    actT = PA.a("actT", [128, FC, 512], BF16)
    P.dma("pool", "wo", Wo[:], w_out_r, (), ["Wo"])
    P.dma("sp", "gv", gA[:], gpostA_d[0].partition_broadcast(128), (), ["gA0"])
    P.dma("sp", "gv", gF[:], gpostF_d[0].partition_broadcast(128), (), ["gF0"])
    TS1("pool", gA[:], gA[:], 32.0, ALU.mult, ["gA0"], ["gA"])
    TS1("pool", gF[:], gF[:], 32.0, ALU.mult, ["gF0"], ["gF"])
    for k in range(2):
        sl = slice(k * 11, (k + 1) * 11)
        P.dma("pool", "wd", Wdn[:, sl, :], wd_d[:, sl, :], (), [("Wdn", k)])
    g2_bc = g2T[:, :].unsqueeze(2).to_broadcast([128, 8, 128])
    out_v = out.rearrange("(t p) d -> p t d", p=128)

    def norm_scale(col, ssrc_ap, res_in, tag):
        TS1("pool", st3[:, 1, col:col + 1], ssrc_ap, 1024.0 * EPS, ALU.add, res_in, [(tag + "0", col)])
        TT("pool", st3[:, 1, col:col + 1], st3[:, 1, col:col + 1], mhalf[:, 0:1], ALU.pow, [(tag + "0", col), "mhalf"], [(tag, col)])

    def c0_front(t):
        tp, jx = t // 2, t % 2
        xb = tp % 2
        b = t % 2
        mb = (2, 4)[b]
        if jx == 0:
            P.dma("sp", "aos%d" % xb, aos[xb][:], ao_v[:, 2 * tp:2 * tp + 2, :], (), [("aos", xb)])
            P.dma("act", "xs2%d" % xb, xs2[xb][:], x_v[:, 2 * tp:2 * tp + 2, :], (), [("xs2", xb)])
        pT = pbb[b].rearrange("p (k t) -> p k t", k=8)
        for kc in range(8):
            TR(pT[:, kc, :], aos[xb][:, jx, kc * 128:(kc + 1) * 128], [("aos", xb)], [("pb", b)])
        CP("act", aoT[b][:], pT, [("pb", b)], [("aoT", b)])
        for nh in range(2):
            for kc in range(8):
                MM(pb[mb + nh][:, :], aoT[b][:, kc, :], Wo[:, kc, nh * 512:(nh + 1) * 512], kc == 0, kc == 7,
                   [("aoT", b), "Wo"], [("pb", mb + nh)])

    def c0_back(t):
        tp, jx = t // 2, t % 2
        xb = tp % 2
        mb = (2, 4)[t % 2]
        for nh in range(2):
            jo, jr = junk_slot(512)
            ACT(jo, pb[mb + nh][:, :], AF.Square, [("pb", mb + nh)], [("ssm", t, nh)] + jr,
                accum_out=st3[:, 2 + nh, t:t + 1])
        TT("pool", st3[:, 0, t:t + 1], st3[:, 2, t:t + 1], st3[:, 3, t:t + 1], ALU.add, [("ssm", t, 0), ("ssm", t, 1)], [("ssmt", t)])
        norm_scale(t, st3[:, 0, t:t + 1], [("ssmt", t)], "rsm")
        for nh in range(2):
            STT(tm[:, jx, nh * 512:(nh + 1) * 512], pb[mb + nh][:, :], st3[:, 1, t:t + 1], gA[:, nh * 512:(nh + 1) * 512],
                ALU.mult, ALU.mult, [("pb", mb + nh), ("rsm", t), "gA"], [("tm", jx, nh)])
        if jx == 1:
            TT("pool", tm[:], tm[:], xs2[xb][:], ALU.add, [("tm", 0, 0), ("tm", 0, 1), ("tm", 1, 0), ("tm", 1, 1), ("xs2", xb)],
               ["x1", ("tm", 0, 0), ("tm", 0, 1), ("tm", 1, 0), ("tm", 1, 1)])
            P.dma("sp", "x1o", out_v[:, 2 * tp:2 * tp + 2, :], tm[:], ["x1", ("tm", 0, 0), ("tm", 0, 1), ("tm", 1, 0), ("tm", 1, 1)], [("out", tp)])

    c0_front(0)
    for t in range(NT):
        if t + 1 < NT:
            c0_front(t + 1)
        c0_back(t)

    P.barrier(keep_chans=("wd",), keep_res=[("Wdn", 0), ("Wdn", 1)])
    wg_r = [("Wg", k) for k in range(4)]
    wu_r = [("Wu", k) for k in range(4)]
    wd_r = [("Wdn", k) for k in range(2)]
    tm_all = [("tm", 0, 0), ("tm", 0, 1), ("tm", 1, 0), ("tm", 1, 1)]
    for blk in range(8):
        for tt in range(4):
            t = blk * 4 + tt
            b = t % 2
            tp, jx = t // 2, t % 2
            xb = tp % 2
            if jx == 0:
                P.dma("sp", "xs2%d" % xb, xs2[xb][:], out_v[:, 2 * tp:2 * tp + 2, :], [("out", tp)], [("xs2", xb)])
            xin = xs2[xb][:, jx, :]
            jo, jr = junk_slot(1024)
            ACT(jo, xin, AF.Square, [("xs2", xb)], [("ss2", t)] + jr, accum_out=st3[:, 4, t:t + 1])
            TS1("pool", st3[:, 5, t:t + 1], st3[:, 4, t:t + 1], 1024.0 * EPS, ALU.add, [("ss2", t)], [("rs20", t)])
            TT("pool", st3[:, 5, t:t + 1], st3[:, 5, t:t + 1], mhalf[:, 0:1], ALU.pow, [("rs20", t), "mhalf"], [("rs2", t)])
            TS("dve", h2n[:], xin, st3[:, 5, t:t + 1], 32.0, ALU.mult, ALU.mult, [("xs2", xb), ("rs2", t)], ["h2n"])
            pT = pbb[b].rearrange("p (k t) -> p k t", k=8)
            for kc in range(8):
                TR(pT[:, kc, :], h2n[:, kc * 128:(kc + 1) * 128], ["h2n"], [("pb", b)])
            TT("dve", h2T[:, :, tt * 128:(tt + 1) * 128], pT, g2_bc, ALU.mult, [("pb", b), "g2T"], [("h2T", tt)])
        h2r = [("h2T", tt) for tt in range(4)]
        for fc in range(FC):
            p2 = fc % 2
            gb, ub = (4, 5) if p2 == 0 else (6, 7)
            for kc in range(8):
                MM(pb[gb][:, :], Wg[:, kc, fc * 128:(fc + 1) * 128], h2T[:, kc, :], kc == 0, kc == 7, wg_r + h2r, [("pb", gb)])
            for kc in range(8):
                MM(pb[ub][:, :], Wu[:, kc, fc * 128:(fc + 1) * 128], h2T[:, kc, :], kc == 0, kc == 7, wu_r + h2r, [("pb", ub)])
            ACT(sg[p2][:], pb[gb][:, :], AF.Silu, [("pb", gb)], [("sg", p2)])
            TT("dve", actT[:, fc, :], sg[p2][:], pb[ub][:, :], ALU.mult, [("sg", p2), ("pb", ub)], [("actT", fc)])
        ar = [("actT", fc) for fc in range(FC)]
        for tt in range(4):
            t = blk * 4 + tt
            tp, jx = t // 2, t % 2
            for nh in range(2):
                for fc in range(FC):
                    MM(pb[2 + nh][:, :], actT[:, fc, tt * 128:(tt + 1) * 128], Wdn[:, fc, nh * 512:(nh + 1) * 512],
                       fc == 0, fc == FC - 1, ar + wd_r, [("pb", 2 + nh)])
            for nh in range(2):
                jo, jr = junk_slot(512)
                ACT(jo, pb[2 + nh][:, :], AF.Square, [("pb", 2 + nh)], [("ssy", t, nh)] + jr,
                    accum_out=st3[:, 2 + nh, 32 + t:33 + t])
            TT("pool", st3[:, 0, 32 + t:33 + t], st3[:, 2, 32 + t:33 + t], st3[:, 3, 32 + t:33 + t], ALU.add,
               [("ssy", t, 0), ("ssy", t, 1)], [("ssyt", t)])
            norm_scale(32 + t, st3[:, 0, 32 + t:33 + t], [("ssyt", t)], "rsy")
            for nh in range(2):
                STT(tm[:, jx, nh * 512:(nh + 1) * 512], pb[2 + nh][:, :], st3[:, 1, 32 + t:33 + t], gF[:, nh * 512:(nh + 1) * 512],
                    ALU.mult, ALU.mult, [("pb", 2 + nh), ("rsy", 32 + t), "gF"], [("tm", jx, nh)])
            if jx == 1:
                P.dma("pool", "acc", out_v[:, 2 * tp:2 * tp + 2, :], tm[:], tm_all + [("out", tp)],
                      [("out", tp)] + tm_all, accum_op=ALU.add)
    st = P.emit()
    return nc, st


def _consts():
    c = np.zeros((128, NCONST), np.float32)
    inv = 1.0 / (10000.0 ** (np.arange(0, 64, 2, dtype=np.float32) / 64.0))
    pos = (np.arange(NT)[None, :, None] * 128 + np.arange(128)[:, None, None]).astype(np.float32)
    ang = pos * inv[None, None, :]
    c[:, 0:1024] = np.cos(ang).astype(np.float32).reshape(128, 1024)
    c[:, 1024:2048] = np.sin(ang).astype(np.float32).reshape(128, 1024)
    hi = (np.arange(128) >= 64).astype(np.int64)[:, None]
    m = np.arange(128)[None, :] - 62
    c[:, 2048:2176] = np.where(m <= hi, 1e30, -1e30)
    c[:, 2176:2304] = np.where(m == hi, 2e30, np.where(m == hi - 1, 1e30, -3e30))
    n = np.arange(256)[:, None]
    j = np.arange(64)[None, :]
    ov = np.clip(np.minimum(16 * n + 32, 64 * j + 64) - np.maximum(16 * n, 64 * j), 0, None) / 32.0
    ov[255] = 0.0
    c[:, 2304:2432] = ov.reshape(2, 128, 64).transpose(1, 0, 2).reshape(128, 128)
    return c


def _layout(inp):
    f = lambda k: np.asarray(inp[k], np.float32)[0]
    w_in = f("w_in")
    cols = []
    for h in range(4):
        cols += [np.arange(h * 128, h * 128 + 128), 512 + np.arange(h * 128, h * 128 + 128), 1024 + np.arange(h * 128, h * 128 + 128)]
    for g in range(2):
        cols += [1536 + g * 256 + np.arange(256)]
        for base in (2048, 2304, 2560, 2176, 2432, 2688):
            cols += [base + g * 64 + np.arange(64)]
        cols += [2816 + g * 12 + np.arange(12)]
    cols = np.concatenate(cols)
    r8 = lambda w: np.ascontiguousarray(w.reshape(8, 128, -1).transpose(1, 0, 2))
    m = {
        "consts": _consts(),
        "gpreT": np.ascontiguousarray(f("attn_pre_norm").reshape(8, 128).T),
        "g2T": np.ascontiguousarray(f("ffn_pre_norm").reshape(8, 128).T),
        "w_in_u": r8(w_in[:, cols]),
        "lam4": np.stack([f("lambda_q1"), f("lambda_k1"), f("lambda_q2"), f("lambda_k2")]),
        "subln": f("diff_subln")[None, :],
        "kw1": np.ascontiguousarray(f("k_cmp_w1").reshape(32, 64, 256).transpose(1, 0, 2)),
        "vw1": np.ascontiguousarray(f("v_cmp_w1").reshape(32, 64, 256).transpose(1, 0, 2)),
        "kposT": np.ascontiguousarray(f("k_cmp_pos").T),
        "vposT": np.ascontiguousarray(f("v_cmp_pos").T),
        "kw2": np.ascontiguousarray(f("k_cmp_w2").reshape(2, 128, 64).transpose(1, 0, 2)),
        "vw2": np.ascontiguousarray(f("v_cmp_w2").reshape(2, 128, 64).transpose(1, 0, 2)),
        "w_out_r": r8(f("w_out")),
        "gpostA": f("attn_post_norm")[None, :],
        "gpostF": f("ffn_post_norm")[None, :],
        "wg": r8(f("w_gate")),
        "wu": r8(f("w_up")),
        "wd": np.ascontiguousarray(f("w_down").reshape(FC, 128, D).transpose(1, 0, 2)),
    }
    return m


_CACHE = {}


def kernel(**inputs):
    if "nc" not in _CACHE:
        _CACHE["nc"] = build()[0]
    nc = _CACHE["nc"]
    shared = _layout(inputs)
    xs = np.asarray(inputs["x"], np.float32)
    in_maps = [dict(shared, x=np.ascontiguousarray(xs[b])) for b in range(8)]
    res = run_bass_kernel_spmd(nc, in_maps, core_ids=list(range(8)))
    return np.stack([np.asarray(r["out"], np.float32) for r in res.results], axis=0)
```

```python
import numpy as np
import concourse.bass as bass
import concourse.mybir as mybir
from concourse.bass_utils import run_bass_kernel_spmd

F32 = mybir.dt.float32
BF16 = mybir.dt.bfloat16
ALU = mybir.AluOpType
AF = mybir.ActivationFunctionType
AX = mybir.AxisListType

S, D, NT, KC = 4096, 1024, 32, 8
DFF, FC = 2816, 22
NEG = -30000.0
EPS = 1e-6
LAM_INIT = 0.2
NCONST = 2432
ENGS = ("pe", "act", "dve", "pool", "sp")


class _Op:
    __slots__ = ("eng", "fn", "deps", "sig", "ticket", "chan", "is_dma")

    def __init__(self, eng, fn, deps, chan=None):
        self.eng, self.fn, self.deps = eng, fn, deps
        self.sig, self.ticket, self.chan = False, 0, chan
        self.is_dma = chan is not None


class Prog:
    def __init__(self, nc):
        self.nc = nc
        self.ops = []
        self.last_w = {}
        self.readers = {}
        self.chan_last = {}
        self.eng_last = {}

    def _deps(self, reads, writes):
        d = set()
        for r in reads:
            w = self.last_w.get(r)
            if w is not None:
                d.add(w)
        for w_ in writes:
            w = self.last_w.get(w_)
            if w is not None:
                d.add(w)
            d.update(self.readers.get(w_, ()))
        return d

    def _commit(self, idx, reads, writes):
        for r in reads:
            self.readers.setdefault(r, []).append(idx)
        for w_ in writes:
            self.last_w[w_] = idx
            self.readers[w_] = []

    def op(self, eng, fn, reads=(), writes=()):
        d = self._deps(reads, writes)
        idx = len(self.ops)
        if eng == "pe":
            d = {x for x in d if self.ops[x].eng != "pe" or self.ops[x].is_dma}
        self.ops.append(_Op(eng, fn, d))
        self._commit(idx, reads, writes)
        self.eng_last[eng] = idx
        return idx

    def dma(self, eng, chan, out, in_, reads=(), writes=(), after_all=False, **kw):
        d = self._deps(reads, writes)
        if after_all:
            d.update(self.eng_last.values())
        prev = self.chan_last.get(chan)
        if prev is not None:
            d.add(prev)
        idx = len(self.ops)
        self.ops.append(_Op(eng, lambda e: e.dma_start(out=out, in_=in_, **kw), d, chan=chan))
        self.chan_last[chan] = idx
        self._commit(idx, reads, writes)
        return idx

    def barrier(self, keep_chans=(), keep_res=()):
        deps = set(self.eng_last.values()) | {v for c, v in self.chan_last.items() if c not in keep_chans}
        kept = {r: self.last_w[r] for r in keep_res if r in self.last_w}
        for eng in ENGS:
            idx = len(self.ops)
            self.ops.append(_Op(eng, None, set(deps)))
            self.eng_last[eng] = idx
        self.last_w.clear()
        self.readers.clear()
        self.last_w.update(kept)

    def emit(self):
        nc, ops = self.nc, self.ops
        for o in ops:
            best = {}
            for d in o.deps:
                od = ops[d]
                if od.is_dma:
                    od.sig = True
                elif od.fn is not None and best.get(od.eng, -1) < d:
                    best[od.eng] = d
            for d in best.values():
                ops[d].sig = True
        cnt = {e: 0 for e in ENGS}
        chan_cnt = {}
        for o in ops:
            if o.is_dma:
                o.sig = True
                chan_cnt[o.chan] = chan_cnt.get(o.chan, 0) + 16
                o.ticket = chan_cnt[o.chan]
            elif o.fn is None:
                o.sig = False
            elif o.sig:
                cnt[o.eng] += 1
                o.ticket = cnt[o.eng]
        sems = {e: nc.alloc_semaphore("s_" + e) for e in ENGS if e != "sp"}
        csems = {c: nc.alloc_semaphore("c_" + str(c)) for c in chan_cnt}
        per_eng = {e: [] for e in ENGS}
        for i, o in enumerate(ops):
            per_eng[o.eng].append(i)

        def run(engname):
            def f(e):
                waited = {}
                for i in per_eng[engname]:
                    o = ops[i]
                    need = {}
                    for d in o.deps:
                        od = ops[d]
                        if od.fn is None and not od.is_dma:
                            continue
                        key = ("c", od.chan) if od.is_dma else ("e", od.eng)
                        if need.get(key, 0) < od.ticket:
                            need[key] = od.ticket
                    for key, val in need.items():
                        if waited.get(key, 0) >= val:
                            continue
                        waited[key] = val
                        e.wait_ge(csems[key[1]] if key[0] == "c" else sems[key[1]], val)
                    if o.fn is None:
                        continue
                    ins = o.fn(e)
                    if o.is_dma:
                        ins.then_inc(csems[o.chan], 16)
                    elif o.sig:
                        ins.then_inc(sems[o.eng], 1)
                if engname == "sp":
                    for c, v in chan_cnt.items():
                        e.wait_ge(csems[c], v)
                    for en in ("pe", "act", "dve", "pool"):
                        if cnt[en]:
                            e.wait_ge(sems[en], cnt[en])
            return f

        with nc.Block() as block:
            block.tensor(run("pe"))
            block.scalar(run("act"))
            block.vector(run("dve"))
            block.gpsimd(run("pool"))
            block.sync(run("sp"))
        return {e: len(per_eng[e]) for e in ENGS}


class Arena:
    def __init__(self, nc, base, limit):
        self.nc, self.o, self.limit = nc, base, limit

    def a(self, name, shape, dt):
        nb = 2 if dt == BF16 else 4
        size = int(np.prod(shape[1:])) * nb
        off = (self.o + 31) // 32 * 32
        self.o = off + size
        assert self.o <= self.limit, (name, self.o, self.limit)
        return self.nc.alloc_sbuf_tensor_at(name, list(shape), dt, offset=off)


def build(stage="all", dbg=False):
    nc = bass.Bass("TRN2", target_bir_lowering=False)
    P = Prog(nc)

    def din(name, shape, dt=F32):
        return nc.dram_tensor(name, list(shape), dt, kind="ExternalInput").ap()

    x = din("x", [S, D])
    consts = din("consts", [128, NCONST])
    gpreT_d = din("gpreT", [128, 8])
    g2T_d = din("g2T", [128, 8])
    w_in_u = din("w_in_u", [128, 8, 2840])
    lam4 = din("lam4", [4, 64])
    subln = din("subln", [1, 128])
    cw1 = [din("kw1", [64, 32, 256]), din("vw1", [64, 32, 256])]
    cposT = [din("kposT", [64, 32]), din("vposT", [64, 32])]
    cw2 = [din("kw2", [128, 2, 64]), din("vw2", [128, 2, 64])]
    w_out_r = din("w_out_r", [128, 8, D])
    gpostA_d = din("gpostA", [1, D])
    gpostF_d = din("gpostF", [1, D])
    wg_d = din("wg", [128, 8, DFF])
    wu_d = din("wu", [128, 8, DFF])
    wd_d = din("wd", [128, FC, D])
    out = nc.dram_tensor("out", [S, D], F32, kind="ExternalOutput").ap()
    ao_s = nc.dram_tensor("ao_s", [S, D], BF16, kind="ExternalOutput" if dbg else "Internal").ap()
    ao_v = ao_s.rearrange("(t p) c -> p t c", p=128)

    BASE = 16512
    TOP = 229344
    CA0 = Arena(nc, BASE, BASE + 3 * 1024)
    PC_OFF = BASE + 3 * 1024
    CA = Arena(nc, PC_OFF, BASE + 26 * 1024)
    HT_OFF = BASE + 26 * 1024
    UA = Arena(nc, HT_OFF + 65536, TOP)
    hT = nc.alloc_sbuf_tensor_at("hT", [128, 8, S], BF16, offset=HT_OFF)

    pb = [nc.alloc_psum_tensor("pb%d" % i, [128, 512], F32) for i in range(8)]
    pbb = [p[:, :].bitcast(BF16) for p in pb]

    def MM(o, lhsT, rhs, start, stop, r, w, sg=False):
        if sg:
            P.op("pe", lambda e: e.matmul(o, lhsT=lhsT, rhs=rhs, start=start, stop=stop, skip_group_check=True), r, w)
        else:
            P.op("pe", lambda e: e.matmul(o, lhsT=lhsT, rhs=rhs, start=start, stop=stop), r, w)

    def ACT(o, i, func, r, w, **kw):
        P.op("act", lambda e: e.activation(out=o, in_=i, func=func, **kw), r, w)

    def TS(eng, o, i, s1, s2, op0, op1, r, w):
        P.op(eng, lambda e: e.tensor_scalar(out=o, in0=i, scalar1=s1, scalar2=s2, op0=op0, op1=op1), r, w)

    def TS1(eng, o, i, s1, op0, r, w):
        P.op(eng, lambda e: e.tensor_scalar(out=o, in0=i, scalar1=s1, scalar2=None, op0=op0), r, w)

    def TT(eng, o, a, b, op, r, w):
        P.op(eng, lambda e: e.tensor_tensor(out=o, in0=a, in1=b, op=op), r, w)

    def STT(o, a, sc, b, op0, op1, r, w):
        P.op("dve", lambda e: e.scalar_tensor_tensor(out=o, in0=a, scalar=sc, in1=b, op0=op0, op1=op1), r, w)

    def CP(eng, o, i, r, w):
        if eng == "act":
            P.op("act", lambda e: e.copy(out=o, in_=i), r, w)
        else:
            P.op(eng, lambda e: e.tensor_copy(out=o, in_=i), r, w)

    def MS(eng, o, val, w):
        P.op(eng, lambda e: e.memset(o, val), (), w)

    def RECIP(o, i, r, w):
        P.op("dve", lambda e: e.reciprocal(out=o, in_=i), r, w)

    def TR(o, i, r, w):
        P.op("pe", lambda e: e.transpose(out=o, in_=i, identity=ident[:]), list(r) + ["ident"], w)

    def ASEL(o, i, pattern, cmp, fill, base, cm, r, w):
        P.op("pool", lambda e: e.affine_select(out=o, in_=i, pattern=pattern, compare_op=cmp, fill=fill,
                                               base=base, channel_multiplier=cm), r, w)

    cst = CA.a("cst", [128, NCONST], F32)
    P.dma("sp", "cst", cst[:], consts, (), ["cst"])
    cosT = cst[:, 0:1024].rearrange("p (t f) -> p t f", f=32)
    sinT = cst[:, 1024:2048].rearrange("p (t f) -> p t f", f=32)
    CAP0, FLO0, OV0 = 2048, 2176, 2304
    gpreT = CA0.a("gpreT", [128, 8], F32)
    g2T = CA0.a("g2T", [128, 8], F32)
    P.dma("sp", "gv", gpreT[:], gpreT_d, (), ["gpreT"])
    P.dma("sp", "gv", g2T[:], g2T_d, (), ["g2T"])
    ident = CA0.a("ident", [128, 128], BF16)
    maskC = CA.a("maskC", [128, 512], BF16)
    maskW = CA.a("maskW", [128, 512], BF16)
    mhalf = CA0.a("mhalf", [128, 4], F32)
    zf = UA.a("zf", [128, 512], F32)
    MS("pool", zf[:], 0.0, ["zf"])
    MS("pool", mhalf[:], -0.5, ["mhalf"])
    ASEL(ident[:], zf[:, 0:128], [[-1, 128]], ALU.not_equal, 1.0, 0, 1, ["zf"], ["ident"])
    ASEL(maskC[:], zf[:], [[0, 4], [1, 128]], ALU.is_ge, NEG, 0, -1, ["zf"], ["maskC"])
    ASEL(maskW[:], zf[:], [[0, 4], [-1, 128]], ALU.is_gt, NEG, 0, 1, ["zf"], ["maskW"])
    maskC4 = maskC[:, :].rearrange("p (h q) -> p h q", h=4)
    maskW4 = maskW[:, :].rearrange("p (h q) -> p h q", h=4)
    lamt = UA.a("lamt", [128, 4, 64], F32)
    P.dma("sp", "gv", lamt[:], lam4.partition_broadcast(128), (), ["lamt"])
    lprod = UA.a("lprod", [128, 2, 64], F32)
    lsum = CA.a("lsum", [128, 2], F32)
    lexp = CA.a("lexp", [128, 2], F32)
    neglam = CA.a("neglam", [128, 1], F32)
    TT("dve", lprod[:], lamt[:, 0:4:2, :], lamt[:, 1:4:2, :], ALU.mult, ["lamt"], ["lprod"])
    P.op("dve", lambda e: e.reduce_sum(out=lsum[:], in_=lprod[:], axis=AX.X), ["lprod"], ["lsum"])
    ACT(lexp[:], lsum[:], AF.Exp, ["lsum"], ["lexp"])
    TT("dve", neglam[:], lexp[:, 1:2], lexp[:, 0:1], ALU.subtract, ["lexp"], ["neglam0"])
    TS1("dve", neglam[:], neglam[:], -LAM_INIT, ALU.add, ["neglam0"], ["neglam"])
    subtab = CA.a("subtab", [128, 128], F32)
    P.dma("sp", "gv", subtab[:], subln[0].partition_broadcast(128), (), ["subtab0"])
    TS1("dve", subtab[:], subtab[:], (1.0 - LAM_INIT) * float(np.sqrt(128.0)), ALU.mult, ["subtab0"], ["subtab"])
    posT = [CA.a("posT%d" % i, [64, 32], BF16) for i in range(2)]
    w2 = [CA.a("w2_%d" % i, [128, 2, 64], BF16) for i in range(2)]
    for i in range(2):
        P.dma("pool", "gvp", posT[i][:], cposT[i], (), [("posT", i)])
        P.dma("pool", "gvp", w2[i][:], cw2[i], (), [("w2", i)])
    VCA = CA.a("VCA", [128, 2, 130], BF16)
    MS("dve", VCA[:, :, 64:65], 1.0, ["VCA1"])
    CP("dve", VCA[:, :, 65:129], cst[:, OV0:OV0 + 128].rearrange("p (n j) -> p n j", n=2), ["cst"], ["VCAov"])
    maskcmp = CA.a("maskcmp", [128, 33, 128], BF16)
    onesb = UA.a("onesb", [128, 128], BF16)
    MS("pool", onesb[:], 1.0, ["onesb"])
    for i in range(17):
        ASEL(maskcmp[:, i, :], onesb[:], [[1, 128]], ALU.is_ge, 0.0, 128 * i - 31, -16, ["onesb"], [("mcmp", i)])
    for i in range(16, 32):
        ASEL(maskcmp[:, 17 + i - 16, :], onesb[:], [[1, 128]], ALU.is_ge, 0.0, 128 * i - 31 - 2048, -16,
             ["onesb"], [("mcmp", 17 + i - 16)])
    ssA = CA.a("ssA", [128, 32], F32)
    rsA = CA.a("rsA", [128, 32], F32)
    junk = CA0.a("junk", [128, D], BF16)
    jctr = [0, 0]

    def junk_slot(width):
        if width == 1024:
            return junk[:, :], ["j%d" % i for i in range(8)]
        if width == 512:
            hh = jctr[0] % 2
            jctr[0] += 1
            return junk[:, hh * 512:(hh + 1) * 512], ["j%d" % i for i in range(4 * hh, 4 * hh + 4)]
        sl = jctr[1] % 8
        jctr[1] += 1
        return junk[:, sl * 128:(sl + 1) * 128], ["j%d" % sl]

    P.barrier()
    UA.o = HT_OFF + 65536
    UA_mark = UA.o
    xs = [UA.a("xs%d" % i, [128, 2, D], F32) for i in range(2)]
    xn = [UA.a("xn%d" % i, [128, D], BF16) for i in range(2)]
    gpre_bc = gpreT[:, :].unsqueeze(2).to_broadcast([128, 8, 128])
    x_v = x.rearrange("(t p) d -> p t d", p=128)
    for tp in range(NT // 2):
        xb = tp % 2
        P.dma("sp", "xs%d" % xb, xs[xb][:], x_v[:, 2 * tp:2 * tp + 2, :], (), [("xs", xb)])
        for jx in range(2):
            t = 2 * tp + jx
            b = t % 2
            xin = xs[xb][:, jx, :]
            jo, jr = junk_slot(1024)
            ACT(jo, xin, AF.Square, [("xs", xb)], [("ssA", t)] + jr, accum_out=ssA[:, t:t + 1])
            TS1("pool", rsA[:, t:t + 1], ssA[:, t:t + 1], 1024.0 * EPS, ALU.add, [("ssA", t)], [("rsA0", t)])
            TT("pool", rsA[:, t:t + 1], rsA[:, t:t + 1], mhalf[:, 0:1], ALU.pow, [("rsA0", t), "mhalf"], [("rsA", t)])
            TS("dve", xn[b][:], xin, rsA[:, t:t + 1], 32.0, ALU.mult, ALU.mult, [("xs", xb), ("rsA", t)], [("xn", b)])
            pT = pbb[b].rearrange("p (k t) -> p k t", k=8)
            for kc in range(8):
                TR(pT[:, kc, :], xn[b][:, kc * 128:(kc + 1) * 128], [("xn", b)], [("pb", b)])
            TT("dve", hT[:, :, t * 128:(t + 1) * 128], pT, gpre_bc, ALU.mult, [("pb", b), "gpreT"], [("hT", t)])
    UA.o = UA_mark
    P.barrier()

    hT_all = [("hT", t) for t in range(NT)]

    import os

    class Step:
        __slots__ = ("pre", "s", "pv", "post")

        def __init__(self, s, pv, pre=None, post=None):
            self.s, self.pv, self.pre, self.post = s, pv, pre, post

    def run_steps(steps, L=2):
        n = len(steps)
        nxt = 0
        for k in range(n):
            while nxt < n and nxt <= k + L:
                st = steps[nxt]
                if st.pre is not None:
                    if nxt > k:
                        break
                    st.pre()
                st.s()
                nxt += 1
            steps[k].pv()
            if steps[k].post:
                steps[k].post()

    SBANKS = (3, 4, 1)
    sctr = [0]

    PA0 = Arena(nc, PC_OFF, HT_OFF)
    gF = PA0.a("gF", [128, D], F32)
    xs2 = [PA0.a("xs2_%d" % i, [128, 2, D], F32) for i in range(2)]
    st3 = PA0.a("st3", [128, 6, 64], F32)
    PA = Arena(nc, HT_OFF, TOP)
    Wg = PA.a("Wg", [128, 8, DFF], BF16)
    Wu = PA.a("Wu", [128, 8, DFF], BF16)
    assert PA.o <= HT_OFF + 65536 + 31424

    n_diff = int(os.environ.get("DBG_HEADS", 4)) if stage != "none" else 0
    n_chunks = int(os.environ.get("DBG_CH", 16))
    UA_mark = UA.o
    Wd_ = [UA.a("Wdh%d" % i, [128, 8, 384], BF16) for i in range(2)]
    QK = UA.a("QK", [128, 3, S], BF16)
    Vh = UA.a("Vh", [128, NT, 130], BF16)
    AOd = UA.a("AOd", [128, NT, 128], BF16)
    qk = [UA.a("qk%d" % i, [128, 256], F32) for i in range(2)]
    rp = [UA.a("rp%d" % i, [128, 256], BF16) for i in range(2)]
    tmpd = [UA.a("tmpd%d" % i, [128, 2, 128], F32) for i in range(2)]
    tmpp = [UA.a("tmpp%d" % i, [128, 2, 128], F32) for i in range(2)]
    pTs = [UA.a("pTs%d" % i, [128, 2, 256], BF16) for i in range(5)]
    rcd = UA.a("rcd", [128, 16, 4], F32)
    cf1 = UA.a("cf1", [128, 16, 2], F32)
    asb = [UA.a("asb%d" % i, [128, 128], F32) for i in range(2)]
    a2sb = [UA.a("a2sb%d" % i, [128, 128], F32) for i in range(2)]
    ssd = UA.a("ssd", [128, 4 * NT], F32)
    rsd = UA.a("rsd", [128, 4 * NT], F32)
    if n_diff:
        MS("dve", Vh[:, :, 128:129], 1.0, ["Vones"])
        MS("pool", QK[64:128, 0, :], 0.0, ["Qz0"])
        MS("pool", QK[0:64, 2, :], 0.0, ["Qz1"])
        P.dma("pool", "wdh0", Wd_[0][:], w_in_u[:, :, 0:384], (), [("Wdh", 0)])

    def diff_inproj(h):
        wb = h % 2
        if h + 1 < n_diff:
            P.dma("pool", "wdh%d" % ((h + 1) % 2), Wd_[(h + 1) % 2][:], w_in_u[:, :, (h + 1) * 384:(h + 2) * 384], (),
                  [("Wdh", (h + 1) % 2)])
        def dfront(t):
            tb = t % 2
            pin = pb[tb]
            for kc in range(8):
                MM(pin[:, 0:384], hT[:, kc, t * 128:(t + 1) * 128], Wd_[wb][:, kc, :], kc == 0, kc == 7,
                   [("hT", t), ("Wdh", wb)], [("pb", tb)])
            ACT(qk[tb][:], pin[:, 0:256], AF.Copy, [("pb", tb)], [("qk", tb)])
            ACT(Vh[:, t, 0:128], pin[:, 256:384], AF.Copy, [("pb", tb)], [("Vh", t)])
            v4 = qk[tb][:, :].rearrange("p (a h d) -> p a h d", a=4, h=2)
            r4 = rp[tb][:, :].rearrange("p (a h d) -> p a h d", a=4, h=2)
            cb = cosT[:, t, :].unsqueeze(1).to_broadcast([128, 4, 32])
            sb = sinT[:, t, :].unsqueeze(1).to_broadcast([128, 4, 32])
            td = tmpd[tb][:, :, :].rearrange("p a (h d) -> p a h d", h=4)
            tp = tmpp[tb][:, :, :].rearrange("p a (h d) -> p a h d", h=4)
            TT("dve", td[:, 0], v4[:, :, 0, :], cb, ALU.mult, [("qk", tb), "cst"], [("td0", tb)])
            TT("dve", td[:, 1], v4[:, :, 1, :], sb, ALU.mult, [("qk", tb), "cst"], [("td1", tb)])
            TT("dve", r4[:, :, 0, :], td[:, 0], td[:, 1], ALU.subtract, [("td0", tb), ("td1", tb)], [("rpA", tb)])
            TT("pool", tp[:, 0], v4[:, :, 1, :], cb, ALU.mult, [("qk", tb), "cst"], [("tp0", tb)])
            TT("pool", tp[:, 1], v4[:, :, 0, :], sb, ALU.mult, [("qk", tb), "cst"], [("tp1", tb)])
            TT("pool", r4[:, :, 1, :], tp[:, 0], tp[:, 1], ALU.add, [("tp0", tb), ("tp1", tb)], [("rpB", tb)])

        def dback(t):
            tb = t % 2
            pT = pbb[2].rearrange("p (k t) -> p k t", k=8)
            TR(pT[:, 0, :], rp[tb][:, 0:128], [("rpA", tb), ("rpB", tb)], [("pb", 2)])
            TR(pT[:, 1, :], rp[tb][:, 128:256], [("rpA", tb), ("rpB", tb)], [("pb", 2)])
            CP("act", QK[0:64, 0, t * 128:(t + 1) * 128], pT[0:64, 0, :], [("pb", 2)], [("QKa", t)])
            CP("act", QK[64:128, 2, t * 128:(t + 1) * 128], pT[64:128, 0, :], [("pb", 2)], [("QKb", t)])
            CP("act", QK[:, 1, t * 128:(t + 1) * 128], pT[:, 1, :], [("pb", 2)], [("QK", t)])

        dfront(0)
        for t in range(NT):
            if t + 1 < NT:
                dfront(t + 1)
            dback(t)

    def mk_diff_step(h, c, kt):
        j = kt - 2 * c
        k = sctr[0]
        sctr[0] += 1
        sbk, pbuf = (3, 4, 1, 0)[k % 4], k % 5
        obank = (5, 6) if c % 2 == 0 else (7, 2)
        O = [pb[obank[m]][:, 0:258].rearrange("p (j e) -> p j e", e=129) for m in range(2)]
        Sv = pb[sbk][:, :].rearrange("p (m q) -> p m q", m=2)
        qres = [("QKa", 2 * c), ("QKa", 2 * c + 1), ("QKb", 2 * c), ("QKb", 2 * c + 1), "Qz0", "Qz1"]

        def s_():
            kT = QK[:, 1, kt * 128:(kt + 1) * 128]
            for m in range(2):
                qp = 2 * m
                if j < 0:
                    MM(Sv[:, m, :], kT, QK[:, qp, c * 256:(c + 1) * 256], True, True, [("QK", kt)] + qres, [("pb", sbk)])
                else:
                    MM(Sv[:, m, 128 * j:128 * j + 128], ident[:], maskC[:, 0:128], True, False, ["ident", "maskC"], [("pb", sbk)])
                    MM(Sv[:, m, 128 * j:128 * j + 128], kT, QK[:, qp, c * 256 + 128 * j:c * 256 + 128 * j + 128], False, True,
                       [("QK", kt)] + qres, [("pb", sbk)])
                    if j == 0:
                        MM(Sv[:, m, 128:256], kT, QK[:, qp, c * 256 + 128:c * 256 + 256], True, True,
                           [("QK", kt)] + qres, [("pb", sbk)])
            q0 = 128 * j if j > 0 else 0
            ACT(pTs[pbuf][:, :, q0:256], Sv[:, :, q0:256], AF.Exp, [("pb", sbk)], [("pTs", pbuf)], scale=0.125)

        def pv_():
            for m in range(2):
                for jj in range(max(j, 0), 2):
                    MM(O[m][:, jj, :], pTs[pbuf][:, m, 128 * jj:128 * jj + 128], Vh[:, kt, 0:129],
                       kt == 0 and jj == 0, kt == 2 * c + jj, [("pTs", pbuf), ("Vh", kt), "Vones"], [("pb", obank[m])], sg=True)

        def post_():
            for m in range(2):
                RECIP(rcd[:, c, 2 * m:2 * m + 2], O[m][:, :, 128], [("pb", obank[m])], [("rcd", c, m)])
            TS1("dve", cf1[:, c, :], rcd[:, c, 2:4], neglam[:, 0:1], ALU.mult, [("rcd", c, 1), "neglam"], [("cf1", c)])
            for jj in range(2):
                t = 2 * c + jj
                col = h * NT + t
                ab = t % 2
                TS1("dve", asb[ab][:], O[0][:, jj, 0:128], rcd[:, c, jj:jj + 1], ALU.mult,
                    [("pb", obank[0]), ("rcd", c, 0)], [("asb", ab)])
                STT(a2sb[ab][:], O[1][:, jj, 0:128], cf1[:, c, jj:jj + 1], asb[ab][:], ALU.mult, ALU.add,
                    [("pb", obank[1]), ("cf1", c), ("asb", ab)], [("a2sb", ab)])
                jo, jr = junk_slot(128)
                ACT(jo, a2sb[ab][:], AF.Square, [("a2sb", ab)], [("ssd", col)] + jr, accum_out=ssd[:, col:col + 1])
                TS1("pool", rsd[:, col:col + 1], ssd[:, col:col + 1], 128.0 * EPS, ALU.add, [("ssd", col)], [("rsd0", col)])
                TT("pool", rsd[:, col:col + 1], rsd[:, col:col + 1], mhalf[:, 0:1], ALU.pow, [("rsd0", col), "mhalf"], [("rsd", col)])
                STT(AOd[:, t, :], a2sb[ab][:], rsd[:, col:col + 1], subtab[:], ALU.mult, ALU.mult,
                    [("a2sb", ab), ("rsd", col), "subtab"], [("AOd", t)])
            if c == n_chunks - 1:
                P.dma("sp", "aod", ao_v[:, :, h * 128:(h + 1) * 128], AOd[:], [("AOd", t) for t in range(NT)], [("ao_s", h)])

        pre = (lambda: diff_inproj(h)) if (c == 0 and kt == 0) else None
        post = post_ if kt == 2 * c + 1 else None
        return Step(s_, pv_, pre, post)

    run_steps([mk_diff_step(h, c, kt) for h in range(n_diff) for c in range(n_chunks) for kt in range(2 * c + 2)], L=3)
    UA.o = UA_mark

    n_nsa = 2 if stage in ("all", "nsa") else 0
    n_q = int(os.environ.get("DBG_NQ", NT))
    P.barrier()
    UA_mark = UA.o
    Wn = UA.a("Wn", [128, 8, 652], BF16)
    W1 = UA.a("W1", [64, 16, 256], BF16)
    qn = [UA.a("qn%d" % i, [128, 448], F32) for i in range(2)]
    rn = [UA.a("rn%d" % i, [128, 512], BF16) for i in range(2)]
    tnd = [UA.a("tnd%d" % i, [128, 2, 224], F32) for i in range(2)]
    tnp = [UA.a("tnp%d" % i, [128, 2, 224], F32) for i in range(2)]
    assert UA.o - (HT_OFF + 65536) >= 31424
    nqT = UA.a("nqT", [128, 4, S], BF16)
    KV4 = UA.a("KV4", [128, 4, S], BF16)
    VS = UA.a("VS", [128, NT, 66], BF16)
    VW = UA.a("VW", [128, NT, 66], BF16)
    gat = UA.a("gat", [128, NT, 12], F32)
    AOn = [UA.a("AOn%d" % i, [128, 2, 256], BF16) for i in range(2)]
    kcmpT = UA.a("kcmpT", [64, 256], BF16)
    hsb = UA.a("hsb", [128, 2, 256], BF16)
    bsb = UA.a("bsb", [128, 2], F32)
    gex = UA.a("gex", [128, 12], F32)
    pcs = [UA.a("pcs%d" % i, [128, 4, 128], BF16) for i in range(2)]
    pss = [UA.a("pss%d" % i, [128, 4, 128], BF16) for i in range(4)]
    NB = CA.a("NB", [128, 128], BF16)
    imp = CA.a("imp", [128, 64], F32)
    imp2 = CA.a("imp2", [128, 64], F32)
    mrp = CA.a("mrp", [128, 64], F32)
    m8 = UA.a("m8", [128, 16], F32)
    rc4 = [UA.a("rc4_%d" % i, [128, 4], F32) for i in range(2)]
    cco = [UA.a("cco%d" % i, [128, 3, 4], F32) for i in range(2)]
    acc = [UA.a("acc%d" % i, [128, 4, 64], F32) for i in range(2)]
    tac = UA.a("tac", [128, 4, 64], F32)
    if n_nsa:
        MS("dve", VS[:, :, 64:65], 1.0, ["VSones"])
        MS("dve", VW[:, :, 64:65], 1.0, ["VWones"])
        MS("dve", NB[:, 0:64], 0.0, ["NB0"])
        MS("dve", hsb[:, :, 255:256], 0.0, ["hsbpad"])
        Et = nqT[0:64, 0, :]
        MS("pool", Et, 1.0, ["Et"])
        ASEL(Et, Et, [[1, S]], ALU.is_ge, 0.0, 0, -64, ["Et"], ["Et"])
        ASEL(Et, Et, [[-1, S]], ALU.is_ge, 0.0, 63, 64, ["Et"], ["Et"])
        CP("dve", KV4[64:128, 1, :], Et, ["Et"] + [("nqT", t) for t in range(NT)], ["E"])
    kv_all = [("KV4", t) for t in range(NT)]
    vca_r = ["VCAv", "VCA1", "VCAov"]

    def nsa_pre(g):
        P.dma("pool", "wn", Wn[:], w_in_u[:, :, 1536 + g * 652:1536 + (g + 1) * 652], (), ["Wn"])
        def nfront(t):
            tb = t % 2
            pa, pbk, pbi = pb[tb], pb[(2, 5)[tb]], (2, 5)[tb]
            for kc in range(8):
                MM(pa[:, :], hT[:, kc, t * 128:(t + 1) * 128], Wn[:, kc, 0:512], kc == 0, kc == 7,
                   [("hT", t), "Wn"], [("pb", tb)])
            for kc in range(8):
                MM(pbk[:, 0:140], hT[:, kc, t * 128:(t + 1) * 128], Wn[:, kc, 512:652], kc == 0, kc == 7,
                   [("hT", t), "Wn"], [("pb", pbi)])
            ACT(qn[tb][:], pa[:, 0:448], AF.Copy, [("pb", tb)], [("qn", tb)])
            ACT(rn[tb][:, 448:512], pa[:, 448:512], AF.Copy, [("pb", tb)], [("rnV", tb)])
            ACT(VS[:, t, 0:64], pbk[:, 0:64], AF.Copy, [("pb", pbi)], [("VS", t)])
            ACT(VW[:, t, 0:64], pbk[:, 64:128], AF.Copy, [("pb", pbi)], [("VW", t)])
            ACT(gex[:], pbk[:, 128:140], AF.Exp, [("pb", pbi)], ["gex"], scale=-1.0)
            TS1("dve", gex[:], gex[:], 1.0, ALU.add, ["gex"], ["gex"])
            RECIP(gat[:, t, :], gex[:], ["gex"], [("gat", t)])
            v4 = qn[tb][:, :].rearrange("p (a h d) -> p a h d", a=7, h=2)
            r4 = rn[tb][:, 0:448].rearrange("p (a h d) -> p a h d", a=7, h=2)
            cb = cosT[:, t, :].unsqueeze(1).to_broadcast([128, 7, 32])
            sb = sinT[:, t, :].unsqueeze(1).to_broadcast([128, 7, 32])
            td = tnd[tb][:, :, :].rearrange("p a (h d) -> p a h d", h=7)
            tp = tnp[tb][:, :, :].rearrange("p a (h d) -> p a h d", h=7)
            TT("dve", td[:, 0], v4[:, :, 0, :], cb, ALU.mult, [("qn", tb), "cst"], [("nd0", tb)])
            TT("dve", td[:, 1], v4[:, :, 1, :], sb, ALU.mult, [("qn", tb), "cst"], [("nd1", tb)])
            TT("dve", r4[:, :, 0, :], td[:, 0], td[:, 1], ALU.subtract, [("nd0", tb), ("nd1", tb)], [("rnA", tb)])
            TT("pool", tp[:, 0], v4[:, :, 1, :], cb, ALU.mult, [("qn", tb), "cst"], [("np0", tb)])
            TT("pool", tp[:, 1], v4[:, :, 0, :], sb, ALU.mult, [("qn", tb), "cst"], [("np1", tb)])
            TT("pool", r4[:, :, 1, :], tp[:, 0], tp[:, 1], ALU.add, [("np0", tb), ("np1", tb)], [("rnB", tb)])

        def nback(t):
            tb = t % 2
            pT = pbb[3 + tb].rearrange("p (k t) -> p k t", k=8)
            for k8 in range(8):
                TR(pT[0:64, k8, :], rn[tb][:, k8 * 64:(k8 + 1) * 64], [("rnA", tb), ("rnB", tb), ("rnV", tb)], [("pb", 3 + tb)])
            CP("dve", nqT[0:64, :, t * 128:(t + 1) * 128], pT[0:64, 0:4, :], [("pb", 3 + tb)], [("nqT", t)])
            CP("dve", KV4[0:64, :, t * 128:(t + 1) * 128], pT[0:64, 4:8, :], [("pb", 3 + tb)], [("KV4", t)])

        nfront(0)
        for t in range(NT):
            if t + 1 < NT:
                nfront(t + 1)
            nback(t)

        for kvi in range(2):
            plane = 0 if kvi == 0 else 3
            hid = pb[3][:, :].rearrange("p (j n) -> p j n", j=2)
            for half in range(2):
                P.dma("pool", "w1", W1[:], cw1[kvi][:, half * 16:(half + 1) * 16, :], (), ["W1"])
                for jh in range(2):
                    for l16 in range(16):
                        l = half * 16 + l16
                        MM(hid[:, jh, 0:255], W1[:, l16, jh * 128:(jh + 1) * 128], KV4[0:64, plane, l:l + 16 * 254 + 1:16],
                           l == 0 and jh == 0, l == 31, ["W1"] + kv_all, [("pb", 3)], sg=True)
                for jh in range(2):
                    for l16 in range(16):
                        l = half * 16 + l16
                        MM(pb[4][:, jh:jh + 1], W1[:, l16, jh * 128:(jh + 1) * 128], posT[kvi][:, l:l + 1],
                           l == 0 and jh == 0, l == 31, ["W1", ("posT", kvi)], [("pb", 4)], sg=True)
            CP("dve", bsb[:], pb[4][:, 0:2], [("pb", 4)], ["bsb"])
            for jh in range(2):
                ACT(hsb[:, jh, 0:255], hid[:, jh, 0:255], AF.Silu, [("pb", 3), "bsb", "hsbpad"], [("hsb", jh)],
                    bias=bsb[:, jh:jh + 1])
            if kvi == 0:
                for jh in range(2):
                    MM(pb[5][0:64, 0:256], w2[0][:, jh, :], hsb[:, jh, :], jh == 0, jh == 1,
                       [("hsb", 0), ("hsb", 1), ("w2", 0)], [("pb", 5)])
                CP("dve", kcmpT[:], pb[5][0:64, 0:256], [("pb", 5)], ["kcmpT"])
            else:
                pv = pb[6][:, 0:128].rearrange("p (n d) -> p n d", n=2)
                for nt in range(2):
                    for jh in range(2):
                        MM(pv[:, nt, :], hsb[:, jh, nt * 128:(nt + 1) * 128], w2[1][:, jh, :], jh == 0, jh == 1,
                           [("hsb", 0), ("hsb", 1), ("w2", 1)], [("pb", 6)])
                CP("dve", VCA[:, :, 0:64], pv, [("pb", 6)], ["VCAv"])
        if g == n_nsa - 1 and stage == "all":
            for k in range(4):
                sl = slice(k * 704, (k + 1) * 704)
                P.dma("pool", "wg", Wg[:, :, sl], wg_d[:, :, sl], (), [("Wg", k)], after_all=True)
                P.dma("pool", "wu", Wu[:, :, sl], wu_d[:, :, sl], (), [("Wu", k)], after_all=True)

    def mk_cmp_step(g, i, nt, first, last):
        pi = i % 2
        k = sctr[0]
        sctr[0] += 1
        sbk = SBANKS[k % 3]
        pc = pss[k % 4]
        pcr = ("pss", k % 4)
        Sc = pb[sbk][:, :].rearrange("p (h q) -> p h q", h=4)
        q64 = nqT[0:64, :, i * 128:(i + 1) * 128]
        U = [pb[5 + b][:, 0:258].rearrange("p (j e) -> p j e", e=129) for b in range(2)]

        def s_():
            MM(Sc, kcmpT[:, nt * 128:(nt + 1) * 128], q64, True, True, ["kcmpT", ("nqT", i)], [("pb", sbk)])
            ACT(pc[:], Sc, AF.Exp, [("pb", sbk)], [pcr], scale=0.125)
            midx = None
            if nt == 0 and i <= 16:
                midx = i
            if nt == 1:
                midx = 17 + i - 16
            if midx is not None:
                TT("pool", pc[:], pc[:], maskcmp[:, midx, :].unsqueeze(1).to_broadcast([128, 4, 128]),
                   ALU.mult, [pcr, ("mcmp", midx)], [pcr])

        def pv_():
            for hh in range(4):
                MM(U[hh // 2][:, hh % 2, :], pc[:, hh, :], VCA[:, nt, 0:129], first and hh % 2 == 0, last,
                   [pcr] + vca_r, [("pb", 5 + hh // 2)], sg=True)

        def post_():
            for b in range(2):
                TS1("dve", rc4[pi][:, 2 * b:2 * b + 2], U[b][:, :, 64], 1e-30, ALU.add, [("pb", 5 + b)], [("rc4", pi)])
            RECIP(rc4[pi][:], rc4[pi][:], [("rc4", pi)], [("rc4", pi)])
            TS1("dve", imp[:], U[0][:, 0, 65:129], rc4[pi][:, 0:1], ALU.mult, [("pb", 5), ("rc4", pi)], ["imp"])
            for hh in range(1, 4):
                STT(imp[:], U[hh // 2][:, hh % 2, 65:129], rc4[pi][:, hh:hh + 1], imp[:], ALU.mult, ALU.add,
                    [("pb", 5 + hh // 2), ("rc4", pi), "imp"], ["imp"])
            c0 = 62 - 2 * i
            TT("dve", imp2[:], imp[:], cst[:, CAP0 + c0:CAP0 + c0 + 64], ALU.min, ["imp", "cst"], ["imp2"])
            TT("dve", imp2[:], imp2[:], cst[:, FLO0 + c0:FLO0 + c0 + 64], ALU.max, ["imp2", "cst"], ["imp2"])
            MS("dve", imp2[:, 0:1], 3e30, ["imp2"])
            P.op("dve", lambda e: e.max(out=m8[:, 0:8], in_=imp2[:]), ["imp2"], ["m8a"])
            P.op("dve", lambda e: e.match_replace(out=mrp[:], in_to_replace=m8[:, 0:8], in_values=imp2[:], imm_value=-3e30),
                 ["imp2", "m8a"], ["mrp"])
            P.op("dve", lambda e: e.max(out=m8[:, 8:16], in_=mrp[:]), ["mrp"], ["m8b"])
            TS("dve", NB[:, 64:128], imp2[:], m8[:, 15:16], NEG, ALU.is_lt, ALU.mult, ["imp2", "m8b"], ["NB"])
            pTn = pbb[0][:, 0:128]
            TR(pTn, NB[:, :], ["NB", "NB0"], [("pb", 0)])
            CP("act", nqT[64:128, :, i * 128:(i + 1) * 128], pTn[64:128, :].unsqueeze(1).to_broadcast([64, 4, 128]),
               [("pb", 0)], [("nqTb", i)])
            TT("dve", cco[pi][:, 0, :], rc4[pi][:], gat[:, i, 0:12:3], ALU.mult, [("rc4", pi), ("gat", i)], [("cco", pi, 0)])
            for b in range(2):
                TT("dve", acc[pi][:, 2 * b:2 * b + 2, :], U[b][:, :, 0:64],
                   cco[pi][:, 0, 2 * b:2 * b + 2].unsqueeze(2).to_broadcast([128, 2, 64]), ALU.mult,
                   [("pb", 5 + b), ("cco", pi, 0)], [("acc", pi)])

        return Step(s_, pv_, None, post_ if last else None)

    def mk_br_step(g, i, br, kt, kts):
        pi = i % 2
        k = sctr[0]
        sctr[0] += 1
        sbk, pbuf = SBANKS[k % 3], k % 4
        Ss = pb[sbk][:, :].rearrange("p (h q) -> p h q", h=4)
        q64 = nqT[0:64, :, i * 128:(i + 1) * 128]
        q128 = nqT[:, :, i * 128:(i + 1) * 128]
        if br == 2:
            obk, Vt, vres = 2, VW, "VW"
        else:
            obk, Vt, vres = 7, VS, "VS"
        Ob = pb[obk][:, 0:260].rearrange("p (h e) -> p h e", e=65)

        def s_():
            first = True
            if kt == i:
                MM(Ss, ident[:], maskC4, True, False, ["ident", "maskC"], [("pb", sbk)])
                first = False
            elif br == 2 and kt == i - 4:
                MM(Ss, ident[:], maskW4, True, False, ["ident", "maskW"], [("pb", sbk)])
                first = False
            if br == 2:
                MM(Ss, KV4[0:64, 2, kt * 128:(kt + 1) * 128], q64, first, True, [("KV4", kt), ("nqT", i)], [("pb", sbk)])
            else:
                MM(Ss, KV4[:, 1, kt * 128:(kt + 1) * 128], q128, first, True,
                   [("KV4", kt), "E", ("nqTb", i), ("nqT", i)], [("pb", sbk)])
            ACT(pss[pbuf][:], Ss, AF.Exp, [("pb", sbk)], [("pss", pbuf)], scale=0.125)

        def pv_():
            for hh in range(4):
                MM(Ob[:, hh, :], pss[pbuf][:, hh, :], Vt[:, kt, 0:65], kt == kts[0] and hh == 0, kt == kts[-1],
                   [("pss", pbuf), (vres, kt), vres + "ones"], [("pb", obk)], sg=True)

        def post_():
            RECIP(cco[pi][:, br, :], Ob[:, :, 64], [("pb", obk)], [("cco", pi, br)])
            TT("dve", cco[pi][:, br, :], cco[pi][:, br, :], gat[:, i, br:12:3], ALU.mult, [("cco", pi, br), ("gat", i)], [("cco", pi, br)])
            TT("dve", tac[:], Ob[:, :, 0:64], cco[pi][:, br, :].unsqueeze(2).to_broadcast([128, 4, 64]), ALU.mult,
               [("pb", obk), ("cco", pi, br)], ["tac"])
            TT("pool", acc[pi][:], acc[pi][:], tac[:], ALU.add, ["tac", ("acc", pi)], [("acc", pi)])
            if br == 1:
                ab = (i // 2) % 2
                CP("pool", AOn[ab][:, i % 2, :], acc[pi][:, :, :].rearrange("p h d -> p (h d)"), [("acc", pi)], [("AOn", ab)])
                if i % 2 == 1:
                    P.dma("act", "aon%d" % ab, ao_v[:, i - 1:i + 1, 512 + g * 256:512 + (g + 1) * 256], AOn[ab][:],
                          [("AOn", ab)], [("ao_n", g, i)])

        return Step(s_, pv_, None, post_ if kt == kts[-1] else None)

    def cmp_steps(g, i):
        nts = [0] if i < 16 else [0, 1]
        return [mk_cmp_step(g, i, nt, nt == nts[0], nt == nts[-1]) for nt in nts]

    nsa_steps = []
    for g in range(n_nsa):
        gsteps = []
        if n_q:
            gsteps += cmp_steps(g, 0)
        for i in range(n_q):
            kts_w = list(range(max(0, i - 4), i + 1))
            gsteps += [mk_br_step(g, i, 2, kt, kts_w) for kt in kts_w]
            if i + 1 < n_q:
                gsteps += cmp_steps(g, i + 1)
            kts_s = list(range(0, i + 1))
            gsteps += [mk_br_step(g, i, 1, kt, kts_s) for kt in kts_s]
        if gsteps:
            gsteps[0].pre = (lambda g=g: nsa_pre(g))
        else:
            nsa_pre(g)
        nsa_steps += gsteps
    run_steps(nsa_steps)
    UA.o = UA_mark

    if stage in ("diff", "nsa", "none"):
        st = P.emit()
        return nc, st

    P.barrier()
    Wdn = PA.a("Wdn", [128, FC, D], BF16)
    tm = PA.a("tm", [128, 2, D], F32)
    h2n = PA.a("h2n", [128, D], BF16)
    h2T_off = (PA.o + 31) // 32 * 32
    h2T = PA.a("h2T", [128, 8, 512], BF16)
    ov_base = PA.o
    Wo = PA.a("Wo", [128, 8, D], BF16)
    gA = PA.a("gA", [128, D], F32)
    aos = [PA.a("aos%d" % i, [128, 2, D], BF16) for i in range(2)]
    aoT = [nc.alloc_sbuf_tensor_at("aoT%d" % i, [128, 8, 128], BF16, offset=h2T_off + 2048 * i) for i in range(2)]
    PA.o = ov_base
    sg = [PA.a("sg%d" % i, [128, 512], F32) for i in range(2)]
    actT = PA.a("actT", [128, FC, 512], BF16)
    P.dma("pool", "wo", Wo[:], w_out_r, (), ["Wo"])
    P.dma("sp", "gv", gA[:], gpostA_d[0].partition_broadcast(128), (), ["gA0"])
    P.dma("sp", "gv", gF[:], gpostF_d[0].partition_broadcast(128), (), ["gF0"])
    TS1("pool", gA[:], gA[:], 32.0, ALU.mult, ["gA0"], ["gA"])
    TS1("pool", gF[:], gF[:], 32.0, ALU.mult, ["gF0"], ["gF"])
    for k in range(2):
        sl = slice(k * 11, (k + 1) * 11)
        P.dma("pool", "wd", Wdn[:, sl, :], wd_d[:, sl, :], (), [("Wdn", k)])
    g2_bc = g2T[:, :].unsqueeze(2).to_broadcast([128, 8, 128])
    out_v = out.rearrange("(t p) d -> p t d", p=128)

    def norm_scale(col, ssrc_ap, res_in, tag):
        TS1("pool", st3[:, 1, col:col + 1], ssrc_ap, 1024.0 * EPS, ALU.add, res_in, [(tag + "0", col)])
        TT("pool", st3[:, 1, col:col + 1], st3[:, 1, col:col + 1], mhalf[:, 0:1], ALU.pow, [(tag + "0", col), "mhalf"], [(tag, col)])

    def c0_front(t):
        tp, jx = t // 2, t % 2
        xb = tp % 2
        b = t % 2
        mb = (2, 4)[b]
        if jx == 0:
            P.dma("sp", "aos%d" % xb, aos[xb][:], ao_v[:, 2 * tp:2 * tp + 2, :], (), [("aos", xb)])
            P.dma("act", "xs2%d" % xb, xs2[xb][:], x_v[:, 2 * tp:2 * tp + 2, :], (), [("xs2", xb)])
        pT = pbb[b].rearrange("p (k t) -> p k t", k=8)
        for kc in range(8):
            TR(pT[:, kc, :], aos[xb][:, jx, kc * 128:(kc + 1) * 128], [("aos", xb)], [("pb", b)])
        CP("act", aoT[b][:], pT, [("pb", b)], [("aoT", b)])
        for nh in range(2):
            for kc in range(8):
                MM(pb[mb + nh][:, :], aoT[b][:, kc, :], Wo[:, kc, nh * 512:(nh + 1) * 512], kc == 0, kc == 7,
                   [("aoT", b), "Wo"], [("pb", mb + nh)])

    def c0_back(t):
        tp, jx = t // 2, t % 2
        xb = tp % 2
        mb = (2, 4)[t % 2]
        for nh in range(2):
            jo, jr = junk_slot(512)
            ACT(jo, pb[mb + nh][:, :], AF.Square, [("pb", mb + nh)], [("ssm", t, nh)] + jr,
                accum_out=st3[:, 2 + nh, t:t + 1])
        TT("pool", st3[:, 0, t:t + 1], st3[:, 2, t:t + 1], st3[:, 3, t:t + 1], ALU.add, [("ssm", t, 0), ("ssm", t, 1)], [("ssmt", t)])
        norm_scale(t, st3[:, 0, t:t + 1], [("ssmt", t)], "rsm")
        for nh in range(2):
            STT(tm[:, jx, nh * 512:(nh + 1) * 512], pb[mb + nh][:, :], st3[:, 1, t:t + 1], gA[:, nh * 512:(nh + 1) * 512],
                ALU.mult, ALU.mult, [("pb", mb + nh), ("rsm", t), "gA"], [("tm", jx, nh)])
        if jx == 1:
            TT("pool", tm[:], tm[:], xs2[xb][:], ALU.add, [("tm", 0, 0), ("tm", 0, 1), ("tm", 1, 0), ("tm", 1, 1), ("xs2", xb)],
               ["x1", ("tm", 0, 0), ("tm", 0, 1), ("tm", 1, 0), ("tm", 1, 1)])
            P.dma("sp", "x1o", out_v[:, 2 * tp:2 * tp + 2, :], tm[:], ["x1", ("tm", 0, 0), ("tm", 0, 1), ("tm", 1, 0), ("tm", 1, 1)], [("out", tp)])

    c0_front(0)
    for t in range(NT):
        if t + 1 < NT:
            c0_front(t + 1)
        c0_back(t)

    P.barrier(keep_chans=("wd",), keep_res=[("Wdn", 0), ("Wdn", 1)])
    wg_r = [("Wg", k) for k in range(4)]
    wu_r = [("Wu", k) for k in range(4)]
    wd_r = [("Wdn", k) for k in range(2)]
    tm_all = [("tm", 0, 0), ("tm", 0, 1), ("tm", 1, 0), ("tm", 1, 1)]
    for blk in range(8):
        for tt in range(4):
            t = blk * 4 + tt
            b = t % 2
            tp, jx = t // 2, t % 2
            xb = tp % 2
            if jx == 0:
                P.dma("sp", "xs2%d" % xb, xs2[xb][:], out_v[:, 2 * tp:2 * tp + 2, :], [("out", tp)], [("xs2", xb)])
            xin = xs2[xb][:, jx, :]
            jo, jr = junk_slot(1024)
            ACT(jo, xin, AF.Square, [("xs2", xb)], [("ss2", t)] + jr, accum_out=st3[:, 4, t:t + 1])
            TS1("pool", st3[:, 5, t:t + 1], st3[:, 4, t:t + 1], 1024.0 * EPS, ALU.add, [("ss2", t)], [("rs20", t)])
            TT("pool", st3[:, 5, t:t + 1], st3[:, 5, t:t + 1], mhalf[:, 0:1], ALU.pow, [("rs20", t), "mhalf"], [("rs2", t)])
            TS("dve", h2n[:], xin, st3[:, 5, t:t + 1], 32.0, ALU.mult, ALU.mult, [("xs2", xb), ("rs2", t)], ["h2n"])
            pT = pbb[b].rearrange("p (k t) -> p k t", k=8)
            for kc in range(8):
                TR(pT[:, kc, :], h2n[:, kc * 128:(kc + 1) * 128], ["h2n"], [("pb", b)])
            TT("dve", h2T[:, :, tt * 128:(tt + 1) * 128], pT, g2_bc, ALU.mult, [("pb", b), "g2T"], [("h2T", tt)])
        h2r = [("h2T", tt) for tt in range(4)]
        for fc in range(FC):
            p2 = fc % 2
            gb, ub = (4, 5) if p2 == 0 else (6, 7)
            for kc in range(8):
                MM(pb[gb][:, :], Wg[:, kc, fc * 128:(fc + 1) * 128], h2T[:, kc, :], kc == 0, kc == 7, wg_r + h2r, [("pb", gb)])
            for kc in range(8):
                MM(pb[ub][:, :], Wu[:, kc, fc * 128:(fc + 1) * 128], h2T[:, kc, :], kc == 0, kc == 7, wu_r + h2r, [("pb", ub)])
            ACT(sg[p2][:], pb[gb][:, :], AF.Silu, [("pb", gb)], [("sg", p2)])
            TT("dve", actT[:, fc, :], sg[p2][:], pb[ub][:, :], ALU.mult, [("sg", p2), ("pb", ub)], [("actT", fc)])
        ar = [("actT", fc) for fc in range(FC)]
        for tt in range(4):
            t = blk * 4 + tt
            tp, jx = t // 2, t % 2
            for nh in range(2):
                for fc in range(FC):
                    MM(pb[2 + nh][:, :], actT[:, fc, tt * 128:(tt + 1) * 128], Wdn[:, fc, nh * 512:(nh + 1) * 512],
                       fc == 0, fc == FC - 1, ar + wd_r, [("pb", 2 + nh)])
            for nh in range(2):
                jo, jr = junk_slot(512)
                ACT(jo, pb[2 + nh][:, :], AF.Square, [("pb", 2 + nh)], [("ssy", t, nh)] + jr,
                    accum_out=st3[:, 2 + nh, 32 + t:33 + t])
            TT("pool", st3[:, 0, 32 + t:33 + t], st3[:, 2, 32 + t:33 + t], st3[:, 3, 32 + t:33 + t], ALU.add,
               [("ssy", t, 0), ("ssy", t, 1)], [("ssyt", t)])
            norm_scale(32 + t, st3[:, 0, 32 + t:33 + t], [("ssyt", t)], "rsy")
            for nh in range(2):
                STT(tm[:, jx, nh * 512:(nh + 1) * 512], pb[2 + nh][:, :], st3[:, 1, 32 + t:33 + t], gF[:, nh * 512:(nh + 1) * 512],
                    ALU.mult, ALU.mult, [("pb", 2 + nh), ("rsy", 32 + t), "gF"], [("tm", jx, nh)])
            if jx == 1:
                P.dma("pool", "acc", out_v[:, 2 * tp:2 * tp + 2, :], tm[:], tm_all + [("out", tp)],
                      [("out", tp)] + tm_all, accum_op=ALU.add)
    st = P.emit()
    return nc, st


def _consts():
    c = np.zeros((128, NCONST), np.float32)
    inv = 1.0 / (10000.0 ** (np.arange(0, 64, 2, dtype=np.float32) / 64.0))
    pos = (np.arange(NT)[None, :, None] * 128 + np.arange(128)[:, None, None]).astype(np.float32)
    ang = pos * inv[None, None, :]
    c[:, 0:1024] = np.cos(ang).astype(np.float32).reshape(128, 1024)
    c[:, 1024:2048] = np.sin(ang).astype(np.float32).reshape(128, 1024)
    hi = (np.arange(128) >= 64).astype(np.int64)[:, None]
    m = np.arange(128)[None, :] - 62
    c[:, 2048:2176] = np.where(m <= hi, 1e30, -1e30)
    c[:, 2176:2304] = np.where(m == hi, 2e30, np.where(m == hi - 1, 1e30, -3e30))
    n = np.arange(256)[:, None]
    j = np.arange(64)[None, :]
    ov = np.clip(np.minimum(16 * n + 32, 64 * j + 64) - np.maximum(16 * n, 64 * j), 0, None) / 32.0
    ov[255] = 0.0
    c[:, 2304:2432] = ov.reshape(2, 128, 64).transpose(1, 0, 2).reshape(128, 128)
    return c


def _layout(inp):
    f = lambda k: np.asarray(inp[k], np.float32)[0]
    w_in = f("w_in")
    cols = []
    for h in range(4):
        cols += [np.arange(h * 128, h * 128 + 128), 512 + np.arange(h * 128, h * 128 + 128), 1024 + np.arange(h * 128, h * 128 + 128)]
    for g in range(2):
        cols += [1536 + g * 256 + np.arange(256)]
        for base in (2048, 2304, 2560, 2176, 2432, 2688):
            cols += [base + g * 64 + np.arange(64)]
        cols += [2816 + g * 12 + np.arange(12)]
    cols = np.concatenate(cols)
    r8 = lambda w: np.ascontiguousarray(w.reshape(8, 128, -1).transpose(1, 0, 2))
    m = {
        "consts": _consts(),
        "gpreT": np.ascontiguousarray(f("attn_pre_norm").reshape(8, 128).T),
        "g2T": np.ascontiguousarray(f("ffn_pre_norm").reshape(8, 128).T),
        "w_in_u": r8(w_in[:, cols]),
        "lam4": np.stack([f("lambda_q1"), f("lambda_k1"), f("lambda_q2"), f("lambda_k2")]),
        "subln": f("diff_subln")[None, :],
        "kw1": np.ascontiguousarray(f("k_cmp_w1").reshape(32, 64, 256).transpose(1, 0, 2)),
        "vw1": np.ascontiguousarray(f("v_cmp_w1").reshape(32, 64, 256).transpose(1, 0, 2)),
        "kposT": np.ascontiguousarray(f("k_cmp_pos").T),
        "vposT": np.ascontiguousarray(f("v_cmp_pos").T),
        "kw2": np.ascontiguousarray(f("k_cmp_w2").reshape(2, 128, 64).transpose(1, 0, 2)),
        "vw2": np.ascontiguousarray(f("v_cmp_w2").reshape(2, 128, 64).transpose(1, 0, 2)),
        "w_out_r": r8(f("w_out")),
        "gpostA": f("attn_post_norm")[None, :],
        "gpostF": f("ffn_post_norm")[None, :],
        "wg": r8(f("w_gate")),
        "wu": r8(f("w_up")),
        "wd": np.ascontiguousarray(f("w_down").reshape(FC, 128, D).transpose(1, 0, 2)),
    }
    return m


_CACHE = {}


def kernel(**inputs):
    if "nc" not in _CACHE:
        _CACHE["nc"] = build()[0]
    nc = _CACHE["nc"]
    shared = _layout(inputs)
    xs = np.asarray(inputs["x"], np.float32)
    in_maps = [dict(shared, x=np.ascontiguousarray(xs[b])) for b in range(8)]
    res = run_bass_kernel_spmd(nc, in_maps, core_ids=list(range(8)))
    return np.stack([np.asarray(r["out"], np.float32) for r in res.results], axis=0)
```
